# Optimizing a Trainium2 kernel written in Bass

```python
import math
import jax
import jax.numpy as jnp
from jax import lax
import numpy as np

D_MODEL = 2048
BATCH = 4
SEQ = 8192
DEPTH = 2

N_MEM = 256
N_GROUPS = 4
GROUP_WIDTH = D_MODEL // N_GROUPS
HEADS = 4
HEAD_V = GROUP_WIDTH // HEADS
GLA_DK = HEAD_V // 2
GLA_GATE_RANK = 16
GLA_GATE_NORM = 16.0
HGRN_DK = HEAD_V
DIFF_DK = HEAD_V // 2
MLSTM_DK = HEAD_V // 2
CONV_WIDTH = 4
CHUNK = 64
Q_BLOCK = 128
T5_BUCKETS = 32
T5_MAX_DIST = 128
N_XHEADS = 4
XHEAD_DIM = D_MODEL // N_XHEADS
D_FF = 5632
LN_EPS = 1e-5
LB_EPS = 1e-12
MASK_NEG = -1e30
F32 = jnp.float32

IN_SPLITS = (
    HEADS * GLA_DK, HEADS * GLA_DK, GROUP_WIDTH, GLA_GATE_RANK, GROUP_WIDTH,
    HEADS * HGRN_DK, HEADS * HGRN_DK, GROUP_WIDTH, GROUP_WIDTH,
    2 * HEADS * DIFF_DK, 2 * HEADS * DIFF_DK, GROUP_WIDTH,
    HEADS * MLSTM_DK, HEADS * MLSTM_DK, GROUP_WIDTH, 2 * HEADS, GROUP_WIDTH,
)
N_IN = sum(IN_SPLITS)

kernel_name = 'hybrid_parallel_heads_decoder'


def layer_norm(x, g, b):
    xf = x.astype(F32)
    mu = jnp.mean(xf, -1, keepdims=True)
    var = jnp.mean(jnp.square(xf - mu), -1, keepdims=True)
    return ((xf - mu) * lax.rsqrt(var + LN_EPS) * g.astype(F32) + b.astype(F32)).astype(x.dtype)


def rms_norm(x, g):
    xf = x.astype(F32)
    return xf * lax.rsqrt(jnp.mean(xf * xf, -1, keepdims=True) + LN_EPS) * g.astype(F32)


def swiglu_ffn(x, w_in, w_out):
    gate, up = jnp.split(x @ w_in, 2, axis=-1)
    return (jax.nn.silu(gate) * up) @ w_out


def to_heads(t, h):
    b, s, hd = t.shape
    return t.reshape(b, s, h, hd // h).transpose(0, 2, 1, 3)


def from_heads(t):
    b, h, s, d = t.shape
    return t.transpose(0, 2, 1, 3).reshape(b, s, h * d)


def to_chunks(t):
    b, h, s = t.shape[:3]
    return jnp.moveaxis(t.reshape(b, h, s // CHUNK, CHUNK, *t.shape[3:]), 2, 0)


def from_chunks(t):
    t = jnp.moveaxis(t, 0, 2)
    return t.reshape(t.shape[0], t.shape[1], -1, t.shape[-1])


def chunked_gated_linear_attention(q, k, v, log_g):
    b, h, _, dk = q.shape
    dv = v.shape[-1]
    causal = jnp.tril(jnp.ones((CHUNK, CHUNK), dtype=bool))[:, :, None]

    def step(state, inp):
        qc, kc, vc, gc = inp
        cum = jnp.cumsum(gc.astype(F32), axis=2)
        o_inter = jnp.einsum('bhcd,bhde->bhce', qc * jnp.exp(cum), state)
        diff = cum[:, :, :, None, :] - cum[:, :, None, :, :]
        decay = jnp.where(causal, jnp.exp(jnp.where(causal, diff, 0.0)), 0.0)
        attn = jnp.einsum('bhid,bhjd,bhijd->bhij', qc, kc, decay)
        o_intra = jnp.einsum('bhij,bhje->bhie', attn, vc)
        last = cum[:, :, -1:, :]
        new_state = jnp.exp(last[:, :, 0, :])[..., None] * state + jnp.einsum('bhcd,bhce->bhde', kc * jnp.exp(last - cum), vc)
        return new_state, o_inter + o_intra

    state0 = jnp.zeros((b, h, dk, dv), F32)
    _, o = lax.scan(step, state0, (to_chunks(q), to_chunks(k), to_chunks(v), to_chunks(log_g)))
    return from_chunks(o)


def chunked_mlstm(q, k, v, log_i, log_f):
    b, h, _, dk = q.shape
    dv = v.shape[-1]
    causal = jnp.tril(jnp.ones((CHUNK, CHUNK), dtype=bool))

    def step(carry, inp):
        c_st, n_st, m_st = carry
        qc, kc, vc, ic, fc = inp
        ic = ic.astype(F32)
        cum = jnp.cumsum(fc.astype(F32), axis=-1)
        log_inter = cum + m_st[..., None]
        log_intra = jnp.where(causal, cum[..., :, None] - cum[..., None, :] + ic[..., None, :], MASK_NEG)
        m_t = jnp.maximum(log_inter, jnp.max(log_intra, axis=-1))
        w_inter = jnp.exp(log_inter - m_t)
        w_intra = jnp.where(causal, jnp.exp(log_intra - m_t[..., None]), 0.0)
        scores = jnp.einsum('bhid,bhjd->bhij', qc, kc) * w_intra
        num = w_inter[..., None] * jnp.einsum('bhcd,bhde->bhce', qc, c_st) + jnp.einsum('bhij,bhje->bhie', scores, vc)
        den = w_inter * jnp.einsum('bhcd,bhd->bhc', qc, n_st) + jnp.sum(scores, axis=-1)
        h_t = num / jnp.maximum(jnp.abs(den), jnp.exp(-m_t))[..., None]
        log_last_inter = cum[..., -1] + m_st
        log_last_intra = cum[..., -1:] - cum + ic
        m_new = jnp.maximum(log_last_inter, jnp.max(log_last_intra, axis=-1))
        wk = jnp.exp(log_last_intra - m_new[..., None])
        dec = jnp.exp(log_last_inter - m_new)
        c_new = dec[..., None, None] * c_st + jnp.einsum('bhcd,bhce->bhde', kc * wk[..., None], vc)
        n_new = dec[..., None] * n_st + jnp.einsum('bhcd,bhc->bhd', kc, wk)
        return (c_new, n_new, m_new), h_t

    carry0 = (jnp.zeros((b, h, dk, dv), F32), jnp.zeros((b, h, dk), F32), jnp.zeros((b, h), F32))
    _, o = lax.scan(step, carry0, (to_chunks(q), to_chunks(k), to_chunks(v), to_chunks(log_i), to_chunks(log_f)))
    return from_chunks(o)


def t5_bucket(rel):
    n = jnp.maximum(rel, 0)
    max_exact = T5_BUCKETS // 2
    large = max_exact + (jnp.log(jnp.maximum(n, 1).astype(F32) / max_exact)
                         / math.log(T5_MAX_DIST / max_exact) * (T5_BUCKETS - max_exact)).astype(jnp.int32)
    large = jnp.clip(large, max_exact, T5_BUCKETS - 1)
    return jnp.where(n < max_exact, n, large)


def differential_attention(q, k, v, lam, t5_table, norm_g, lambda_init):
    b, h, s, _ = q.shape
    nb = s // Q_BLOCK
    q1, q2 = jnp.split(q * DIFF_DK ** -0.5, 2, axis=-1)
    k1, k2 = jnp.split(k, 2, axis=-1)
    blocks = lambda t: t.reshape(b, h, nb, Q_BLOCK, DIFF_DK).transpose(2, 0, 1, 3, 4)
    k_pos = jnp.arange(s)

    def block(args):
        i, q1b, q2b = args
        q_pos = i * Q_BLOCK + jnp.arange(Q_BLOCK)
        rel = q_pos[:, None] - k_pos[None, :]
        bias = jnp.transpose(t5_table[t5_bucket(rel)], (2, 0, 1)).astype(F32)
        mask = rel >= 0

        def probs(qq, kk):
            logits = jnp.einsum('bhqd,bhkd->bhqk', qq, kk).astype(F32) + bias
            return jax.nn.softmax(jnp.where(mask, logits, MASK_NEG), axis=-1)

        w = probs(q1b, k1) - lam * probs(q2b, k2)
        return jnp.einsum('bhqk,bhkd->bhqd', w.astype(v.dtype), v)

    out = lax.map(block, (jnp.arange(nb), blocks(q1), blocks(q2)))
    out = out.transpose(1, 2, 0, 3, 4).reshape(b, h, s, -1)
    return rms_norm(out, norm_g) * (1.0 - lambda_init)


def causal_dwconv(x, w):
    return lax.conv_general_dilated(x, w[:, None, :].astype(x.dtype), window_strides=(1,),
                                    padding=[(CONV_WIDTH - 1, 0)], dimension_numbers=('NWC', 'WIO', 'NWC'),
                                    feature_group_count=x.shape[-1])


def hybrid_mixer(h, layer, w_in, w_out, gla_gate_w, gla_gate_b, gla_norm_g, lb, hgrn_norm_g,
                 diff_lambda, diff_norm_g, t5_table, mlstm_conv_w, mlstm_gate_b):
    proj = h @ w_in
    (a_q, a_k, a_v, a_lr, a_r, b_q, b_f, b_i, b_g, c_q, c_k, c_v,
     d_q, d_k, d_v, d_if, d_o) = jnp.split(proj, list(np.cumsum(IN_SPLITS)[:-1]), axis=-1)

    a_logg = jax.nn.log_sigmoid((a_lr @ gla_gate_w + gla_gate_b).astype(F32)) / GLA_GATE_NORM
    o_a = chunked_gated_linear_attention(to_heads(a_q, HEADS) * GLA_DK ** -0.5, to_heads(a_k, HEADS),
                                         to_heads(a_v, HEADS), to_heads(a_logg, HEADS))
    o_a = from_heads(rms_norm(o_a, gla_norm_g)) * jax.nn.silu(a_r.astype(F32))

    f_pre = b_f.astype(F32)
    lb = lb.astype(F32)
    log_f = jnp.logaddexp(jnp.log(jnp.maximum(lb, LB_EPS)), jnp.log1p(-lb) + jax.nn.log_sigmoid(f_pre))
    k_b = (1.0 - lb) * jax.nn.sigmoid(-f_pre)
    o_b = chunked_gated_linear_attention(to_heads(jax.nn.silu(b_q), HEADS) * HGRN_DK ** -0.5, to_heads(k_b, HEADS),
                                         to_heads(b_i, HEADS), to_heads(log_f, HEADS))
    o_b = from_heads(rms_norm(o_b, hgrn_norm_g)) * jax.nn.silu(b_g.astype(F32))

    lambda_init = 0.8 - 0.6 * math.exp(-0.3 * layer)
    lq1, lk1, lq2, lk2 = [diff_lambda[j].astype(F32) for j in range(4)]
    lam = jnp.exp(jnp.sum(lq1 * lk1)) - jnp.exp(jnp.sum(lq2 * lk2)) + lambda_init
    o_c = differential_attention(to_heads(c_q, HEADS), to_heads(c_k, HEADS), to_heads(c_v, HEADS),
                                 lam, t5_table, diff_norm_g, lambda_init)
    o_c = from_heads(o_c)

    qk = jax.nn.silu(causal_dwconv(jnp.concatenate([d_q, d_k], axis=-1), mlstm_conv_w))
    d_qc, d_kc = jnp.split(qk, 2, axis=-1)
    gates = (d_if + mlstm_gate_b).astype(F32)
    log_i = gates[..., :HEADS].transpose(0, 2, 1)
    log_fd = jax.nn.log_sigmoid(gates[..., HEADS:]).transpose(0, 2, 1)
    o_d = chunked_mlstm(to_heads(d_qc, HEADS), to_heads(d_kc, HEADS) * MLSTM_DK ** -0.5, to_heads(d_v, HEADS), log_i, log_fd)
    o_d = from_heads(o_d) * jax.nn.sigmoid(d_o.astype(F32))

    mixed = jnp.concatenate([o_a, o_b, o_c, o_d], axis=-1).astype(h.dtype)
    return mixed @ w_out


def memory_cross_attention(h, mem, w_q, w_kv, w_o):
    q = to_heads(h @ w_q, N_XHEADS)
    k, v = jnp.split(mem @ w_kv, 2, axis=-1)
    k, v = to_heads(k, N_XHEADS), to_heads(v, N_XHEADS)
    logits = jnp.einsum('bhqd,bhkd->bhqk', q, k).astype(F32) * XHEAD_DIM ** -0.5
    p = jax.nn.softmax(logits, axis=-1).astype(v.dtype)
    return from_heads(jnp.einsum('bhqk,bhkd->bhqd', p, v)) @ w_o


def setup_inputs(seed: int = 0) -> dict:
    key = jax.random.key(seed)
    ks = jax.random.split(key, 24)
    nrm = lambda k, shape, scale: scale * jax.random.normal(k, shape, F32)
    beta = (8 * DEPTH) ** -0.25
    mlstm_gate_b = jnp.concatenate(
        [nrm(ks[17], (DEPTH, HEADS), 0.1),
         jnp.linspace(3.0, 6.0, HEADS, dtype=F32)[None, :] + nrm(ks[18], (DEPTH, HEADS), 0.1)], axis=-1)
    return {
        'x': nrm(ks[0], (BATCH, SEQ, D_MODEL), 1.0),
        'mem': nrm(ks[1], (BATCH, N_MEM, D_MODEL), 1.0),
        'ln_g': 1.0 + nrm(ks[2], (DEPTH, 4, D_MODEL), 0.02),
        'ln_b': nrm(ks[3], (DEPTH, 4, D_MODEL), 0.02),
        'ffn_w_in': nrm(ks[4], (DEPTH, 2, D_MODEL, 2 * D_FF), D_MODEL ** -0.5),
        'ffn_w_out': nrm(ks[5], (DEPTH, 2, D_FF, D_MODEL), beta * D_FF ** -0.5),
        'w_in': nrm(ks[6], (DEPTH, D_MODEL, N_IN), D_MODEL ** -0.5),
        'w_out': nrm(ks[7], (DEPTH, D_MODEL, D_MODEL), beta * D_MODEL ** -0.5),
        'gla_gate_w': nrm(ks[8], (DEPTH, GLA_GATE_RANK, HEADS * GLA_DK), GLA_GATE_RANK ** -0.5),
        'gla_gate_b': nrm(ks[9], (DEPTH, HEADS * GLA_DK), 0.1),
        'gla_norm_g': 1.0 + nrm(ks[10], (DEPTH, HEAD_V), 0.02),
        'hgrn_lb': nrm(ks[11], (DEPTH, HEADS * HGRN_DK), 0.5),
        'hgrn_norm_g': 1.0 + nrm(ks[12], (DEPTH, HEAD_V), 0.02),
        'diff_lambda': nrm(ks[13], (DEPTH, 4, DIFF_DK), 0.1),
        'diff_norm_g': 1.0 + nrm(ks[14], (DEPTH, 2 * DIFF_DK), 0.02),
        't5_table': nrm(ks[15], (T5_BUCKETS, HEADS), 0.5),
        'mlstm_conv_w': nrm(ks[16], (DEPTH, CONV_WIDTH, 2 * HEADS * MLSTM_DK), CONV_WIDTH ** -0.5),
        'mlstm_gate_b': mlstm_gate_b,
        'xattn_w_q': nrm(ks[19], (DEPTH, D_MODEL, D_MODEL), D_MODEL ** -0.5),
        'xattn_w_kv': nrm(ks[20], (DEPTH, D_MODEL, 2 * D_MODEL), D_MODEL ** -0.5),
        'xattn_w_o': nrm(ks[21], (DEPTH, D_MODEL, D_MODEL), beta * D_MODEL ** -0.5),
    }


def reference(x, mem, ln_g, ln_b, ffn_w_in, ffn_w_out, w_in, w_out, gla_gate_w, gla_gate_b, gla_norm_g,
              hgrn_lb, hgrn_norm_g, diff_lambda, diff_norm_g, t5_table, mlstm_conv_w, mlstm_gate_b,
              xattn_w_q, xattn_w_kv, xattn_w_o):
    alpha = (2 * DEPTH) ** 0.25
    sm = jax.nn.softmax(hgrn_lb.astype(F32), axis=0)
    lower_bounds = jnp.clip(jnp.cumsum(sm, axis=0) - sm[0], 0.0, 1.0 - 1e-6)
    for l in range(DEPTH):
        x = layer_norm(alpha * x + 0.5 * swiglu_ffn(x, ffn_w_in[l, 0], ffn_w_out[l, 0]), ln_g[l, 0], ln_b[l, 0])
        mix = hybrid_mixer(x, l, w_in[l], w_out[l], gla_gate_w[l], gla_gate_b[l], gla_norm_g[l], lower_bounds[l],
                           hgrn_norm_g[l], diff_lambda[l], diff_norm_g[l], t5_table, mlstm_conv_w[l], mlstm_gate_b[l])
        x = layer_norm(alpha * x + mix, ln_g[l, 1], ln_b[l, 1])
        x = layer_norm(alpha * x + memory_cross_attention(x, mem, xattn_w_q[l], xattn_w_kv[l], xattn_w_o[l]), ln_g[l, 2], ln_b[l, 2])
        x = layer_norm(alpha * x + 0.5 * swiglu_ffn(x, ffn_w_in[l, 1], ffn_w_out[l, 1]), ln_g[l, 3], ln_b[l, 3])
    return x
```

```python
import numpy as np
import concourse.bass as bass
import concourse.mybir as mybir
from concourse.bass_utils import run_bass_kernel_spmd

F32 = mybir.dt.float32
BF16 = mybir.dt.bfloat16
AF = mybir.ActivationFunctionType
ALU = mybir.AluOpType
AX = mybir.AxisListType

ENGS = ['sync', 'scalar', 'gpsimd', 'vector', 'tensor']
NDMA = 8


class Buf:
    __slots__ = ('name', 'lw', 'rd', 'excl')

    def __init__(self, name='', excl=False):
        self.name = name
        self.lw = None
        self.rd = {}
        self.excl = excl


class Prog:
    def __init__(self, nc):
        self.nc = nc
        self.q = {e: [] for e in ENGS}
        self.cnt = {e: 0 for e in ENGS}
        self.known = {e: {} for e in ENGS}
        self.dma_next = {e: 0 for e in ENGS}
        self.dma_val = {}
        self.ninst = 0

    def emit(self, eng, fn, reads=(), writes=(), dma=False, acc=False):
        deps = {}

        def add(t):
            if t is None:
                return
            k, v = t
            if deps.get(k, 0) < v:
                deps[k] = v
        for b in reads:
            if b.excl and b.lw is not None and b.lw[0] == eng:
                continue
            add(b.lw)
        for b in writes:
            if not (acc and b.lw is not None and b.lw[0] == eng):
                add(b.lw)
            for k, v in b.rd.items():
                add((k, v))
        if dma:
            slot = self.dma_next[eng]
            self.dma_next[eng] = (slot + 1) % NDMA
            key = ('dma', eng, slot)
            prev = self.dma_val.get(key, 0)
            if prev > 0:
                add((key, prev))
            tok = (key, prev + 16)
            self.dma_val[key] = prev + 16
        else:
            self.cnt[eng] += 1
            tok = (eng, self.cnt[eng])
        kn = self.known[eng]
        waits = []
        for k, v in deps.items():
            if kn.get(k, 0) >= v:
                continue
            kn[k] = v
            waits.append((k, v))
        self.q[eng].append((waits, fn, tok))
        self.ninst += 1
        for b in reads:
            if b.excl:
                b.lw = tok
                b.rd = {}
            elif b.rd.get(tok[0], 0) < tok[1]:
                b.rd[tok[0]] = tok[1]
        for b in writes:
            b.lw = tok
            b.rd = {}
        return tok

    def barrier(self):
        allk = {}
        for e in ENGS:
            if self.cnt[e] > 0:
                allk[e] = self.cnt[e]
        for k, v in self.dma_val.items():
            allk[k] = v
        for e in ENGS:
            kn = self.known[e]
            waits = []
            for k, v in allk.items():
                if kn.get(k, 0) < v:
                    kn[k] = v
                    waits.append((k, v))
            if waits:
                self.q[e].append((waits, None, None))

    def finalize(self):
        nc = self.nc
        self.barrier()
        sems = {}

        def sem(k):
            if k not in sems:
                nm = k if isinstance(k, str) else "d_%s_%d" % (k[1], k[2])
                sems[k] = nc.alloc_semaphore("s_" + nm)
            return sems[k]
        for e in ENGS:
            sem(e)
        for k in self.dma_val:
            sem(k)
        prog = self

        def replay(ename):
            def body(e):
                for waits, fn, tok in prog.q[ename]:
                    for k, v in waits:
                        e.wait_ge(sem(k), v)
                    if fn is None:
                        continue
                    ins = fn(e)
                    if tok[0] == ename:
                        ins.then_inc(sem(ename), 1)
                    else:
                        ins.then_inc(sem(tok[0]), 16)
            return body
        with nc.Block() as block:
            block.sync(replay('sync'))
            block.scalar(replay('scalar'))
            block.gpsimd(replay('gpsimd'))
            block.vector(replay('vector'))
            block.tensor(replay('tensor'))


class Arena:
    def __init__(self, nc, nwords, name="arena"):
        self.t = nc.alloc_sbuf_tensor(name, [128, nwords], F32)
        self.n = nwords
        self.off = 0

    def mark(self):
        return self.off

    def reset(self, m=0):
        self.off = m

    def take(self, shape, dtype, name=''):
        nel = 1
        for s in shape[1:]:
            nel *= s
        nw = nel if dtype == F32 else (nel + 1) // 2
        nw = (nw + 7) // 8 * 8
        assert self.off + nw <= self.n, "arena overflow %d+%d>%d (%s)" % (self.off, nw, self.n, name)
        v = self.t[0:shape[0], self.off:self.off + nw]
        self.off += nw
        if dtype != F32:
            v = v.bitcast(dtype)
        v = v[:, 0:nel]
        if len(shape) == 3:
            v = v.rearrange("p (a b) -> p a b", a=shape[1])
        elif len(shape) == 4:
            v = v.rearrange("p (a b c) -> p a b c", a=shape[1], b=shape[2])
        return v


D = 2048
DFF = 5632
NIN = 6680
NMEM = 256
DEPTH = 2
ALPHA = float(4.0 ** 0.25)
EPS = 1e-5
NWORDS = 51000


class KB:
    def __init__(self, T, dbg=()):
        self.T = T
        self.TB = min(512, T)
        self.NTB = T // self.TB
        self.nc = bass.Bass("TRN2", target_bir_lowering=False)
        self.P = Prog(self.nc)
        self.A = Arena(self.nc, NWORDS)
        self.ps = [self.nc.alloc_psum_tensor("ps%d" % i, [128, 512], F32) for i in range(8)]
        self.bps = [Buf('ps%d' % i, excl=True) for i in range(8)]
        self.dbg = set(dbg)
        self.rr = 0

    def dram(self, name, shape, dtype):
        kind = "ExternalOutput" if name in self.dbg else "Internal"
        return self.nc.dram_tensor(name, list(shape), dtype, kind=kind).ap()

    def dram_in(self, name, shape, dtype=F32):
        return self.nc.dram_tensor(name, list(shape), dtype, kind="ExternalInput").ap()

    def dma(self, q, out, in_, reads=(), writes=()):
        return self.P.emit(q, lambda e: e.dma_start(out=out, in_=in_), reads=reads, writes=writes, dma=True)

    def mm(self, out, lhsT, rhs, start, stop, reads, writes):
        return self.P.emit('tensor', lambda e: e.matmul(out, lhsT, rhs, start=start, stop=stop),
                           reads=reads, writes=writes, acc=True)

    def tr(self, out, in_, ident, reads, writes):
        return self.P.emit('tensor', lambda e: e.transpose(out, in_, ident), reads=reads, writes=writes, acc=True)

    def act(self, out, in_, func, reads, writes, bias=0.0, scale=1.0, acc=False):
        return self.P.emit('scalar', lambda e: e.activation(out, in_, func, bias=bias, scale=scale),
                           reads=reads, writes=writes, acc=acc)

    def tt(self, out, in0, in1, op, reads, writes, eng='vector', acc=False):
        return self.P.emit(eng, lambda e: e.tensor_tensor(out, in0, in1, op), reads=reads, writes=writes, acc=acc)

    def ts(self, out, in0, s1, s2, op0, op1, reads, writes, eng='vector', acc=False):
        if op1 is None:
            return self.P.emit(eng, lambda e: e.tensor_scalar(out, in0, s1, None, op0), reads=reads, writes=writes, acc=acc)
        return self.P.emit(eng, lambda e: e.tensor_scalar(out, in0, s1, s2, op0, op1), reads=reads, writes=writes, acc=acc)

    def stt(self, out, in0, scalar, in1, op0, op1, reads, writes, eng='vector', acc=False):
        return self.P.emit(eng, lambda e: e.scalar_tensor_tensor(out, in0, scalar, in1, op0, op1),
                           reads=reads, writes=writes, acc=acc)

    def cp(self, out, in_, reads, writes, eng='vector', acc=False):
        return self.P.emit(eng, lambda e: e.tensor_copy(out, in_), reads=reads, writes=writes, acc=acc)

    def setup_consts(self, cst, nsp):
        A = self.A
        self.cf = A.take([128, 128 + 128 + 64 + 512 + nsp], F32, 'consts')
        self.bc = Buf('consts')
        self.dma('sync', self.cf, cst, writes=[self.bc])
        self.ident = self.cf[:, 0:128]
        self.ones = self.cf[:, 128:256]
        self.maskT = self.cf[0:64, 256:320]
        self.reset = self.cf[:, 320:832]
        self.sp = self.cf[:, 832:832 + nsp]
        self.identb = A.take([128, 128], BF16, 'identb')
        self.onesb = A.take([128, 128], BF16, 'onesb')
        self.cp(self.identb, self.ident, [self.bc], [self.bc])
        self.cp(self.onesb, self.ones, [self.bc], [self.bc])
        self.P.barrier()
        self.base = A.mark()

    def transpose_in(self, x, T, TB, out_f32, out_bf):
        A = self.A
        A.reset(self.base)
        NTB = T // TB
        nsub = TB // 128
        xin = [A.take([128, D], F32) for _ in range(2)]
        bxin = [Buf(), Buf()]
        blkf = [A.take([128, 16, TB], F32) for _ in range(2)] if out_f32 is not None else None
        blkb = [A.take([128, 16, TB], BF16) for _ in range(2)]
        bblk = [Buf(), Buf()]
        n = 0
        for tb in range(NTB):
            for s in range(nsub):
                ttile = tb * nsub + s
                xi = xin[ttile % 2]
                bxi = bxin[ttile % 2]
                self.dma('sync', xi, x[ttile * 128:(ttile + 1) * 128, :], writes=[bxi])
                for g4 in range(4):
                    bank = n % 8
                    n += 1
                    for j in range(4):
                        kc = g4 * 4 + j
                        self.tr(self.ps[bank][:, j * 128:(j + 1) * 128], xi[:, kc * 128:(kc + 1) * 128], self.ident,
                                [bxi, self.bc], [self.bps[bank]])
                    src = self.ps[bank][:, :].rearrange("p (a b) -> p a b", a=4)
                    if blkf is not None:
                        self.P.emit('scalar', (lambda o, i: (lambda e: e.copy(o, i)))(
                            blkf[tb % 2][:, g4 * 4:(g4 + 1) * 4, s * 128:(s + 1) * 128], src),
                            reads=[self.bps[bank]], writes=[bblk[tb % 2]], acc=True)
                    self.cp(blkb[tb % 2][:, g4 * 4:(g4 + 1) * 4, s * 128:(s + 1) * 128], src,
                            [self.bps[bank]], [bblk[tb % 2]], acc=True)
            if blkf is not None:
                self.dma('sync', out_f32[tb], blkf[tb % 2], reads=[bblk[tb % 2]])
            self.dma('sync', out_bf[tb], blkb[tb % 2], reads=[bblk[tb % 2]])
        self.P.barrier()

    def transpose_out(self, xsrc, out):
        A = self.A
        A.reset(self.base)
        TB, NTB = self.TB, self.NTB
        nsub = TB // 128
        blk = [A.take([128, 16, TB], F32) for _ in range(2)]
        bblk = [Buf(), Buf()]
        ot = [A.take([128, D], F32) for _ in range(2)]
        bot = [Buf(), Buf()]
        n = 0
        for tb in range(NTB):
            self.dma('sync', blk[tb % 2], xsrc[tb], writes=[bblk[tb % 2]])
            for s in range(nsub):
                ttile = tb * nsub + s
                o = ot[ttile % 2]
                bo = bot[ttile % 2]
                for g4 in range(4):
                    bank = n % 8
                    n += 1
                    for j in range(4):
                        kc = g4 * 4 + j
                        self.tr(self.ps[bank][:, j * 128:(j + 1) * 128], blk[tb % 2][:, kc, s * 128:(s + 1) * 128],
                                self.ident, [bblk[tb % 2], self.bc], [self.bps[bank]])
                    if g4 % 2 == 0:
                        self.P.emit('scalar', (lambda o_, i_: (lambda e: e.copy(o_, i_)))(
                            o[:, g4 * 512:(g4 + 1) * 512], self.ps[bank][:, :]),
                            reads=[self.bps[bank]], writes=[bo], acc=True)
                    else:
                        self.cp(o[:, g4 * 512:(g4 + 1) * 512], self.ps[bank][:, :], [self.bps[bank]], [bo], acc=True)
                self.dma('sync', out[ttile * 128:(ttile + 1) * 128, :], o, reads=[bo])
        self.P.barrier()

    def gemm_fm(self, xsrc, KC, NTB, TB, w, groups, epi, MG, pre=None, tail=0):
        A = self.A
        KS = 16
        wst = [A.take([128, KS, 128], F32) for _ in range(3)]
        bwst = [Buf() for _ in range(3)]
        wbuf = [A.take([128, KC, MG * 128], BF16) for _ in range(2)]
        bwb = [Buf(), Buf()]
        xb = [A.take([128, KC, TB], BF16) for _ in range(2)]
        bxb = [Buf(), Buf()]
        wv = w.rearrange("(kc p) n -> p kc n", p=128)
        NG = len(groups)
        st = {'wp': 0}

        def pieces(g):
            lst = []
            for s, (c0, n) in enumerate(groups[g]):
                for k0 in range(0, KC, KS):
                    lst.append((g, s, c0, n, k0, min(KC, k0 + KS)))
            return lst

        def load_piece(pc):
            g, s, c0, n, k0, k1 = pc
            j = st['wp'] % 3
            st['wp'] += 1
            self.dma('sync', wst[j][:, 0:k1 - k0, 0:n], wv[:, k0:k1, c0:c0 + n], writes=[bwst[j]])
            self.cp(wbuf[g % 2][:, k0:k1, s * 128:s * 128 + n], wst[j][:, 0:k1 - k0, 0:n],
                    [bwst[j]], [bwb[g % 2]], eng='gpsimd', acc=True)

        items = [(g, tb) for g in range(NG) for tb in range(NTB)]
        for pc in pieces(0):
            load_piece(pc)
        self.dma('sync', xb[0], xsrc[0], writes=[bxb[0]])
        pend = []
        for i, (g, tb) in enumerate(items):
            if tb == 0:
                pend = pieces(g + 1) if g + 1 < NG else []
            per = (len(pend) + (NTB - tb) - 1) // (NTB - tb)
            for _ in range(per):
                load_piece(pend.pop(0))
            if i + 1 < len(items):
                self.dma('sync', xb[(i + 1) % 2], xsrc[items[i + 1][1]], writes=[bxb[(i + 1) % 2]])
            if pre is not None:
                pre(g, tb)
            tiles = []
            for s, (c0, n) in enumerate(groups[g]):
                bank = (i % 2) * 4 + s
                pst = self.ps[bank][0:n, 0:TB]
                for kc in range(KC):
                    self.mm(pst, wbuf[g % 2][:, kc, s * 128:s * 128 + n], xb[i % 2][:, kc, :], kc == 0, kc == KC - 1,
                            [bwb[g % 2], bxb[i % 2]], [self.bps[bank]])
                tiles.append((pst, self.bps[bank]))
            epi(g, tb, tiles)
        self.P.barrier()

    def layernorm(self, zsrc, gcol, bcol, out_f32, out_bf):
        A = self.A
        A.reset(self.base)
        TB, NTB = self.TB, self.NTB
        zb = [A.take([128, 16, TB], F32) for _ in range(2)]
        bzb = [Buf(), Buf()]
        ob = [A.take([128, 16, TB], BF16) for _ in range(2)]
        bob = [Buf(), Buf()]
        sq = [A.take([128, TB], F32) for _ in range(2)]
        bsq = [Buf(), Buf()]
        tmp = [A.take([128, TB], F32) for _ in range(2)]
        btmp = [Buf(), Buf()]
        mean = A.take([128, TB], F32)
        rstd = A.take([128, TB], F32)
        bst = Buf()
        self.dma('sync', zb[0], zsrc[0], writes=[bzb[0]])
        for tb in range(NTB):
            z = zb[tb % 2]
            bz = bzb[tb % 2]
            if tb + 1 < NTB:
                self.dma('sync', zb[(tb + 1) % 2], zsrc[tb + 1], writes=[bzb[(tb + 1) % 2]])
            b1, b2 = (tb % 2) * 2, (tb % 2) * 2 + 1
            p1 = self.ps[b1][:, 0:TB]
            p2 = self.ps[b2][:, 0:TB]
            for kc in range(16):
                self.mm(p1, self.ones, z[:, kc, :], kc == 0, kc == 15, [self.bc, bz], [self.bps[b1]])
                j = kc % 2
                self.act(sq[j], z[:, kc, :], AF.Square, [bz], [bsq[j]])
                self.mm(p2, self.ones, sq[j], kc == 0, kc == 15, [self.bc, bsq[j]], [self.bps[b2]])
            self.act(mean, p1, AF.Copy, [self.bps[b1]], [bst], scale=1.0 / D)
            self.tt(rstd, mean, mean, ALU.mult, [bst], [bst])
            self.stt(rstd, p2, 1.0 / D, rstd, ALU.mult, ALU.subtract, [self.bps[b2], bst], [bst])
            self.act(rstd, rstd, AF.Sqrt, [bst], [bst], bias=EPS)
            self.P.emit('vector', lambda e: e.reciprocal(rstd, rstd), reads=[bst], writes=[bst])
            for kc in range(16):
                j = kc % 2
                self.tt(tmp[j], z[:, kc, :], mean, ALU.subtract, [bz, bst], [btmp[j]])
                self.tt(tmp[j], tmp[j], rstd, ALU.mult, [btmp[j], bst], [btmp[j]], eng='gpsimd')
                self.act(z[:, kc, :], tmp[j], AF.Identity, [btmp[j], self.bc], [bz],
                         bias=self.sp[:, bcol + kc:bcol + kc + 1], scale=self.sp[:, gcol + kc:gcol + kc + 1])
                self.cp(ob[tb % 2][:, kc, :], z[:, kc, :], [bz], [bob[tb % 2]])
            self.dma('sync', out_f32[tb], z, reads=[bz])
            self.dma('sync', out_bf[tb], ob[tb % 2], reads=[bob[tb % 2]])
        self.P.barrier()

    def ffn(self, xT, xres_in, w_in, w_out, hT, z):
        A = self.A
        TB, NTB = self.TB, self.NTB
        A.reset(self.base)
        sg = [A.take([128, TB], F32) for _ in range(2)]
        bsg = [Buf(), Buf()]
        ht = [A.take([128, TB], BF16) for _ in range(3)]
        bht = [Buf() for _ in range(3)]
        st = {'r': 0}

        def epi(g, tb, tiles):
            for s in range(2):
                r = st['r']
                st['r'] += 1
                pg, bg = tiles[s]
                pu, bu = tiles[s + 2]
                self.act(sg[r % 2], pg, AF.Silu, [bg], [bsg[r % 2]])
                self.stt(ht[r % 3], pu, 0.5, sg[r % 2], ALU.mult, ALU.mult, [bu, bsg[r % 2]], [bht[r % 3]])
                self.dma('sync', hT[tb, :, 2 * g + s, :], ht[r % 3], reads=[bht[r % 3]])
        groups = [[(256 * g, 128), (256 * g + 128, 128), (DFF + 256 * g, 128), (DFF + 256 * g + 128, 128)]
                  for g in range(DFF // 256)]
        self.gemm_fm(xT, 16, NTB, TB, w_in, groups, epi, 4)
        self.gemm_resid(hT, 44, w_out, xres_in, z)

    def gemm_resid(self, src, KC, w, xres_in, z):
        A = self.A
        TB, NTB = self.TB, self.NTB
        A.reset(self.base)
        xr = [A.take([128, TB], F32) for _ in range(4)]
        bxr = [Buf() for _ in range(4)]
        zt = [A.take([128, TB], F32) for _ in range(3)]
        bzt = [Buf() for _ in range(3)]
        st = {'r': 0, 'q': 0}

        def pre(g, tb):
            for s in range(2):
                q = st['q']
                st['q'] += 1
                self.dma('sync', xr[q % 4], xres_in[tb, :, 2 * g + s, :], writes=[bxr[q % 4]])

        def epi(g, tb, tiles):
            for s in range(2):
                r = st['r']
                st['r'] += 1
                pz, bz = tiles[s]
                self.stt(zt[r % 3], xr[r % 4], ALPHA, pz, ALU.mult, ALU.add, [bxr[r % 4], bz], [bzt[r % 3]])
                self.dma('sync', z[tb, :, 2 * g + s, :], zt[r % 3], reads=[bzt[r % 3]])
        groups = [[(256 * g, 128), (256 * g + 128, 128)] for g in range(8)]
        self.gemm_fm(src, KC, NTB, TB, w, groups, epi, 2, pre=pre)


LSTRIDE = 416
SP_HLB = 2 * LSTRIDE
SP_T5 = SP_HLB + 8
NSP = SP_T5 + 128


def sp_g(l, i):
    return l * LSTRIDE + i * 32


def sp_b(l, i):
    return l * LSTRIDE + i * 32 + 16


def pack_consts(inp):
    c = np.zeros((128, 832 + NSP), np.float32)
    c[:, 0:128] = np.eye(128, dtype=np.float32)
    c[:, 128:256] = 1.0
    jj, ii = np.meshgrid(np.arange(64), np.arange(64), indexing='ij')
    c[0:64, 256:320] = (jj <= ii).astype(np.float32)
    r = np.ones(512, np.float32)
    r[::64] = 0.0
    c[:, 320:832] = r[None, :]
    sp = c[:, 832:]

    def fm(v):
        return np.asarray(v, np.float32).reshape(-1, 128).T
    for l in range(DEPTH):
        o = l * LSTRIDE
        for i in range(4):
            sp[:, o + i * 32:o + i * 32 + 16] = fm(inp['ln_g'][l, i])
            sp[:, o + i * 32 + 16:o + i * 32 + 32] = fm(inp['ln_b'][l, i])
        sp[:, o + 128:o + 130] = fm(inp['gla_gate_b'][l])
        sp[:, o + 130] = inp['gla_norm_g'][l]
        sp[:, o + 131] = inp['hgrn_norm_g'][l]
        sp[:, o + 132] = inp['diff_norm_g'][l]
        for tap in range(4):
            sp[:, o + 133 + tap * 4:o + 133 + tap * 4 + 4] = fm(inp['mlstm_conv_w'][l, tap])
        sp[0:8, o + 149] = inp['mlstm_gate_b'][l]
        sp[:, o + 150:o + 406] = np.asarray(inp['diff_lambda'][l], np.float32).reshape(1, 256)
        sp[:, SP_HLB + 4 * l:SP_HLB + 4 * l + 4] = fm(inp['hgrn_lb'][l])
    sp[:, SP_T5:SP_T5 + 128] = np.asarray(inp['t5_table'], np.float32).reshape(1, 128)
    return c


SEGS = [('a_q', 0, 256), ('a_k', 256, 256), ('a_v', 512, 512), ('a_lr', 1024, 16), ('a_r', 1040, 512),
        ('b_q', 1552, 512), ('b_f', 2064, 512), ('b_i', 2576, 512), ('b_g', 3088, 512),
        ('c_q', 3600, 512), ('c_k', 4112, 512), ('c_v', 4624, 512),
        ('d_q', 5136, 256), ('d_k', 5392, 256), ('d_v', 5648, 512), ('d_if', 6160, 8), ('d_o', 6168, 512)]


def _kb_method(f):
    setattr(KB, f.__name__, f)
    return f


@_kb_method
def alloc_scratch(self):
    NTB, TB = self.NTB, self.TB
    S = {}

    def fmt(name, nch, dt):
        S[name] = self.dram(name, [NTB, 128, nch, TB], dt)
    for i in range(2):
        fmt('xres%d' % i, 16, F32)
    fmt('xT', 16, BF16)
    fmt('z', 16, F32)
    fmt('hT', 44, BF16)
    fmt('qx', 16, BF16)
    fmt('ox', 16, BF16)
    fmt('mixT', 16, BF16)
    for nm, n, dt in [('gla_q', 2, BF16), ('gla_k', 2, BF16), ('gla_v', 4, BF16), ('gla_r', 4, F32), ('gla_g', 2, F32),
                      ('hg_q', 4, BF16), ('hg_k', 4, BF16), ('hg_g', 4, F32), ('hg_v', 4, BF16), ('hg_o', 4, F32),
                      ('df_q', 4, BF16), ('df_k', 4, BF16), ('df_v', 4, BF16),
                      ('ml_q', 2, F32), ('ml_k', 2, F32), ('ml_v', 4, BF16), ('ml_o', 4, F32),
                      ('ml_qc', 2, BF16), ('ml_kc', 2, BF16), ('ml_g', 2, F32)]:
        fmt(nm, n, dt)
    S['gla_lr'] = self.dram('gla_lr', [NTB, 16, TB], F32)
    S['ml_if'] = self.dram('ml_if', [NTB, 8, TB], F32)
    S['memT'] = self.dram('memT', [1, 128, 16, NMEM], BF16)
    S['kxT'] = self.dram('kxT', [1, 128, 16, NMEM], BF16)
    S['vxT'] = self.dram('vxT', [1, 128, 16, NMEM], BF16)
    self.S = S


@_kb_method
def hgrn_consts(self):
    A = self.A
    self.hl = A.take([128, 16], F32, 'hl')
    self.bhl = Buf()
    hl = self.hl
    h0 = self.sp[:, SP_HLB:SP_HLB + 4]
    h1 = self.sp[:, SP_HLB + 4:SP_HLB + 8]
    self.P.emit('vector', lambda e: e.memset(hl[:, 0:4], 1e-12), writes=[self.bhl])
    self.P.emit('vector', lambda e: e.memset(hl[:, 4:8], 1.0), writes=[self.bhl])
    self.tt(hl[:, 8:12], h1, h0, ALU.subtract, [self.bc], [self.bhl])
    self.act(hl[:, 8:12], hl[:, 8:12], AF.Sigmoid, [self.bhl], [self.bhl])
    self.ts(hl[:, 8:12], hl[:, 8:12], 1.0 - 1e-6, 0.0, ALU.min, ALU.max, [self.bhl], [self.bhl])
    self.ts(hl[:, 12:16], hl[:, 8:12], -1.0, 1.0, ALU.mult, ALU.add, [self.bhl], [self.bhl])
    self.ts(hl[:, 8:12], hl[:, 8:12], 1e-12, None, ALU.max, None, [self.bhl], [self.bhl])
    self.P.barrier()
    self.base = A.mark()


@_kb_method
def log_sigmoid(self, out, x, t1, reads, bufs, post_scale=1.0):
    bx, bo, bt = bufs
    self.act(t1, x, AF.Abs, reads + [bx], [bt])
    self.act(t1, t1, AF.Exp, [bt], [bt], scale=-1.0)
    self.act(t1, t1, AF.Ln, [bt], [bt], bias=1.0)
    self.ts(out, x, 0.0, None, ALU.min, None, [bx], [bo])
    self.tt(out, out, t1, ALU.subtract, [bo, bt], [bo])
    if post_scale != 1.0:
        self.ts(out, out, post_scale, None, ALU.mult, None, [bo], [bo])


@_kb_method
def in_proj(self, l, w_in):
    A = self.A
    A.reset(self.base)
    TB, NTB, S = self.TB, self.NTB, self.S
    chunks = []
    for nm, c0, wd in SEGS:
        for i in range(0, wd, 128):
            chunks.append((nm, i // 128, c0 + i, min(128, wd - i)))
    groups = [chunks[i:i + 4] for i in range(0, len(chunks), 4)]
    ef = [A.take([128, TB], F32) for _ in range(4)]
    bef = [Buf() for _ in range(4)]
    eb = [A.take([128, TB], BF16) for _ in range(4)]
    beb = [Buf() for _ in range(4)]
    st = {'f': 0, 'b': 0}
    o = l * LSTRIDE

    def nf():
        st['f'] += 1
        return ef[st['f'] % 4], bef[st['f'] % 4]

    def nb():
        st['b'] += 1
        return eb[st['b'] % 4], beb[st['b'] % 4]
    plain = {'a_q': ('gla_q', 64 ** -0.5), 'a_k': ('gla_k', 1.0), 'a_v': ('gla_v', 1.0), 'b_i': ('hg_v', 1.0),
             'c_q': ('df_q', 64 ** -0.5), 'c_k': ('df_k', 1.0), 'c_v': ('df_v', 1.0), 'd_v': ('ml_v', 1.0)}
    actf = {'a_r': ('gla_r', AF.Silu), 'b_g': ('hg_o', AF.Silu), 'd_o': ('ml_o', AF.Sigmoid),
            'd_q': ('ml_q', AF.Copy), 'd_k': ('ml_k', AF.Copy)}

    def epi(g, tb, tiles):
        for (nm, idx, c0, n), (pst, bp) in zip(groups[g], tiles):
            if nm in plain:
                dst, sc = plain[nm]
                t, bt = nb()
                self.act(t, pst, AF.Copy, [bp], [bt], scale=sc)
                self.dma('sync', S[dst][tb, :, idx, :], t, reads=[bt])
            elif nm in actf:
                dst, fn = actf[nm]
                t, bt = nf()
                self.act(t, pst, fn, [bp], [bt])
                self.dma('sync', S[dst][tb, :, idx, :], t, reads=[bt])
            elif nm == 'a_lr':
                t, bt = nf()
                self.act(t[0:16, :], pst, AF.Copy, [bp], [bt])
                self.dma('sync', S['gla_lr'][tb], t[0:16, :], reads=[bt])
            elif nm == 'd_if':
                t, bt = nf()
                self.act(t[0:8, :], pst, AF.Identity, [bp, self.bc], [bt], bias=self.sp[0:8, o + 149:o + 150])
                self.dma('sync', S['ml_if'][tb], t[0:8, :], reads=[bt])
            elif nm == 'b_q':
                t, bt = nf()
                self.act(t, pst, AF.Silu, [bp], [bt])
                t2, bt2 = nb()
                self.ts(t2, t, 128 ** -0.5, None, ALU.mult, None, [bt], [bt2])
                self.dma('sync', S['hg_q'][tb, :, idx, :], t2, reads=[bt2])
            elif nm == 'b_f':
                sg, bsg = nf()
                self.act(sg, pst, AF.Sigmoid, [bp], [bsg])
                f, bf_ = nf()
                lbc = self.hl[:, l * 8 + idx:l * 8 + idx + 1]
                oml = self.hl[:, l * 8 + 4 + idx:l * 8 + 4 + idx + 1]
                self.ts(f, sg, oml, lbc, ALU.mult, ALU.add, [bsg, self.bhl], [bf_])
                self.act(f, f, AF.Ln, [bf_], [bf_])
                self.dma('sync', S['hg_g'][tb, :, idx, :], f, reads=[bf_])
                self.ts(sg, sg, -1.0, 1.0, ALU.mult, ALU.add, [bsg], [bsg])
                t2, bt2 = nb()
                self.ts(t2, sg, oml, None, ALU.mult, None, [bsg, self.bhl], [bt2])
                self.dma('sync', S['hg_k'][tb, :, idx, :], t2, reads=[bt2])
            else:
                raise ValueError(nm)
    self.gemm_fm(S['xT'], 16, NTB, TB, w_in, [[(c[2], c[3]) for c in g] for g in groups], epi, 4)


@_kb_method
def gla_gate(self, l, gate_w):
    A = self.A
    A.reset(self.base)
    TB, NTB, S = self.TB, self.NTB, self.S
    o = l * LSTRIDE
    gw = A.take([16, 256], F32)
    bgw = Buf()
    self.dma('sync', gw, gate_w, writes=[bgw])
    lr = [A.take([16, TB], F32) for _ in range(2)]
    blr = [Buf(), Buf()]
    xs = [A.take([128, TB], F32) for _ in range(2)]
    bxs = [Buf(), Buf()]
    t1 = [A.take([128, TB], F32) for _ in range(2)]
    bt1 = [Buf(), Buf()]
    n = 0
    for tb in range(NTB):
        self.dma('sync', lr[tb % 2], S['gla_lr'][tb], writes=[blr[tb % 2]])
        for t in range(2):
            bank = n % 8
            j = n % 2
            n += 1
            pst = self.ps[bank][:, 0:TB]
            self.mm(pst, gw[:, t * 128:(t + 1) * 128], lr[tb % 2], True, True, [bgw, blr[tb % 2]], [self.bps[bank]])
            self.act(xs[j], pst, AF.Identity, [self.bps[bank], self.bc], [bxs[j]], bias=self.sp[:, o + 128 + t:o + 129 + t])
            self.log_sigmoid(xs[j], xs[j], t1[j], [], (bxs[j], bxs[j], bt1[j]), post_scale=1.0 / 16.0)
            self.dma('sync', S['gla_g'][tb, :, t, :], xs[j], reads=[bxs[j]])
    self.P.barrier()


@_kb_method
def mlstm_prep(self, l, sel):
    A = self.A
    A.reset(self.base)
    TB, NTB, S = self.TB, self.NTB, self.S
    o = l * LSTRIDE
    sl = A.take([8, 512], F32)
    bsl = Buf()
    self.dma('sync', sl, sel, writes=[bsl])
    gi = [A.take([8, TB], F32) for _ in range(2)]
    bgi = [Buf(), Buf()]
    ei = [A.take([128, TB], F32) for _ in range(2)]
    bei = [Buf(), Buf()]
    gt = [A.take([128, TB], F32) for _ in range(2)]
    bgt = [Buf(), Buf()]
    t1 = [A.take([128, TB], F32) for _ in range(2)]
    bt1 = [Buf(), Buf()]
    xr = [A.take([128, TB + 3], F32) for _ in range(2)]
    bxr = [Buf(), Buf()]
    cv = [A.take([128, TB], F32) for _ in range(2)]
    bcv = [Buf(), Buf()]
    ob = [A.take([128, TB], BF16) for _ in range(2)]
    bob = [Buf(), Buf()]
    n = 0
    m = 0
    for tb in range(NTB):
        self.dma('sync', gi[tb % 2], S['ml_if'][tb], writes=[bgi[tb % 2]])
        for t in range(2):
            j = n % 2
            b1, b2 = (n % 4) * 2, (n % 4) * 2 + 1
            n += 1
            p1 = self.ps[b1][:, 0:TB]
            p2 = self.ps[b2][:, 0:TB]
            self.mm(p1, sl[:, t * 128:(t + 1) * 128], gi[tb % 2], True, True, [bsl, bgi[tb % 2]], [self.bps[b1]])
            self.mm(p2, sl[:, 256 + t * 128:256 + (t + 1) * 128], gi[tb % 2], True, True, [bsl, bgi[tb % 2]], [self.bps[b2]])
            self.act(ei[j], p1, AF.Exp, [self.bps[b1]], [bei[j]])
            self.act(gt[j], p2, AF.Copy, [self.bps[b2]], [bgt[j]])
            self.log_sigmoid(gt[j], gt[j], t1[j], [], (bgt[j], bgt[j], bt1[j]))
            self.dma('sync', S['ml_g'][tb, :, t, :], gt[j], reads=[bgt[j]])
            for which in range(2):
                src = S['ml_q'] if which == 0 else S['ml_k']
                dst = S['ml_qc'] if which == 0 else S['ml_kc']
                jj = m % 2
                m += 1
                x_ = xr[jj]
                if tb == 0:
                    self.P.emit('vector', (lambda a: (lambda e: e.memset(a, 0.0)))(x_[:, 0:3]), writes=[bxr[jj]])
                else:
                    self.dma('sync', x_[:, 0:3], src[tb - 1, :, t, TB - 3:TB], writes=[bxr[jj]])
                self.dma('sync', x_[:, 3:3 + TB], src[tb, :, t, :], writes=[bxr[jj]])
                c_ = cv[jj]
                tile_ = which * 2 + t
                for tap in range(4):
                    cw = self.sp[:, o + 133 + tap * 4 + tile_:o + 134 + tap * 4 + tile_]
                    if tap == 0:
                        self.ts(c_, x_[:, 0:TB], cw, None, ALU.mult, None, [bxr[jj], self.bc], [bcv[jj]])
                    else:
                        self.stt(c_, x_[:, tap:tap + TB], cw, c_, ALU.mult, ALU.add, [bxr[jj], self.bc, bcv[jj]], [bcv[jj]])
                if which == 0:
                    self.act(ob[jj], c_, AF.Silu, [bcv[jj]], [bob[jj]])
                else:
                    self.act(c_, c_, AF.Silu, [bcv[jj]], [bcv[jj]])
                    self.stt(ob[jj], c_, 64 ** -0.5, ei[j], ALU.mult, ALU.mult, [bcv[jj], bei[j]], [bob[jj]])
                self.dma('sync', dst[tb, :, t, :], ob[jj], reads=[bob[jj]])
    self.P.barrier()


@_kb_method
def gla_core(self, qsrc, ksrc, gsrc, vsrc, dk, mode, gsrc_gate, ngcol, dst_c0):
    A = self.A
    A.reset(self.base)
    TB, NTB, S = self.TB, self.NTB, self.S
    NCH = TB // 64
    NH = 4
    ml = (mode == 'mlstm')
    PS = self.ps
    ABANK = (4, 7)
    trb = PS[5][:, :].bitcast(BF16)
    Sst = [A.take([dk, 128], F32) for _ in range(NH)]
    bS = [Buf() for _ in range(NH)]
    Snt = [A.take([dk, 128], F32) for _ in range(NH)] if ml else None
    bSn = [Buf() for _ in range(NH)]
    for h in range(NH):
        self.P.emit('vector', (lambda a: (lambda e: e.memset(a, 0.0)))(Sst[h]), writes=[bS[h]])
        if ml:
            self.P.emit('vector', (lambda a: (lambda e: e.memset(a, 0.0)))(Snt[h]), writes=[bSn[h]])

    def ring(n, shape, dt):
        return [A.take(shape, dt) for _ in range(n)], [Buf() for _ in range(n)]
    qb, bqb = ring(2, [dk, TB], BF16)
    kb_, bkb = ring(2, [dk, TB], BF16)
    gb, bgb = ring(2, [dk, TB], F32)
    vb, bvb = ring(2, [128, TB], BF16)
    cum, bcum = ring(2, [dk, TB], F32)
    eq, beq = ring(2, [dk, TB], F32)
    ek, bek = ring(2, [dk, TB], F32)
    qt, bqt = ring(2, [dk, TB], BF16)
    kt, bkt = ring(2, [dk, TB], BF16)
    esm, besm = ring(2, [dk, 2 * NCH], F32)
    vk, bvk = ring(3, [64, 128 + dk], BF16)
    Sp, bSp = ring(2, [dk, 128], BF16)
    Snp, bSnp = ring(2, [dk, 128], BF16)
    at, bat = ring(4, [64, 64], BF16)
    osb, bosb = ring(2, [128, TB], F32)
    dsb, bdsb = ring(2, [128, TB], F32)
    gate, bgate = ring(2, [128, TB], F32)
    w1, bw1 = ring(2, [128, TB], F32)
    yb, byb = ring(2, [128, TB], BF16)
    cnt = {'c': 0, 'a': 0}

    def load(tb, h, j):
        tile_, r0 = (h * dk) // 128, (h * dk) % 128
        self.dma('sync', qb[j], qsrc[tb, r0:r0 + dk, tile_, :], writes=[bqb[j]])
        self.dma('sync', kb_[j], ksrc[tb, r0:r0 + dk, tile_, :], writes=[bkb[j]])
        self.dma('sync', gb[j], gsrc[tb, r0:r0 + dk, tile_, :], writes=[bgb[j]])
        self.dma('sync', vb[j], vsrc[tb, :, h, :], writes=[bvb[j]])
        self.dma('sync', gate[j], gsrc_gate[tb, :, h, :], writes=[bgate[j]])
    items = [(tb, h) for tb in range(NTB) for h in range(NH)]
    load(items[0][0], items[0][1], 0)
    for it, (tb, h) in enumerate(items):
        j = it % 2
        if it + 1 < len(items):
            load(items[it + 1][0], items[it + 1][1], (it + 1) % 2)
        self.P.emit('vector', (lambda o_, m_, g_: (lambda e: e.tensor_tensor_scan(o_, m_, g_, 0.0, ALU.mult, ALU.add)))(
            cum[j], self.reset[0:dk, 0:TB], gb[j]), reads=[self.bc, bgb[j]], writes=[bcum[j]])
        c3 = cum[j].rearrange("p (c t) -> p c t", t=64)
        e3 = eq[j].rearrange("p (c t) -> p c t", t=64)
        self.tt(e3, c3, c3[:, :, 31:32].broadcast_to([dk, NCH, 64]), ALU.subtract, [bcum[j]], [beq[j]])
        self.act(ek[j], eq[j], AF.Exp, [beq[j]], [bek[j]], scale=-1.0)
        self.act(eq[j], eq[j], AF.Exp, [beq[j]], [beq[j]])
        self.act(esm[j][:, 0:NCH], c3[:, :, 31], AF.Exp, [bcum[j]], [besm[j]])
        self.act(esm[j][:, NCH:2 * NCH], c3[:, :, 63], AF.Exp, [bcum[j]], [besm[j]], acc=True)
        self.tt(qt[j], qb[j], eq[j], ALU.mult, [bqb[j], beq[j]], [bqt[j]])
        self.tt(kt[j], kb_[j], ek[j], ALU.mult, [bkb[j], bek[j]], [bkt[j]], eng='gpsimd')
        for c in range(NCH):
            cs = slice(c * 64, (c + 1) * 64)
            n = cnt['c']
            cnt['c'] += 1
            self.tr(trb[0:64, 0:128], vb[j][:, cs], self.identb, [bvb[j], self.bc], [self.bps[5]])
            self.tr(trb[0:64, 128:128 + dk], kt[j][:, cs], self.identb[0:dk, 0:dk], [bkt[j], self.bc], [self.bps[5]])
            v_ = vk[n % 3]
            bv_ = bvk[n % 3]
            self.cp(v_, trb[0:64, 0:128 + dk], [self.bps[5]], [bv_])
            ktm = v_[:, 128:128 + dk]
            sj = n % 2
            self.ts(Sp[sj], Sst[h], esm[j][:, c:c + 1], None, ALU.mult, None, [bS[h], besm[j]], [bSp[sj]])
            if ml:
                self.ts(Snp[sj], Snt[h], esm[j][:, c:c + 1], None, ALU.mult, None, [bSn[h], besm[j]], [bSnp[sj]],
                        eng='gpsimd')
            a = cnt['a']
            cnt['a'] += 1
            ab = ABANK[a % 2]
            aps = PS[ab][0:64, 0:64]
            self.mm(aps, kt[j][:, cs], qt[j][:, cs], True, True, [bkt[j], bqt[j]], [self.bps[ab]])
            self.tt(at[a % 4], aps, self.maskT, ALU.mult, [self.bps[ab], self.bc], [bat[a % 4]])
            self.mm(PS[0][:, cs], v_[:, 0:128], at[a % 4], True, False, [bv_, bat[a % 4]], [self.bps[0]])
            self.mm(PS[0][:, cs], Sp[sj], qt[j][:, cs], False, True, [bSp[sj], bqt[j]], [self.bps[0]])
            if ml:
                self.mm(PS[1][:, cs], self.onesb[0:64, :], at[a % 4], True, False, [self.bc, bat[a % 4]], [self.bps[1]])
                self.mm(PS[1][:, cs], Snp[sj], qt[j][:, cs], False, True, [bSnp[sj], bqt[j]], [self.bps[1]])
            kvs = PS[6][0:dk, 0:128]
            self.mm(kvs, ktm, v_[:, 0:128], True, True, [bv_], [self.bps[6]])
            if ml:
                kvn = PS[6][0:dk, 256:384]
                self.mm(kvn, ktm, self.onesb[0:64, :], True, True, [bv_, self.bc], [self.bps[6]])
            self.ts(Sst[h], Sst[h], esm[j][:, NCH + c:NCH + c + 1], None, ALU.mult, None, [bS[h], besm[j]], [bS[h]])
            self.stt(Sst[h], kvs, eq[j][:, c * 64 + 63:c * 64 + 64], Sst[h], ALU.mult, ALU.add,
                     [self.bps[6], beq[j], bS[h]], [bS[h]])
            if ml:
                self.ts(Snt[h], Snt[h], esm[j][:, NCH + c:NCH + c + 1], None, ALU.mult, None,
                        [bSn[h], besm[j]], [bSn[h]], eng='gpsimd')
                self.stt(Snt[h], kvn, eq[j][:, c * 64 + 63:c * 64 + 64], Snt[h], ALU.mult, ALU.add,
                         [self.bps[6], beq[j], bSn[h]], [bSn[h]])
        r = j
        self.act(osb[r], PS[0][:, 0:TB], AF.Copy, [self.bps[0]], [bosb[r]])
        if ml:
            self.act(dsb[r], PS[1][:, 0:TB], AF.Abs, [self.bps[1]], [bdsb[r]])
            self.ts(dsb[r], dsb[r], 1.0, None, ALU.max, None, [bdsb[r]], [bdsb[r]])
            self.P.emit('vector', (lambda a_: (lambda e: e.reciprocal(a_, a_)))(dsb[r]), reads=[bdsb[r]], writes=[bdsb[r]])
            self.tt(w1[r], osb[r], dsb[r], ALU.mult, [bosb[r], bdsb[r]], [bw1[r]])
            self.tt(yb[r], w1[r], gate[r], ALU.mult, [bw1[r], bgate[r]], [byb[r]])
        else:
            self.act(w1[r], osb[r], AF.Square, [bosb[r]], [bw1[r]])
            self.mm(PS[2][:, 0:TB], self.ones, w1[r], True, True, [self.bc, bw1[r]], [self.bps[2]])
            self.act(w1[r], PS[2][:, 0:TB], AF.Sqrt, [self.bps[2]], [bw1[r]], bias=EPS, scale=1.0 / 128.0)
            self.P.emit('vector', (lambda a_: (lambda e: e.reciprocal(a_, a_)))(w1[r]), reads=[bw1[r]], writes=[bw1[r]])
            self.tt(w1[r], w1[r], osb[r], ALU.mult, [bw1[r], bosb[r]], [bw1[r]])
            self.stt(yb[r], w1[r], self.sp[:, ngcol:ngcol + 1], gate[r], ALU.mult, ALU.mult,
                     [bw1[r], self.bc, bgate[r]], [byb[r]])
        self.dma('sync', S['mixT'][tb, :, dst_c0 + h, :], yb[r], reads=[byb[r]])
    self.P.barrier()


@_kb_method
def t5_bias(self, oh):
    A = self.A
    self.bt5 = A.take([128, 4, 2, 128], F32, 't5b')
    self.b31 = A.take([128, 4], F32, 'b31')
    self.bbt = Buf()
    m = A.mark()
    ohs = A.take([128, 32 * 128 + 128], F32)
    boh = Buf()
    prod = A.take([128, 32, 128], F32)
    bpr = Buf()
    t5 = self.sp[:, SP_T5:SP_T5 + 128].rearrange("p (b h) -> p b h", h=4)
    self.cp(self.b31, t5[:, 31, :], [self.bc], [self.bbt])
    for ty in range(2):
        self.dma('sync', ohs, oh[ty], writes=[boh])
        o3 = ohs[:, 0:4096].rearrange("p (b i) -> p b i", b=32)
        for h in range(4):
            self.tt(prod, o3, t5[:, :, h:h + 1].broadcast_to([128, 32, 128]), ALU.mult, [boh, self.bc], [bpr])
            self.P.emit('vector', (lambda o_, i_: (lambda e: e.tensor_reduce(o_, i_, AX.X, ALU.add)))(
                self.bt5[:, h, ty, :], prod.rearrange("p b i -> p i b")), reads=[bpr], writes=[self.bbt])
            self.tt(self.bt5[:, h, ty, :], self.bt5[:, h, ty, :], ohs[:, 4096:4224], ALU.add, [self.bbt, boh], [self.bbt])
    self.P.barrier()
    A.reset(m)
    self.base = A.mark()


@_kb_method
def diff_attn(self, l):
    A = self.A
    A.reset(self.base)
    TB, NTB, S, T = self.TB, self.NTB, self.S, self.T
    o = l * LSTRIDE
    NKT = T // 128
    nsub = TB // 128
    lam_init = 0.8 - 0.6 * float(np.exp(-0.3 * l))
    PS = self.ps
    lm = A.take([128, 8], F32)
    blm = Buf()
    dl = self.sp[:, o + 150:o + 406]
    pr = A.take([128, 128], F32)
    self.tt(pr[:, 0:64], dl[:, 0:64], dl[:, 64:128], ALU.mult, [self.bc], [blm])
    self.tt(pr[:, 64:128], dl[:, 128:192], dl[:, 192:256], ALU.mult, [self.bc], [blm])
    self.P.emit('vector', lambda e: e.tensor_reduce(lm[:, 0:2], pr.rearrange("p (a b) -> p a b", a=2), AX.X, ALU.add),
                reads=[blm], writes=[blm])
    self.act(lm[:, 0:2], lm[:, 0:2], AF.Exp, [blm], [blm])
    self.tt(lm[:, 2:3], lm[:, 1:2], lm[:, 0:1], ALU.subtract, [blm], [blm])
    self.ts(lm[:, 3:4], lm[:, 2:3], -lam_init, None, ALU.add, None, [blm], [blm])
    self.ts(lm[:, 4:5], self.sp[:, o + 132:o + 133], 1.0 - lam_init, None, ALU.mult, None, [self.bc], [blm])
    neglam = lm[:, 3:4]
    ngs = lm[:, 4:5]
    qa = A.take([128, T], BF16)
    ka = A.take([128, T], BF16)
    va = A.take([128, T], BF16)
    bq, bk, bv = Buf(), Buf(), Buf()
    vt = A.take([128, NKT, 128], BF16)
    bvt = Buf()
    trb = [PS[4][:, :].bitcast(BF16), PS[5][:, :].bitcast(BF16)]

    def ring(n, shape, dt):
        return [A.take(shape, dt) for _ in range(n)], [Buf() for _ in range(n)]
    p1, bp1 = ring(2, [128, TB], BF16)
    p2, bp2 = ring(2, [128, TB], BF16)
    tmp, btmp = ring(2, [128, 128], F32)
    w1, bw1 = ring(2, [128, TB], F32)
    w2, bw2 = ring(2, [128, TB], F32)
    yb, byb = ring(2, [128, TB], BF16)
    nn = 0
    tn = [0]
    for h in range(4):
        for tb in range(NTB):
            self.dma('sync', qa[:, tb * TB:(tb + 1) * TB], S['df_q'][tb, :, h, :], writes=[bq])
            self.dma('sync', ka[:, tb * TB:(tb + 1) * TB], S['df_k'][tb, :, h, :], writes=[bk])
            self.dma('sync', va[:, tb * TB:(tb + 1) * TB], S['df_v'][tb, :, h, :], writes=[bv])
        for kt_ in range(NKT):
            bank = 4 + kt_ % 2
            self.tr(trb[kt_ % 2][:, 0:128], va[:, kt_ * 128:(kt_ + 1) * 128], self.identb, [bv, self.bc], [self.bps[bank]])
            self.cp(vt[:, kt_, :], trb[kt_ % 2][:, 0:128], [self.bps[bank]], [bvt], acc=True)
        b31 = self.b31[:, h:h + 1]
        for I in range(NTB):
            qs = slice(I * TB, (I + 1) * TB)
            last = nsub * I + nsub - 1
            for J in range(last + 1):
                a = J - nsub * I
                st_ = nn % 2
                nn += 1
                sA, sB = PS[4 + st_ * 2], PS[5 + st_ * 2]
                bA, bB = self.bps[4 + st_ * 2], self.bps[5 + st_ * 2]
                ks = slice(J * 128, (J + 1) * 128)
                c0 = max(a, 0) * 128
                self.mm(sA[:, 0:TB], ka[0:64, ks], qa[0:64, qs], True, True, [bk, bq], [bA])
                self.mm(sB[:, 0:TB], ka[64:128, ks], qa[64:128, qs], True, True, [bk, bq], [bB])
                for (sX, bX, pX, bpX) in ((sA, bA, p1[st_], bp1[st_]), (sB, bB, p2[st_], bp2[st_])):
                    first = True
                    if c0 > 0:
                        self.P.emit('gpsimd', (lambda a_: (lambda e: e.memset(a_, 0.0)))(pX[:, 0:c0]), writes=[bpX])
                    cc = c0
                    for sb_ in range(max(a, 0), nsub):
                        ty = sb_ - a
                        if a < -1 or ty >= 2:
                            break
                        if ty < 0:
                            continue
                        if a >= -1 and ty in (0, 1):
                            tj = tn[0] % 2
                            tn[0] += 1
                            sl_ = slice(sb_ * 128, (sb_ + 1) * 128)
                            self.tt(tmp[tj], sX[:, sl_], self.bt5[:, h, ty, :], ALU.add, [bX, self.bbt], [btmp[tj]])
                            self.act(pX[:, sl_], tmp[tj], AF.Exp, [btmp[tj]], [bpX], acc=not first)
                            first = False
                            cc = (sb_ + 1) * 128
                    if cc < TB:
                        self.act(pX[:, cc:TB], sX[:, cc:TB], AF.Exp, [bX, self.bbt], [bpX], bias=b31, acc=not first)
                v_ = vt[:, J, :]
                accs = ((0, v_, p1[st_], bp1[st_]), (1, self.onesb, p1[st_], bp1[st_]),
                        (2, v_, p2[st_], bp2[st_]), (3, self.onesb, p2[st_], bp2[st_]))
                for (bk_, lhs, pX, bpX) in accs:
                    self.mm(PS[bk_][:, 0:TB], lhs, pX[:, 0:TB], J == 0, J == last, [bvt, self.bc, bpX], [self.bps[bk_]])
            r = I % 2
            self.P.emit('vector', (lambda o_, i_: (lambda e: e.reciprocal(o_, i_)))(w1[r], PS[1][:, 0:TB]),
                        reads=[self.bps[1]], writes=[bw1[r]])
            self.tt(w1[r], w1[r], PS[0][:, 0:TB], ALU.mult, [bw1[r], self.bps[0]], [bw1[r]])
            self.P.emit('vector', (lambda o_, i_: (lambda e: e.reciprocal(o_, i_)))(w2[r], PS[3][:, 0:TB]),
                        reads=[self.bps[3]], writes=[bw2[r]])
            self.tt(w2[r], w2[r], PS[2][:, 0:TB], ALU.mult, [bw2[r], self.bps[2]], [bw2[r]])
            self.stt(w1[r], w2[r], neglam, w1[r], ALU.mult, ALU.add, [bw2[r], blm, bw1[r]], [bw1[r]])
            self.act(w2[r], w1[r], AF.Square, [bw1[r]], [bw2[r]])
            self.mm(PS[4][:, 0:TB], self.ones, w2[r], True, True, [self.bc, bw2[r]], [self.bps[4]])
            self.act(w2[r], PS[4][:, 0:TB], AF.Sqrt, [self.bps[4]], [bw2[r]], bias=EPS, scale=1.0 / 128.0)
            self.P.emit('vector', (lambda a_: (lambda e: e.reciprocal(a_, a_)))(w2[r]), reads=[bw2[r]], writes=[bw2[r]])
            self.stt(yb[r], w1[r], ngs, w2[r], ALU.mult, ALU.mult, [bw1[r], blm, bw2[r]], [byb[r]])
            self.dma('sync', S['mixT'][I, :, 8 + h, :], yb[r], reads=[byb[r]])
    self.P.barrier()


@_kb_method
def xattn(self, l, w_q, w_kv, w_o, ln_i):
    A = self.A
    TB, NTB, S = self.TB, self.NTB, self.S
    PS = self.ps
    for (dst, c0) in (('kxT', 0), ('vxT', D)):
        A.reset(self.base)
        eb = [A.take([128, NMEM], BF16) for _ in range(3)]
        beb = [Buf() for _ in range(3)]
        st = {'r': 0}

        def epi(g, tb, tiles, dst=dst, st=st, eb=eb, beb=beb):
            for s, (pst, bp) in enumerate(tiles):
                r = st['r'] % 3
                st['r'] += 1
                self.act(eb[r], pst, AF.Copy, [bp], [beb[r]])
                self.dma('sync', S[dst][0, :, 4 * g + s, :], eb[r], reads=[beb[r]])
        groups = [[(c0 + 512 * g + 128 * s, 128) for s in range(4)] for g in range(4)]
        self.gemm_fm(S['memT'], 16, 1, NMEM, w_kv, groups, epi, 4)
    A.reset(self.base)
    eb2 = [A.take([128, TB], BF16) for _ in range(3)]
    beb2 = [Buf() for _ in range(3)]
    st2 = {'r': 0}

    def epi2(g, tb, tiles):
        for s, (pst, bp) in enumerate(tiles):
            r = st2['r'] % 3
            st2['r'] += 1
            self.act(eb2[r], pst, AF.Copy, [bp], [beb2[r]])
            self.dma('sync', S['qx'][tb, :, 4 * g + s, :], eb2[r], reads=[beb2[r]])
    groups = [[(512 * g + 128 * s, 128) for s in range(4)] for g in range(4)]
    self.gemm_fm(S['xT'], 16, NTB, TB, w_q, groups, epi2, 4)
    A.reset(self.base)
    kx = A.take([128, 16, NMEM], BF16)
    vx = A.take([128, 16, NMEM], BF16)
    bkx, bvx = Buf(), Buf()
    self.dma('sync', kx, S['kxT'][0], writes=[bkx])
    self.dma('sync', vx, S['vxT'][0], writes=[bvx])
    vtm = A.take([128, 2, D], BF16)
    bvtm = Buf()
    trb = [PS[6][:, :].bitcast(BF16), PS[7][:, :].bitcast(BF16)]
    n = 0
    for c in range(16):
        for mj in range(2):
            bank = 6 + n % 2
            self.tr(trb[n % 2][:, 0:128], vx[:, c, mj * 128:(mj + 1) * 128], self.identb, [bvx, self.bc], [self.bps[bank]])
            self.cp(vtm[:, mj, c * 128:(c + 1) * 128], trb[n % 2][:, 0:128], [self.bps[bank]], [bvtm], acc=True)
            n += 1
    qb = [A.take([128, 16, TB], BF16) for _ in range(2)]
    bqb = [Buf(), Buf()]
    pt_ = [A.take([128, TB], BF16) for _ in range(4)]
    bpt = [Buf() for _ in range(4)]
    rz = [A.take([128, TB], F32) for _ in range(2)]
    brz = [Buf(), Buf()]
    ob = [A.take([128, TB], BF16) for _ in range(3)]
    bob = [Buf() for _ in range(3)]
    self.dma('sync', qb[0], S['qx'][0], writes=[bqb[0]])
    n = 0
    m = 0
    for tb in range(NTB):
        if tb + 1 < NTB:
            self.dma('sync', qb[(tb + 1) % 2], S['qx'][tb + 1], writes=[bqb[(tb + 1) % 2]])
        q = qb[tb % 2]
        for h in range(4):
            pp = []
            for mj in range(2):
                bank = 5 + (n % 3)
                j = n % 4
                n += 1
                for dc in range(4):
                    self.mm(PS[bank][:, 0:TB], kx[:, 4 * h + dc, mj * 128:(mj + 1) * 128], q[:, 4 * h + dc, :],
                            dc == 0, dc == 3, [bkx, bqb[tb % 2]], [self.bps[bank]])
                self.act(pt_[j], PS[bank][:, 0:TB], AF.Exp, [self.bps[bank]], [bpt[j]], scale=512 ** -0.5)
                pp.append((pt_[j], bpt[j]))
            for mj in range(2):
                self.mm(PS[4][:, 0:TB], self.onesb, pp[mj][0], mj == 0, mj == 1, [self.bc, pp[mj][1]], [self.bps[4]])
            for dvc in range(4):
                for mj in range(2):
                    self.mm(PS[dvc][:, 0:TB], vtm[:, mj, (4 * h + dvc) * 128:(4 * h + dvc + 1) * 128], pp[mj][0],
                            mj == 0, mj == 1, [bvtm, pp[mj][1]], [self.bps[dvc]])
            r = (tb * 4 + h) % 2
            self.P.emit('vector', (lambda o_, i_: (lambda e: e.reciprocal(o_, i_)))(rz[r], PS[4][:, 0:TB]),
                        reads=[self.bps[4]], writes=[brz[r]])
            for dvc in range(4):
                k3 = m % 3
                m += 1
                self.tt(ob[k3], PS[dvc][:, 0:TB], rz[r], ALU.mult, [self.bps[dvc], brz[r]], [bob[k3]])
                self.dma('sync', S['ox'][tb, :, 4 * h + dvc, :], ob[k3], reads=[bob[k3]])
    self.P.barrier()


def t5_onehots():
    oh = np.zeros((2, 128, 32 * 128 + 128), np.float32)
    j = np.arange(128)[:, None]
    i = np.arange(128)[None, :]
    for ty in range(2):
        rel = i - j + 128 * ty
        n = np.maximum(rel, 0)
        large = 16 + (np.log(np.maximum(n, 1).astype(np.float32) / np.float32(16)) / np.float32(np.log(128 / 16)) * 16).astype(np.int32)
        large = np.clip(large, 16, 31)
        bucket = np.where(n < 16, n, large)
        valid = rel >= 0
        for b in range(32):
            oh[ty, :, b * 128:(b + 1) * 128] = ((bucket == b) & valid).astype(np.float32)
        oh[ty, :, 4096:4224] = np.where(valid, 0.0, -1e30).astype(np.float32)
    return oh


def mlstm_sel():
    sel = np.zeros((8, 512), np.float32)
    for h in range(4):
        sel[h, h * 64:(h + 1) * 64] = 1.0
        sel[4 + h, 256 + h * 64:256 + (h + 1) * 64] = 1.0
    return sel


WNAMES = [('ffn_w_in', [DEPTH, 2, D, 2 * DFF]), ('ffn_w_out', [DEPTH, 2, DFF, D]), ('w_in', [DEPTH, D, NIN]),
          ('w_out', [DEPTH, D, D]), ('gla_gate_w', [DEPTH, 16, 256]), ('xattn_w_q', [DEPTH, D, D]),
          ('xattn_w_kv', [DEPTH, D, 2 * D]), ('xattn_w_o', [DEPTH, D, D])]


def build_program(T, depth=DEPTH, dbg=(), stop=None):
    kb = KB(T, dbg=dbg)
    cst = kb.dram_in("cst", [128, 832 + NSP])
    oh = kb.dram_in("oh", [2, 128, 32 * 128 + 128])
    sel = kb.dram_in("sel", [8, 512])
    x = kb.dram_in("x", [T, D])
    mem = kb.dram_in("mem", [NMEM, D])
    W = {nm: kb.dram_in(nm, shp) for nm, shp in WNAMES}
    out = kb.nc.dram_tensor("out", [T, D], F32, kind="ExternalOutput").ap()
    kb.setup_consts(cst, NSP)
    kb.hgrn_consts()
    kb.t5_bias(oh)
    kb.alloc_scratch()
    S = kb.S
    xres = [S['xres0'], S['xres1']]
    kb.transpose_in(x, T, kb.TB, xres[0], S['xT'])
    kb.transpose_in(mem, NMEM, NMEM, None, S['memT'])
    cur = 0

    def ln(l, i):
        nonlocal cur
        kb.layernorm(S['z'], sp_g(l, i), sp_b(l, i), xres[1 - cur], S['xT'])
        cur = 1 - cur
    for l in range(depth):
        o = l * LSTRIDE
        kb.ffn(S['xT'], xres[cur], W['ffn_w_in'][l, 0], W['ffn_w_out'][l, 0], S['hT'], S['z'])
        ln(l, 0)
        if stop == 'ffn1':
            break
        kb.in_proj(l, W['w_in'][l])
        if stop == 'inproj':
            break
        kb.gla_gate(l, W['gla_gate_w'][l])
        if stop == 'gate':
            break
        kb.mlstm_prep(l, sel)
        if stop == 'mlprep':
            break
        kb.gla_core(S['gla_q'], S['gla_k'], S['gla_g'], S['gla_v'], 64, 'rms', S['gla_r'], o + 130, 0)
        if stop == 'gla':
            break
        kb.gla_core(S['hg_q'], S['hg_k'], S['hg_g'], S['hg_v'], 128, 'rms', S['hg_o'], o + 131, 4)
        if stop == 'hg':
            break
        kb.diff_attn(l)
        if stop == 'diff':
            break
        kb.gla_core(S['ml_qc'], S['ml_kc'], S['ml_g'], S['ml_v'], 64, 'mlstm', S['ml_o'], None, 12)
        kb.gemm_resid(S['mixT'], 16, W['w_out'][l], xres[cur], S['z'])
        ln(l, 1)
        if stop == 'mix':
            break
        kb.xattn(l, W['xattn_w_q'][l], W['xattn_w_kv'][l], W['xattn_w_o'][l], 2)
        kb.gemm_resid(S['ox'], 16, W['xattn_w_o'][l], xres[cur], S['z'])
        ln(l, 2)
        if stop == 'xattn':
            break
        kb.ffn(S['xT'], xres[cur], W['ffn_w_in'][l, 1], W['ffn_w_out'][l, 1], S['hT'], S['z'])
        ln(l, 3)
    kb.transpose_out(xres[cur], out)
    kb.P.finalize()
    return kb


def make_in_maps(inp, T, ncores):
    f = lambda a: np.ascontiguousarray(np.asarray(a, np.float32))
    shared = {"cst": pack_consts(inp), "oh": t5_onehots(), "sel": mlstm_sel()}
    for nm, _ in WNAMES:
        shared[nm] = f(inp[nm])
    maps = []
    for c in range(ncores):
        b = c % inp['x'].shape[0]
        m = dict(shared)
        m["x"] = f(inp['x'][b, :T])
        m["mem"] = f(inp['mem'][b])
        maps.append(m)
    return maps


def kernel(**inputs):
    B, T = inputs['x'].shape[0], inputs['x'].shape[1]
    kb = build_program(T)
    maps = make_in_maps(inputs, T, 8)
    res = run_bass_kernel_spmd(kb.nc, maps, core_ids=list(range(8)))
    out = np.stack([np.asarray(res.results[b]["out"], np.float32) for b in range(B)], axis=0)
    return out
```

```python
import numpy as np
import concourse.bass as bass
import concourse.mybir as mybir
from concourse.bass_utils import run_bass_kernel_spmd

F32 = mybir.dt.float32
BF16 = mybir.dt.bfloat16
AF = mybir.ActivationFunctionType
ALU = mybir.AluOpType
AX = mybir.AxisListType

ENGS = ['sync', 'scalar', 'gpsimd', 'vector', 'tensor']
NDMA = 8


class Buf:
    __slots__ = ('name', 'lw', 'rd', 'excl')

    def __init__(self, name='', excl=False):
        self.name = name
        self.lw = None
        self.rd = {}
        self.excl = excl


class Prog:
    def __init__(self, nc):
        self.nc = nc
        self.q = {e: [] for e in ENGS}
        self.cnt = {e: 0 for e in ENGS}
        self.known = {e: {} for e in ENGS}
        self.dma_next = {e: 0 for e in ENGS}
        self.dma_val = {}
        self.ninst = 0

    def emit(self, eng, fn, reads=(), writes=(), dma=False, acc=False):
        deps = {}

        def add(t):
            if t is None:
                return
            k, v = t
            if deps.get(k, 0) < v:
                deps[k] = v
        for b in reads:
            if b.excl and b.lw is not None and b.lw[0] == eng:
                continue
            add(b.lw)
        for b in writes:
            if not (acc and b.lw is not None and b.lw[0] == eng):
                add(b.lw)
            for k, v in b.rd.items():
                add((k, v))
        if dma:
            slot = self.dma_next[eng]
            self.dma_next[eng] = (slot + 1) % NDMA
            key = ('dma', eng, slot)
            prev = self.dma_val.get(key, 0)
            if prev > 0:
                add((key, prev))
            tok = (key, prev + 16)
            self.dma_val[key] = prev + 16
        else:
            self.cnt[eng] += 1
            tok = (eng, self.cnt[eng])
        kn = self.known[eng]
        waits = []
        for k, v in deps.items():
            if kn.get(k, 0) >= v:
                continue
            kn[k] = v
            waits.append((k, v))
        self.q[eng].append((waits, fn, tok))
        self.ninst += 1
        for b in reads:
            if b.excl:
                b.lw = tok
                b.rd = {}
            elif b.rd.get(tok[0], 0) < tok[1]:
                b.rd[tok[0]] = tok[1]
        for b in writes:
            b.lw = tok
            b.rd = {}
        return tok

    def barrier(self):
        allk = {}
        for e in ENGS:
            if self.cnt[e] > 0:
                allk[e] = self.cnt[e]
        for k, v in self.dma_val.items():
            allk[k] = v
        for e in ENGS:
            kn = self.known[e]
            waits = []
            for k, v in allk.items():
                if kn.get(k, 0) < v:
                    kn[k] = v
                    waits.append((k, v))
            if waits:
                self.q[e].append((waits, None, None))

    def finalize(self):
        nc = self.nc
        self.barrier()
        sems = {}

        def sem(k):
            if k not in sems:
                nm = k if isinstance(k, str) else "d_%s_%d" % (k[1], k[2])
                sems[k] = nc.alloc_semaphore("s_" + nm)
            return sems[k]
        for e in ENGS:
            sem(e)
        for k in self.dma_val:
            sem(k)
        prog = self

        def replay(ename):
            def body(e):
                for waits, fn, tok in prog.q[ename]:
                    for k, v in waits:
                        e.wait_ge(sem(k), v)
                    if fn is None:
                        continue
                    ins = fn(e)
                    if tok[0] == ename:
                        ins.then_inc(sem(ename), 1)
                    else:
                        ins.then_inc(sem(tok[0]), 16)
            return body
        with nc.Block() as block:
            block.sync(replay('sync'))
            block.scalar(replay('scalar'))
            block.gpsimd(replay('gpsimd'))
            block.vector(replay('vector'))
            block.tensor(replay('tensor'))


class Arena:
    def __init__(self, nc, nwords, name="arena"):
        self.t = nc.alloc_sbuf_tensor(name, [128, nwords], F32)
        self.n = nwords
        self.off = 0

    def mark(self):
        return self.off

    def reset(self, m=0):
        self.off = m

    def take(self, shape, dtype, name=''):
        nel = 1
        for s in shape[1:]:
            nel *= s
        nw = nel if dtype == F32 else (nel + 1) // 2
        nw = (nw + 7) // 8 * 8
        assert self.off + nw <= self.n, "arena overflow %d+%d>%d (%s)" % (self.off, nw, self.n, name)
        v = self.t[0:shape[0], self.off:self.off + nw]
        self.off += nw
        if dtype != F32:
            v = v.bitcast(dtype)
        v = v[:, 0:nel]
        if len(shape) == 3:
            v = v.rearrange("p (a b) -> p a b", a=shape[1])
        elif len(shape) == 4:
            v = v.rearrange("p (a b c) -> p a b c", a=shape[1], b=shape[2])
        return v


D = 2048
DFF = 5632
NIN = 6680
NMEM = 256
DEPTH = 2
ALPHA = float(4.0 ** 0.25)
EPS = 1e-5
NWORDS = 52000


class KB:
    def __init__(self, T, dbg=()):
        self.T = T
        self.TB = min(512, T)
        self.NTB = T // self.TB
        self.nc = bass.Bass("TRN2", target_bir_lowering=False)
        self.P = Prog(self.nc)
        self.A = Arena(self.nc, NWORDS)
        self.ps = [self.nc.alloc_psum_tensor("ps%d" % i, [128, 512], F32) for i in range(8)]
        self.bps = [Buf('ps%d' % i, excl=True) for i in range(8)]
        self.dbg = set(dbg)
        self.rr = 0

    def dram(self, name, shape, dtype):
        kind = "ExternalOutput" if name in self.dbg else "Internal"
        return self.nc.dram_tensor(name, list(shape), dtype, kind=kind).ap()

    def dram_in(self, name, shape, dtype=F32):
        return self.nc.dram_tensor(name, list(shape), dtype, kind="ExternalInput").ap()

    def dma(self, q, out, in_, reads=(), writes=()):
        return self.P.emit(q, lambda e: e.dma_start(out=out, in_=in_), reads=reads, writes=writes, dma=True)

    def mm(self, out, lhsT, rhs, start, stop, reads, writes):
        return self.P.emit('tensor', lambda e: e.matmul(out, lhsT, rhs, start=start, stop=stop),
                           reads=reads, writes=writes, acc=True)

    def tr(self, out, in_, ident, reads, writes):
        return self.P.emit('tensor', lambda e: e.transpose(out, in_, ident), reads=reads, writes=writes, acc=True)

    def act(self, out, in_, func, reads, writes, bias=0.0, scale=1.0, acc=False):
        return self.P.emit('scalar', lambda e: e.activation(out, in_, func, bias=bias, scale=scale),
                           reads=reads, writes=writes, acc=acc)

    def tt(self, out, in0, in1, op, reads, writes, eng='vector', acc=False):
        return self.P.emit(eng, lambda e: e.tensor_tensor(out, in0, in1, op), reads=reads, writes=writes, acc=acc)

    def ts(self, out, in0, s1, s2, op0, op1, reads, writes, eng='vector', acc=False):
        if op1 is None:
            return self.P.emit(eng, lambda e: e.tensor_scalar(out, in0, s1, None, op0), reads=reads, writes=writes, acc=acc)
        return self.P.emit(eng, lambda e: e.tensor_scalar(out, in0, s1, s2, op0, op1), reads=reads, writes=writes, acc=acc)

    def stt(self, out, in0, scalar, in1, op0, op1, reads, writes, eng='vector', acc=False):
        return self.P.emit(eng, lambda e: e.scalar_tensor_tensor(out, in0, scalar, in1, op0, op1),
                           reads=reads, writes=writes, acc=acc)

    def cp(self, out, in_, reads, writes, eng='vector', acc=False):
        return self.P.emit(eng, lambda e: e.tensor_copy(out, in_), reads=reads, writes=writes, acc=acc)

    def setup_consts(self, cst, nsp):
        A = self.A
        self.cf = A.take([128, 128 + 128 + 64 + 512 + nsp], F32, 'consts')
        self.bc = Buf('consts')
        self.dma('sync', self.cf, cst, writes=[self.bc])
        self.ident = self.cf[:, 0:128]
        self.ones = self.cf[:, 128:256]
        self.maskT = self.cf[0:64, 256:320]
        self.reset = self.cf[:, 320:832]
        self.sp = self.cf[:, 832:832 + nsp]
        self.identb = A.take([128, 128], BF16, 'identb')
        self.onesb = A.take([128, 128], BF16, 'onesb')
        self.cp(self.identb, self.ident, [self.bc], [self.bc])
        self.cp(self.onesb, self.ones, [self.bc], [self.bc])
        self.P.barrier()
        self.base = A.mark()

    def transpose_in(self, x, T, TB, out_f32, out_bf):
        A = self.A
        A.reset(self.base)
        NTB = T // TB
        nsub = TB // 128
        xin = [A.take([128, D], F32) for _ in range(2)]
        bxin = [Buf(), Buf()]
        blkf = [A.take([128, 16, TB], F32) for _ in range(2)] if out_f32 is not None else None
        blkb = [A.take([128, 16, TB], BF16) for _ in range(2)]
        bblk = [Buf(), Buf()]
        n = 0
        for tb in range(NTB):
            for s in range(nsub):
                ttile = tb * nsub + s
                xi = xin[ttile % 2]
                bxi = bxin[ttile % 2]
                self.dma('sync', xi, x[ttile * 128:(ttile + 1) * 128, :], writes=[bxi])
                for g4 in range(4):
                    bank = n % 8
                    n += 1
                    for j in range(4):
                        kc = g4 * 4 + j
                        self.tr(self.ps[bank][:, j * 128:(j + 1) * 128], xi[:, kc * 128:(kc + 1) * 128], self.ident,
                                [bxi, self.bc], [self.bps[bank]])
                    src = self.ps[bank][:, :].rearrange("p (a b) -> p a b", a=4)
                    if blkf is not None:
                        self.P.emit('scalar', (lambda o, i: (lambda e: e.copy(o, i)))(
                            blkf[tb % 2][:, g4 * 4:(g4 + 1) * 4, s * 128:(s + 1) * 128], src),
                            reads=[self.bps[bank]], writes=[bblk[tb % 2]], acc=True)
                    self.cp(blkb[tb % 2][:, g4 * 4:(g4 + 1) * 4, s * 128:(s + 1) * 128], src,
                            [self.bps[bank]], [bblk[tb % 2]], acc=True)
            if blkf is not None:
                self.dma('sync', out_f32[tb], blkf[tb % 2], reads=[bblk[tb % 2]])
            self.dma('sync', out_bf[tb], blkb[tb % 2], reads=[bblk[tb % 2]])
        self.P.barrier()

    def transpose_out(self, xsrc, out):
        A = self.A
        A.reset(self.base)
        TB, NTB = self.TB, self.NTB
        nsub = TB // 128
        blk = [A.take([128, 16, TB], F32) for _ in range(2)]
        bblk = [Buf(), Buf()]
        ot = [A.take([128, D], F32) for _ in range(2)]
        bot = [Buf(), Buf()]
        n = 0
        for tb in range(NTB):
            self.dma('sync', blk[tb % 2], xsrc[tb], writes=[bblk[tb % 2]])
            for s in range(nsub):
                ttile = tb * nsub + s
                o = ot[ttile % 2]
                bo = bot[ttile % 2]
                for g4 in range(4):
                    bank = n % 8
                    n += 1
                    for j in range(4):
                        kc = g4 * 4 + j
                        self.tr(self.ps[bank][:, j * 128:(j + 1) * 128], blk[tb % 2][:, kc, s * 128:(s + 1) * 128],
                                self.ident, [bblk[tb % 2], self.bc], [self.bps[bank]])
                    if g4 % 2 == 0:
                        self.P.emit('scalar', (lambda o_, i_: (lambda e: e.copy(o_, i_)))(
                            o[:, g4 * 512:(g4 + 1) * 512], self.ps[bank][:, :]),
                            reads=[self.bps[bank]], writes=[bo], acc=True)
                    else:
                        self.cp(o[:, g4 * 512:(g4 + 1) * 512], self.ps[bank][:, :], [self.bps[bank]], [bo], acc=True)
                self.dma('sync', out[ttile * 128:(ttile + 1) * 128, :], o, reads=[bo])
        self.P.barrier()

    def gemm_fm(self, xsrc, KC, NTB, TB, w, groups, epi, MG, pre=None, nwst=3):
        A = self.A
        KS = 16
        wst = [A.take([128, KS, 128], F32) for _ in range(nwst)]
        bwst = [Buf() for _ in range(nwst)]
        wbuf = [A.take([128, KC, MG * 128], BF16) for _ in range(2)]
        bwb = [Buf(), Buf()]
        xb = [A.take([128, KC, TB], BF16) for _ in range(2)]
        bxb = [Buf(), Buf()]
        wv = w.rearrange("(kc p) n -> p kc n", p=128)
        NG = len(groups)
        st = {'wp': 0}

        def pieces(g):
            lst = []
            for s, (c0, n) in enumerate(groups[g]):
                for k0 in range(0, KC, KS):
                    lst.append((g, s, c0, n, k0, min(KC, k0 + KS)))
            return lst

        def load_piece(pc):
            g, s, c0, n, k0, k1 = pc
            j = st['wp'] % nwst
            st['wp'] += 1
            self.dma('sync', wst[j][:, 0:k1 - k0, 0:n], wv[:, k0:k1, c0:c0 + n], writes=[bwst[j]])
            self.cp(wbuf[g % 2][:, k0:k1, s * 128:s * 128 + n], wst[j][:, 0:k1 - k0, 0:n],
                    [bwst[j]], [bwb[g % 2]], eng='gpsimd', acc=True)

        items = [(g, tb) for g in range(NG) for tb in range(NTB)]
        for pc in pieces(0):
            load_piece(pc)
        self.dma('sync', xb[0], xsrc[0], writes=[bxb[0]])
        pend = []
        for i, (g, tb) in enumerate(items):
            if tb == 0:
                pend = pieces(g + 1) if g + 1 < NG else []
            per = (len(pend) + (NTB - tb) - 1) // (NTB - tb)
            for _ in range(per):
                load_piece(pend.pop(0))
            if i + 1 < len(items):
                self.dma('sync', xb[(i + 1) % 2], xsrc[items[i + 1][1]], writes=[bxb[(i + 1) % 2]])
            if pre is not None:
                pre(g, tb)
            tiles = []
            for s, (c0, n) in enumerate(groups[g]):
                bank = (i % 2) * 4 + s
                pst = self.ps[bank][0:n, 0:TB]
                for kc in range(KC):
                    self.mm(pst, wbuf[g % 2][:, kc, s * 128:s * 128 + n], xb[i % 2][:, kc, :], kc == 0, kc == KC - 1,
                            [bwb[g % 2], bxb[i % 2]], [self.bps[bank]])
                tiles.append((pst, self.bps[bank]))
            epi(g, tb, tiles)
        self.P.barrier()

    def layernorm(self, zsrc, gcol, bcol, out_f32, out_bf):
        A = self.A
        A.reset(self.base)
        TB, NTB = self.TB, self.NTB
        zb = [A.take([128, 16, TB], F32) for _ in range(2)]
        bzb = [Buf(), Buf()]
        ob = [A.take([128, 16, TB], BF16) for _ in range(2)]
        bob = [Buf(), Buf()]
        sq = [A.take([128, TB], F32) for _ in range(2)]
        bsq = [Buf(), Buf()]
        tmp = [A.take([128, TB], F32) for _ in range(2)]
        btmp = [Buf(), Buf()]
        mean = A.take([128, TB], F32)
        rstd = A.take([128, TB], F32)
        bst = Buf()
        self.dma('sync', zb[0], zsrc[0], writes=[bzb[0]])
        for tb in range(NTB):
            z = zb[tb % 2]
            bz = bzb[tb % 2]
            if tb + 1 < NTB:
                self.dma('sync', zb[(tb + 1) % 2], zsrc[tb + 1], writes=[bzb[(tb + 1) % 2]])
            b1, b2 = (tb % 2) * 2, (tb % 2) * 2 + 1
            p1 = self.ps[b1][:, 0:TB]
            p2 = self.ps[b2][:, 0:TB]
            for kc in range(16):
                self.mm(p1, self.ones, z[:, kc, :], kc == 0, kc == 15, [self.bc, bz], [self.bps[b1]])
                j = kc % 2
                self.act(sq[j], z[:, kc, :], AF.Square, [bz], [bsq[j]])
                self.mm(p2, self.ones, sq[j], kc == 0, kc == 15, [self.bc, bsq[j]], [self.bps[b2]])
            self.act(mean, p1, AF.Copy, [self.bps[b1]], [bst], scale=1.0 / D)
            self.tt(rstd, mean, mean, ALU.mult, [bst], [bst])
            self.stt(rstd, p2, 1.0 / D, rstd, ALU.mult, ALU.subtract, [self.bps[b2], bst], [bst])
            self.act(rstd, rstd, AF.Sqrt, [bst], [bst], bias=EPS)
            self.P.emit('vector', lambda e: e.reciprocal(rstd, rstd), reads=[bst], writes=[bst])
            for kc in range(16):
                j = kc % 2
                self.tt(tmp[j], z[:, kc, :], mean, ALU.subtract, [bz, bst], [btmp[j]])
                self.tt(tmp[j], tmp[j], rstd, ALU.mult, [btmp[j], bst], [btmp[j]], eng='gpsimd')
                self.act(z[:, kc, :], tmp[j], AF.Identity, [btmp[j], self.bc], [bz],
                         bias=self.sp[:, bcol + kc:bcol + kc + 1], scale=self.sp[:, gcol + kc:gcol + kc + 1])
                self.cp(ob[tb % 2][:, kc, :], z[:, kc, :], [bz], [bob[tb % 2]])
            self.dma('sync', out_f32[tb], z, reads=[bz])
            self.dma('sync', out_bf[tb], ob[tb % 2], reads=[bob[tb % 2]])
        self.P.barrier()

    def ffn(self, xT, xres_in, w_in, w_out, hT, z):
        A = self.A
        TB, NTB = self.TB, self.NTB
        A.reset(self.base)
        sg = [A.take([128, TB], F32) for _ in range(2)]
        bsg = [Buf(), Buf()]
        ht = [A.take([128, TB], BF16) for _ in range(3)]
        bht = [Buf() for _ in range(3)]
        st = {'r': 0}

        def epi(g, tb, tiles):
            for s in range(2):
                r = st['r']
                st['r'] += 1
                pg, bg = tiles[s]
                pu, bu = tiles[s + 2]
                self.act(sg[r % 2], pg, AF.Silu, [bg], [bsg[r % 2]])
                self.stt(ht[r % 3], pu, 0.5, sg[r % 2], ALU.mult, ALU.mult, [bu, bsg[r % 2]], [bht[r % 3]])
                self.dma('sync', hT[tb, :, 2 * g + s, :], ht[r % 3], reads=[bht[r % 3]])
        groups = [[(256 * g, 128), (256 * g + 128, 128), (DFF + 256 * g, 128), (DFF + 256 * g + 128, 128)]
                  for g in range(DFF // 256)]
        self.gemm_fm(xT, 16, NTB, TB, w_in, groups, epi, 4)
        self.gemm_resid(hT, 44, w_out, xres_in, z)

    def gemm_resid(self, src, KC, w, xres_in, z):
        A = self.A
        TB, NTB = self.TB, self.NTB
        A.reset(self.base)
        MG = 4 if KC <= 16 else 3
        groups = []
        c = 0
        while c < 16:
            n = min(MG, 16 - c)
            groups.append([(128 * (c + s), 128) for s in range(n)])
            c += n
        NXR = 2 * MG
        xr = [A.take([128, TB], F32) for _ in range(NXR)]
        bxr = [Buf() for _ in range(NXR)]
        zt = [A.take([128, TB], F32) for _ in range(3)]
        bzt = [Buf() for _ in range(3)]
        st = {'r': 0, 'q': 0}
        gstart = [sum(len(g) for g in groups[:i]) for i in range(len(groups))]

        def pre(g, tb):
            for s in range(len(groups[g])):
                q = st['q']
                st['q'] += 1
                self.dma('sync', xr[q % NXR], xres_in[tb, :, gstart[g] + s, :], writes=[bxr[q % NXR]])

        def epi(g, tb, tiles):
            for s in range(len(groups[g])):
                r = st['r']
                st['r'] += 1
                pz, bz = tiles[s]
                self.stt(zt[r % 3], xr[r % NXR], ALPHA, pz, ALU.mult, ALU.add, [bxr[r % NXR], bz], [bzt[r % 3]])
                self.dma('sync', z[tb, :, gstart[g] + s, :], zt[r % 3], reads=[bzt[r % 3]])
        self.gemm_fm(src, KC, NTB, TB, w, groups, epi, MG, pre=pre, nwst=(3 if KC <= 16 else 2))


LSTRIDE = 416
SP_HLB = 2 * LSTRIDE
SP_T5 = SP_HLB + 8
NSP = SP_T5 + 128


def sp_g(l, i):
    return l * LSTRIDE + i * 32


def sp_b(l, i):
    return l * LSTRIDE + i * 32 + 16


def pack_consts(inp):
    c = np.zeros((128, 832 + NSP), np.float32)
    c[:, 0:128] = np.eye(128, dtype=np.float32)
    c[:, 128:256] = 1.0
    jj, ii = np.meshgrid(np.arange(64), np.arange(64), indexing='ij')
    c[0:64, 256:320] = (jj <= ii).astype(np.float32)
    r = np.ones(512, np.float32)
    r[::64] = 0.0
    c[:, 320:832] = r[None, :]
    sp = c[:, 832:]

    def fm(v):
        return np.asarray(v, np.float32).reshape(-1, 128).T
    for l in range(DEPTH):
        o = l * LSTRIDE
        for i in range(4):
            sp[:, o + i * 32:o + i * 32 + 16] = fm(inp['ln_g'][l, i])
            sp[:, o + i * 32 + 16:o + i * 32 + 32] = fm(inp['ln_b'][l, i])
        sp[:, o + 128:o + 130] = fm(inp['gla_gate_b'][l])
        sp[:, o + 130] = inp['gla_norm_g'][l]
        sp[:, o + 131] = inp['hgrn_norm_g'][l]
        sp[:, o + 132] = inp['diff_norm_g'][l]
        for tap in range(4):
            sp[:, o + 133 + tap * 4:o + 133 + tap * 4 + 4] = fm(inp['mlstm_conv_w'][l, tap])
        sp[0:8, o + 149] = inp['mlstm_gate_b'][l]
        sp[:, o + 150:o + 406] = np.asarray(inp['diff_lambda'][l], np.float32).reshape(1, 256)
        sp[:, SP_HLB + 4 * l:SP_HLB + 4 * l + 4] = fm(inp['hgrn_lb'][l])
    sp[:, SP_T5:SP_T5 + 128] = np.asarray(inp['t5_table'], np.float32).reshape(1, 128)
    return c


SEGS = [('a_q', 0, 256), ('a_k', 256, 256), ('a_v', 512, 512), ('a_lr', 1024, 16), ('a_r', 1040, 512),
        ('b_q', 1552, 512), ('b_f', 2064, 512), ('b_i', 2576, 512), ('b_g', 3088, 512),
        ('c_q', 3600, 512), ('c_k', 4112, 512), ('c_v', 4624, 512),
        ('d_q', 5136, 256), ('d_k', 5392, 256), ('d_v', 5648, 512), ('d_if', 6160, 8), ('d_o', 6168, 512)]


def _kb_method(f):
    setattr(KB, f.__name__, f)
    return f


@_kb_method
def alloc_scratch(self):
    NTB, TB = self.NTB, self.TB
    S = {}

    def fmt(name, nch, dt):
        S[name] = self.dram(name, [NTB, 128, nch, TB], dt)
    for i in range(2):
        fmt('xres%d' % i, 16, F32)
    fmt('xT', 16, BF16)
    fmt('z', 16, F32)
    fmt('hT', 44, BF16)
    fmt('qx', 16, BF16)
    fmt('ox', 16, BF16)
    fmt('mixT', 16, BF16)
    for nm, n, dt in [('gla_q', 2, BF16), ('gla_k', 2, BF16), ('gla_v', 4, BF16), ('gla_r', 4, F32), ('gla_g', 2, F32),
                      ('hg_q', 4, BF16), ('hg_k', 4, BF16), ('hg_g', 4, F32), ('hg_v', 4, BF16), ('hg_o', 4, F32),
                      ('df_q', 4, BF16), ('df_k', 4, BF16), ('df_v', 4, BF16),
                      ('ml_q', 2, F32), ('ml_k', 2, F32), ('ml_v', 4, BF16), ('ml_o', 4, F32),
                      ('ml_qc', 2, BF16), ('ml_kc', 2, BF16), ('ml_g', 2, F32)]:
        fmt(nm, n, dt)
    S['gla_lr'] = self.dram('gla_lr', [NTB, 16, TB], F32)
    S['ml_if'] = self.dram('ml_if', [NTB, 8, TB], F32)
    S['memT'] = self.dram('memT', [1, 128, 16, NMEM], BF16)
    S['kxT'] = self.dram('kxT', [1, 128, 16, NMEM], BF16)
    S['vxT'] = self.dram('vxT', [1, 128, 16, NMEM], BF16)
    self.S = S


@_kb_method
def hgrn_consts(self):
    A = self.A
    self.hl = A.take([128, 16], F32, 'hl')
    self.bhl = Buf()
    hl = self.hl
    h0 = self.sp[:, SP_HLB:SP_HLB + 4]
    h1 = self.sp[:, SP_HLB + 4:SP_HLB + 8]
    self.P.emit('vector', lambda e: e.memset(hl[:, 0:4], 1e-12), writes=[self.bhl])
    self.P.emit('vector', lambda e: e.memset(hl[:, 4:8], 1.0), writes=[self.bhl])
    self.tt(hl[:, 8:12], h1, h0, ALU.subtract, [self.bc], [self.bhl])
    self.act(hl[:, 8:12], hl[:, 8:12], AF.Sigmoid, [self.bhl], [self.bhl])
    self.ts(hl[:, 8:12], hl[:, 8:12], 1.0 - 1e-6, 0.0, ALU.min, ALU.max, [self.bhl], [self.bhl])
    self.ts(hl[:, 12:16], hl[:, 8:12], -1.0, 1.0, ALU.mult, ALU.add, [self.bhl], [self.bhl])
    self.ts(hl[:, 8:12], hl[:, 8:12], 1e-12, None, ALU.max, None, [self.bhl], [self.bhl])
    self.P.barrier()
    self.base = A.mark()


@_kb_method
def log_sigmoid(self, out, x, t1, reads, bufs, post_scale=1.0):
    bx, bo, bt = bufs
    self.act(t1, x, AF.Abs, reads + [bx], [bt])
    self.act(t1, t1, AF.Exp, [bt], [bt], scale=-1.0)
    self.act(t1, t1, AF.Ln, [bt], [bt], bias=1.0)
    self.ts(out, x, 0.0, None, ALU.min, None, [bx], [bo])
    self.tt(out, out, t1, ALU.subtract, [bo, bt], [bo])
    if post_scale != 1.0:
        self.ts(out, out, post_scale, None, ALU.mult, None, [bo], [bo])


@_kb_method
def in_proj(self, l, w_in):
    A = self.A
    A.reset(self.base)
    TB, NTB, S = self.TB, self.NTB, self.S
    chunks = []
    for nm, c0, wd in SEGS:
        for i in range(0, wd, 128):
            chunks.append((nm, i // 128, c0 + i, min(128, wd - i)))
    groups = [chunks[i:i + 4] for i in range(0, len(chunks), 4)]
    ef = [A.take([128, TB], F32) for _ in range(4)]
    bef = [Buf() for _ in range(4)]
    eb = [A.take([128, TB], BF16) for _ in range(4)]
    beb = [Buf() for _ in range(4)]
    st = {'f': 0, 'b': 0}
    o = l * LSTRIDE

    def nf():
        st['f'] += 1
        return ef[st['f'] % 4], bef[st['f'] % 4]

    def nb():
        st['b'] += 1
        return eb[st['b'] % 4], beb[st['b'] % 4]
    plain = {'a_q': ('gla_q', 64 ** -0.5), 'a_k': ('gla_k', 1.0), 'a_v': ('gla_v', 1.0), 'b_i': ('hg_v', 1.0),
             'c_q': ('df_q', 64 ** -0.5), 'c_k': ('df_k', 1.0), 'c_v': ('df_v', 1.0), 'd_v': ('ml_v', 1.0)}
    actf = {'a_r': ('gla_r', AF.Silu), 'b_g': ('hg_o', AF.Silu), 'd_o': ('ml_o', AF.Sigmoid),
            'd_q': ('ml_q', AF.Copy), 'd_k': ('ml_k', AF.Copy)}

    def epi(g, tb, tiles):
        for (nm, idx, c0, n), (pst, bp) in zip(groups[g], tiles):
            if nm in plain:
                dst, sc = plain[nm]
                t, bt = nb()
                self.act(t, pst, AF.Copy, [bp], [bt], scale=sc)
                self.dma('sync', S[dst][tb, :, idx, :], t, reads=[bt])
            elif nm in actf:
                dst, fn = actf[nm]
                t, bt = nf()
                self.act(t, pst, fn, [bp], [bt])
                self.dma('sync', S[dst][tb, :, idx, :], t, reads=[bt])
            elif nm == 'a_lr':
                t, bt = nf()
                self.act(t[0:16, :], pst, AF.Copy, [bp], [bt])
                self.dma('sync', S['gla_lr'][tb], t[0:16, :], reads=[bt])
            elif nm == 'd_if':
                t, bt = nf()
                self.act(t[0:8, :], pst, AF.Identity, [bp, self.bc], [bt], bias=self.sp[0:8, o + 149:o + 150])
                self.dma('sync', S['ml_if'][tb], t[0:8, :], reads=[bt])
            elif nm == 'b_q':
                t, bt = nf()
                self.act(t, pst, AF.Silu, [bp], [bt])
                t2, bt2 = nb()
                self.ts(t2, t, 128 ** -0.5, None, ALU.mult, None, [bt], [bt2])
                self.dma('sync', S['hg_q'][tb, :, idx, :], t2, reads=[bt2])
            elif nm == 'b_f':
                sg, bsg = nf()
                self.act(sg, pst, AF.Sigmoid, [bp], [bsg])
                f, bf_ = nf()
                lbc = self.hl[:, l * 8 + idx:l * 8 + idx + 1]
                oml = self.hl[:, l * 8 + 4 + idx:l * 8 + 4 + idx + 1]
                self.ts(f, sg, oml, lbc, ALU.mult, ALU.add, [bsg, self.bhl], [bf_])
                self.act(f, f, AF.Ln, [bf_], [bf_])
                self.dma('sync', S['hg_g'][tb, :, idx, :], f, reads=[bf_])
                self.ts(sg, sg, -1.0, 1.0, ALU.mult, ALU.add, [bsg], [bsg])
                t2, bt2 = nb()
                self.ts(t2, sg, oml, None, ALU.mult, None, [bsg, self.bhl], [bt2])
                self.dma('sync', S['hg_k'][tb, :, idx, :], t2, reads=[bt2])
            else:
                raise ValueError(nm)
    self.gemm_fm(S['xT'], 16, NTB, TB, w_in, [[(c[2], c[3]) for c in g] for g in groups], epi, 4)


@_kb_method
def gla_gate(self, l, gate_w):
    A = self.A
    A.reset(self.base)
    TB, NTB, S = self.TB, self.NTB, self.S
    o = l * LSTRIDE
    gw = A.take([16, 256], F32)
    bgw = Buf()
    self.dma('sync', gw, gate_w, writes=[bgw])
    lr = [A.take([16, TB], F32) for _ in range(2)]
    blr = [Buf(), Buf()]
    xs = [A.take([128, TB], F32) for _ in range(2)]
    bxs = [Buf(), Buf()]
    t1 = [A.take([128, TB], F32) for _ in range(2)]
    bt1 = [Buf(), Buf()]
    n = 0
    for tb in range(NTB):
        self.dma('sync', lr[tb % 2], S['gla_lr'][tb], writes=[blr[tb % 2]])
        for t in range(2):
            bank = n % 8
            j = n % 2
            n += 1
            pst = self.ps[bank][:, 0:TB]
            self.mm(pst, gw[:, t * 128:(t + 1) * 128], lr[tb % 2], True, True, [bgw, blr[tb % 2]], [self.bps[bank]])
            self.act(xs[j], pst, AF.Identity, [self.bps[bank], self.bc], [bxs[j]], bias=self.sp[:, o + 128 + t:o + 129 + t])
            self.log_sigmoid(xs[j], xs[j], t1[j], [], (bxs[j], bxs[j], bt1[j]), post_scale=1.0 / 16.0)
            self.dma('sync', S['gla_g'][tb, :, t, :], xs[j], reads=[bxs[j]])
    self.P.barrier()


@_kb_method
def mlstm_prep(self, l, sel):
    A = self.A
    A.reset(self.base)
    TB, NTB, S = self.TB, self.NTB, self.S
    o = l * LSTRIDE
    sl = A.take([8, 512], F32)
    bsl = Buf()
    self.dma('sync', sl, sel, writes=[bsl])
    gi = [A.take([8, TB], F32) for _ in range(2)]
    bgi = [Buf(), Buf()]
    ei = [A.take([128, TB], F32) for _ in range(2)]
    bei = [Buf(), Buf()]
    gt = [A.take([128, TB], F32) for _ in range(2)]
    bgt = [Buf(), Buf()]
    t1 = [A.take([128, TB], F32) for _ in range(2)]
    bt1 = [Buf(), Buf()]
    xr = [A.take([128, TB + 3], F32) for _ in range(2)]
    bxr = [Buf(), Buf()]
    cv = [A.take([128, TB], F32) for _ in range(2)]
    bcv = [Buf(), Buf()]
    ob = [A.take([128, TB], BF16) for _ in range(2)]
    bob = [Buf(), Buf()]
    n = 0
    m = 0
    for tb in range(NTB):
        self.dma('sync', gi[tb % 2], S['ml_if'][tb], writes=[bgi[tb % 2]])
        for t in range(2):
            j = n % 2
            b1, b2 = (n % 4) * 2, (n % 4) * 2 + 1
            n += 1
            p1 = self.ps[b1][:, 0:TB]
            p2 = self.ps[b2][:, 0:TB]
            self.mm(p1, sl[:, t * 128:(t + 1) * 128], gi[tb % 2], True, True, [bsl, bgi[tb % 2]], [self.bps[b1]])
            self.mm(p2, sl[:, 256 + t * 128:256 + (t + 1) * 128], gi[tb % 2], True, True, [bsl, bgi[tb % 2]], [self.bps[b2]])
            self.act(ei[j], p1, AF.Exp, [self.bps[b1]], [bei[j]])
            self.act(gt[j], p2, AF.Copy, [self.bps[b2]], [bgt[j]])
            self.log_sigmoid(gt[j], gt[j], t1[j], [], (bgt[j], bgt[j], bt1[j]))
            self.dma('sync', S['ml_g'][tb, :, t, :], gt[j], reads=[bgt[j]])
            for which in range(2):
                src = S['ml_q'] if which == 0 else S['ml_k']
                dst = S['ml_qc'] if which == 0 else S['ml_kc']
                jj = m % 2
                m += 1
                x_ = xr[jj]
                if tb == 0:
                    self.P.emit('vector', (lambda a: (lambda e: e.memset(a, 0.0)))(x_[:, 0:3]), writes=[bxr[jj]])
                else:
                    self.dma('sync', x_[:, 0:3], src[tb - 1, :, t, TB - 3:TB], writes=[bxr[jj]])
                self.dma('sync', x_[:, 3:3 + TB], src[tb, :, t, :], writes=[bxr[jj]])
                c_ = cv[jj]
                tile_ = which * 2 + t
                for tap in range(4):
                    cw = self.sp[:, o + 133 + tap * 4 + tile_:o + 134 + tap * 4 + tile_]
                    if tap == 0:
                        self.ts(c_, x_[:, 0:TB], cw, None, ALU.mult, None, [bxr[jj], self.bc], [bcv[jj]])
                    else:
                        self.stt(c_, x_[:, tap:tap + TB], cw, c_, ALU.mult, ALU.add, [bxr[jj], self.bc, bcv[jj]], [bcv[jj]])
                if which == 0:
                    self.act(ob[jj], c_, AF.Silu, [bcv[jj]], [bob[jj]])
                else:
                    self.act(c_, c_, AF.Silu, [bcv[jj]], [bcv[jj]])
                    self.stt(ob[jj], c_, 64 ** -0.5, ei[j], ALU.mult, ALU.mult, [bcv[jj], bei[j]], [bob[jj]])
                self.dma('sync', dst[tb, :, t, :], ob[jj], reads=[bob[jj]])
    self.P.barrier()


@_kb_method
def gla_core(self, qsrc, ksrc, gsrc, vsrc, dk, mode, gsrc_gate, ngcol, dst_c0):
    A = self.A
    A.reset(self.base)
    TB, NTB, S = self.TB, self.NTB, self.S
    NCH = TB // 64
    NH = 4
    ml = (mode == 'mlstm')
    PS = self.ps
    ABANK = (4, 7)
    trb = PS[5][:, :].bitcast(BF16)
    Sst = [A.take([dk, 128], F32) for _ in range(NH)]
    bS = [Buf() for _ in range(NH)]
    Snt = [A.take([dk, 128], F32) for _ in range(NH)] if ml else None
    bSn = [Buf() for _ in range(NH)]
    for h in range(NH):
        self.P.emit('vector', (lambda a: (lambda e: e.memset(a, 0.0)))(Sst[h]), writes=[bS[h]])
        if ml:
            self.P.emit('vector', (lambda a: (lambda e: e.memset(a, 0.0)))(Snt[h]), writes=[bSn[h]])

    def ring(n, shape, dt):
        return [A.take(shape, dt) for _ in range(n)], [Buf() for _ in range(n)]
    qb, bqb = ring(4, [dk, TB], BF16)
    kb_, bkb = ring(4, [dk, TB], BF16)
    gb, bgb = ring(4, [dk, TB], F32)
    vb, bvb = ring(4, [128, TB], BF16)
    cum, bcum = ring(4, [dk, TB], F32)
    eq, beq = ring(4, [dk, TB], F32)
    ek, bek = ring(4, [dk, TB], F32)
    qt, bqt = ring(4, [dk, TB], BF16)
    kt, bkt = ring(4, [dk, TB], BF16)
    esm, besm = ring(4, [dk, 2 * NCH], F32)
    vk, bvk = ring(4, [64, 128 + dk], BF16)
    Sp, bSp = ring(4, [dk, 128], BF16)
    Snp, bSnp = ring(4, [dk, 128], BF16)
    at, bat = ring(4, [64, 64], BF16)
    osb, bosb = ring(2, [128, TB], F32)
    dsb, bdsb = ring(2, [128, TB], F32)
    gate, bgate = ring(4, [128, TB], F32)
    w1, bw1 = ring(2, [128, TB], F32)
    yb, byb = ring(2, [128, TB], BF16)
    cnt = {'c': 0}

    def load(tb, hp, jb):
        for hh in range(2):
            h = hp * 2 + hh
            j = jb * 2 + hh
            tile_, r0 = (h * dk) // 128, (h * dk) % 128
            self.dma('sync', qb[j], qsrc[tb, r0:r0 + dk, tile_, :], writes=[bqb[j]])
            self.dma('sync', kb_[j], ksrc[tb, r0:r0 + dk, tile_, :], writes=[bkb[j]])
            self.dma('sync', gb[j], gsrc[tb, r0:r0 + dk, tile_, :], writes=[bgb[j]])
            self.dma('sync', vb[j], vsrc[tb, :, h, :], writes=[bvb[j]])
            self.dma('sync', gate[j], gsrc_gate[tb, :, h, :], writes=[bgate[j]])
    items = [(tb, hp) for tb in range(NTB) for hp in range(2)]
    load(items[0][0], items[0][1], 0)
    for it, (tb, hp) in enumerate(items):
        jb = it % 2
        if it + 1 < len(items):
            load(items[it + 1][0], items[it + 1][1], (it + 1) % 2)
        for hh in range(2):
            j = jb * 2 + hh
            self.P.emit('vector', (lambda o_, m_, g_: (lambda e: e.tensor_tensor_scan(o_, m_, g_, 0.0, ALU.mult, ALU.add)))(
                cum[j], self.reset[0:dk, 0:TB], gb[j]), reads=[self.bc, bgb[j]], writes=[bcum[j]])
            c3 = cum[j].rearrange("p (c t) -> p c t", t=64)
            e3 = eq[j].rearrange("p (c t) -> p c t", t=64)
            self.tt(e3, c3, c3[:, :, 31:32].broadcast_to([dk, NCH, 64]), ALU.subtract, [bcum[j]], [beq[j]])
            self.act(ek[j], eq[j], AF.Exp, [beq[j]], [bek[j]], scale=-1.0)
            self.act(eq[j], eq[j], AF.Exp, [beq[j]], [beq[j]])
            self.act(esm[j][:, 0:NCH], c3[:, :, 31], AF.Exp, [bcum[j]], [besm[j]])
            self.act(esm[j][:, NCH:2 * NCH], c3[:, :, 63], AF.Exp, [bcum[j]], [besm[j]], acc=True)
            self.tt(qt[j], qb[j], eq[j], ALU.mult, [bqb[j], beq[j]], [bqt[j]])
            self.tt(kt[j], kb_[j], ek[j], ALU.mult, [bkb[j], bek[j]], [bkt[j]], eng='gpsimd')
        TRB = (5, 5) if ml else (5, 3)
        KVB = (6, 6) if ml else (6, 2)
        for c in range(NCH):
            cs = slice(c * 64, (c + 1) * 64)
            ctx = []
            for hh in range(2):
                h = hp * 2 + hh
                j = jb * 2 + hh
                n = cnt['c']
                cnt['c'] += 1
                tb_ = TRB[hh]
                trh = PS[tb_][:, :].bitcast(BF16)[:, (hh * 512 if ml else 0):]
                self.tr(trh[0:64, 0:128], vb[j][:, cs], self.identb, [bvb[j], self.bc], [self.bps[tb_]])
                self.tr(trh[0:64, 128:128 + dk], kt[j][:, cs], self.identb[0:dk, 0:dk], [bkt[j], self.bc], [self.bps[tb_]])
                ab = ABANK[hh]
                aps = PS[ab][0:64, 0:64]
                self.mm(aps, kt[j][:, cs], qt[j][:, cs], True, True, [bkt[j], bqt[j]], [self.bps[ab]])
                sj = n % 4
                self.act(Sp[sj], Sst[h], AF.Copy, [bS[h], besm[j]], [bSp[sj]], scale=esm[j][:, c:c + 1])
                if ml:
                    self.act(Snp[sj], Snt[h], AF.Copy, [bSn[h], besm[j]], [bSnp[sj]], scale=esm[j][:, c:c + 1])
                ctx.append((h, j, n, tb_, trh, ab, aps, sj))
            for hh in range(2):
                h, j, n, tb_, trh, ab, aps, sj = ctx[hh]
                self.cp(vk[n % 4], trh[0:64, 0:128 + dk], [self.bps[tb_]], [bvk[n % 4]])
                self.tt(at[n % 4], aps, self.maskT, ALU.mult, [self.bps[ab], self.bc], [bat[n % 4]])
            for hh in range(2):
                h, j, n, tb_, trh, ab, aps, sj = ctx[hh]
                v_ = vk[n % 4]
                bv_ = bvk[n % 4]
                ktm = v_[:, 128:128 + dk]
                self.mm(PS[hh][:, cs], v_[:, 0:128], at[n % 4], True, False, [bv_, bat[n % 4]], [self.bps[hh]])
                self.mm(PS[hh][:, cs], Sp[sj], qt[j][:, cs], False, True, [bSp[sj], bqt[j]], [self.bps[hh]])
                if ml:
                    self.mm(PS[2 + hh][:, cs], self.onesb[0:64, :], at[n % 4], True, False,
                            [self.bc, bat[n % 4]], [self.bps[2 + hh]])
                    self.mm(PS[2 + hh][:, cs], Snp[sj], qt[j][:, cs], False, True, [bSnp[sj], bqt[j]], [self.bps[2 + hh]])
                kb2 = KVB[hh]
                ko = hh * 128 if ml else 0
                self.mm(PS[kb2][0:dk, ko:ko + 128], ktm, v_[:, 0:128], True, True, [bv_], [self.bps[kb2]])
                if ml:
                    self.mm(PS[kb2][0:dk, 256 + ko:384 + ko], ktm, self.onesb[0:64, :], True, True, [bv_, self.bc], [self.bps[kb2]])
            for hh in range(2):
                h, j, n, tb_, trh, ab, aps, sj = ctx[hh]
                kb2 = KVB[hh]
                ko = hh * 128 if ml else 0
                kvs = PS[kb2][0:dk, ko:ko + 128]
                self.ts(Sst[h], Sst[h], esm[j][:, NCH + c:NCH + c + 1], None, ALU.mult, None, [bS[h], besm[j]], [bS[h]])
                self.stt(Sst[h], kvs, eq[j][:, c * 64 + 63:c * 64 + 64], Sst[h], ALU.mult, ALU.add,
                         [self.bps[kb2], beq[j], bS[h]], [bS[h]])
                if ml:
                    kvn = PS[kb2][0:dk, 256 + ko:384 + ko]
                    self.ts(Snt[h], Snt[h], esm[j][:, NCH + c:NCH + c + 1], None, ALU.mult, None,
                            [bSn[h], besm[j]], [bSn[h]])
                    self.stt(Snt[h], kvn, eq[j][:, c * 64 + 63:c * 64 + 64], Snt[h], ALU.mult, ALU.add,
                             [self.bps[kb2], beq[j], bSn[h]], [bSn[h]])
        for hh in range(2):
            h = hp * 2 + hh
            j = jb * 2 + hh
            r = hh
            self.act(osb[r], PS[hh][:, 0:TB], AF.Copy, [self.bps[hh]], [bosb[r]])
            if ml:
                self.act(dsb[r], PS[2 + hh][:, 0:TB], AF.Abs, [self.bps[2 + hh]], [bdsb[r]])
                self.ts(dsb[r], dsb[r], 1.0, None, ALU.max, None, [bdsb[r]], [bdsb[r]])
                self.P.emit('vector', (lambda a_: (lambda e: e.reciprocal(a_, a_)))(dsb[r]), reads=[bdsb[r]], writes=[bdsb[r]])
                self.tt(w1[r], osb[r], dsb[r], ALU.mult, [bosb[r], bdsb[r]], [bw1[r]])
                self.tt(yb[r], w1[r], gate[j], ALU.mult, [bw1[r], bgate[j]], [byb[r]])
            else:
                self.act(w1[r], osb[r], AF.Square, [bosb[r]], [bw1[r]])
                self.mm(PS[2 + hh][:, 0:TB], self.ones, w1[r], True, True, [self.bc, bw1[r]], [self.bps[2 + hh]])
                self.act(w1[r], PS[2 + hh][:, 0:TB], AF.Sqrt, [self.bps[2 + hh]], [bw1[r]], bias=EPS, scale=1.0 / 128.0)
                self.P.emit('vector', (lambda a_: (lambda e: e.reciprocal(a_, a_)))(w1[r]), reads=[bw1[r]], writes=[bw1[r]])
                self.tt(w1[r], w1[r], osb[r], ALU.mult, [bw1[r], bosb[r]], [bw1[r]])
                self.stt(yb[r], w1[r], self.sp[:, ngcol:ngcol + 1], gate[j], ALU.mult, ALU.mult,
                         [bw1[r], self.bc, bgate[j]], [byb[r]])
            self.dma('sync', S['mixT'][tb, :, dst_c0 + h, :], yb[r], reads=[byb[r]])
    self.P.barrier()


@_kb_method
def t5_bias(self, oh):
    A = self.A
    self.bt5 = A.take([128, 4, 2, 128], F32, 't5b')
    self.b31 = A.take([128, 4], F32, 'b31')
    self.bbt = Buf()
    m = A.mark()
    ohs = A.take([128, 32 * 128 + 128], F32)
    boh = Buf()
    prod = A.take([128, 32, 128], F32)
    bpr = Buf()
    t5 = self.sp[:, SP_T5:SP_T5 + 128].rearrange("p (b h) -> p b h", h=4)
    self.cp(self.b31, t5[:, 31, :], [self.bc], [self.bbt])
    for ty in range(2):
        self.dma('sync', ohs, oh[ty], writes=[boh])
        o3 = ohs[:, 0:4096].rearrange("p (b i) -> p b i", b=32)
        for h in range(4):
            self.tt(prod, o3, t5[:, :, h:h + 1].broadcast_to([128, 32, 128]), ALU.mult, [boh, self.bc], [bpr])
            self.P.emit('vector', (lambda o_, i_: (lambda e: e.tensor_reduce(o_, i_, AX.X, ALU.add)))(
                self.bt5[:, h, ty, :], prod.rearrange("p b i -> p i b")), reads=[bpr], writes=[self.bbt])
            self.tt(self.bt5[:, h, ty, :], self.bt5[:, h, ty, :], ohs[:, 4096:4224], ALU.add, [self.bbt, boh], [self.bbt])
    self.P.barrier()
    A.reset(m)
    self.base = A.mark()


@_kb_method
def diff_attn(self, l):
    A = self.A
    A.reset(self.base)
    TB, NTB, S, T = self.TB, self.NTB, self.S, self.T
    o = l * LSTRIDE
    NKT = T // 128
    nsub = TB // 128
    lam_init = 0.8 - 0.6 * float(np.exp(-0.3 * l))
    PS = self.ps
    lm = A.take([128, 8], F32)
    blm = Buf()
    dl = self.sp[:, o + 150:o + 406]
    pr = A.take([128, 128], F32)
    self.tt(pr[:, 0:64], dl[:, 0:64], dl[:, 64:128], ALU.mult, [self.bc], [blm])
    self.tt(pr[:, 64:128], dl[:, 128:192], dl[:, 192:256], ALU.mult, [self.bc], [blm])
    self.P.emit('vector', lambda e: e.tensor_reduce(lm[:, 0:2], pr.rearrange("p (a b) -> p a b", a=2), AX.X, ALU.add),
                reads=[blm], writes=[blm])
    self.act(lm[:, 0:2], lm[:, 0:2], AF.Exp, [blm], [blm])
    self.tt(lm[:, 2:3], lm[:, 1:2], lm[:, 0:1], ALU.subtract, [blm], [blm])
    self.ts(lm[:, 3:4], lm[:, 2:3], -lam_init, None, ALU.add, None, [blm], [blm])
    self.ts(lm[:, 4:5], self.sp[:, o + 132:o + 133], 1.0 - lam_init, None, ALU.mult, None, [self.bc], [blm])
    neglam = lm[:, 3:4]
    ngs = lm[:, 4:5]
    qa = A.take([128, T], BF16)
    ka = A.take([128, T], BF16)
    va = A.take([128, T], BF16)
    bq, bk, bv = Buf(), Buf(), Buf()
    vt = A.take([128, NKT, 128], BF16)
    bvt = Buf()
    trb = [PS[4][:, :].bitcast(BF16), PS[5][:, :].bitcast(BF16)]

    def ring(n, shape, dt):
        return [A.take(shape, dt) for _ in range(n)], [Buf() for _ in range(n)]
    p1, bp1 = ring(2, [128, TB], BF16)
    p2, bp2 = ring(2, [128, TB], BF16)
    tmp, btmp = ring(2, [128, 128], F32)
    w1, bw1 = ring(2, [128, TB], F32)
    w2, bw2 = ring(2, [128, TB], F32)
    yb, byb = ring(2, [128, TB], BF16)
    nn = 0
    tn = [0]
    for h in range(4):
        for tb in range(NTB):
            self.dma('sync', qa[:, tb * TB:(tb + 1) * TB], S['df_q'][tb, :, h, :], writes=[bq])
            self.dma('sync', ka[:, tb * TB:(tb + 1) * TB], S['df_k'][tb, :, h, :], writes=[bk])
            self.dma('sync', va[:, tb * TB:(tb + 1) * TB], S['df_v'][tb, :, h, :], writes=[bv])
        for kt_ in range(NKT):
            bank = 4 + kt_ % 2
            self.tr(trb[kt_ % 2][:, 0:128], va[:, kt_ * 128:(kt_ + 1) * 128], self.identb, [bv, self.bc], [self.bps[bank]])
            self.cp(vt[:, kt_, :], trb[kt_ % 2][:, 0:128], [self.bps[bank]], [bvt], acc=True)
        b31 = self.b31[:, h:h + 1]
        pairs = [(I, J) for I in range(NTB) for J in range(nsub * I + nsub)]

        def emit_S(idx):
            I, J = pairs[idx]
            st_ = idx % 2
            sA, sB = PS[4 + st_ * 2], PS[5 + st_ * 2]
            bA, bB = self.bps[4 + st_ * 2], self.bps[5 + st_ * 2]
            ks = slice(J * 128, (J + 1) * 128)
            qs = slice(I * TB, (I + 1) * TB)
            self.mm(sA[:, 0:TB], ka[0:64, ks], qa[0:64, qs], True, True, [bk, bq], [bA])
            self.mm(sB[:, 0:TB], ka[64:128, ks], qa[64:128, qs], True, True, [bk, bq], [bB])
        emit_S(0)
        for idx, (I, J) in enumerate(pairs):
            if idx + 1 < len(pairs):
                emit_S(idx + 1)
            last = nsub * I + nsub - 1
            a = J - nsub * I
            st_ = idx % 2
            sA, sB = PS[4 + st_ * 2], PS[5 + st_ * 2]
            bA, bB = self.bps[4 + st_ * 2], self.bps[5 + st_ * 2]
            c0 = max(a, 0) * 128
            for (sX, bX, pX, bpX) in ((sA, bA, p1[st_], bp1[st_]), (sB, bB, p2[st_], bp2[st_])):
                first = True
                if c0 > 0:
                    self.P.emit('gpsimd', (lambda a_: (lambda e: e.memset(a_, 0.0)))(pX[:, 0:c0]), writes=[bpX])
                cc = c0
                for sb_ in range(max(a, 0), nsub):
                    ty = sb_ - a
                    if a < -1 or ty >= 2:
                        break
                    if ty < 0:
                        continue
                    tj = tn[0] % 2
                    tn[0] += 1
                    sl_ = slice(sb_ * 128, (sb_ + 1) * 128)
                    self.tt(tmp[tj], sX[:, sl_], self.bt5[:, h, ty, :], ALU.add, [bX, self.bbt], [btmp[tj]])
                    self.act(pX[:, sl_], tmp[tj], AF.Exp, [btmp[tj]], [bpX], acc=not first)
                    first = False
                    cc = (sb_ + 1) * 128
                if cc < TB:
                    self.act(pX[:, cc:TB], sX[:, cc:TB], AF.Exp, [bX, self.bbt], [bpX], bias=b31, acc=not first)
            v_ = vt[:, J, :]
            accs = ((0, v_, p1[st_], bp1[st_]), (1, self.onesb, p1[st_], bp1[st_]),
                    (2, v_, p2[st_], bp2[st_]), (3, self.onesb, p2[st_], bp2[st_]))
            for (bk_, lhs, pX, bpX) in accs:
                self.mm(PS[bk_][:, 0:TB], lhs, pX[:, 0:TB], J == 0, J == last, [bvt, self.bc, bpX], [self.bps[bk_]])
            if J != last:
                continue
            fb = 4 + st_ * 2
            r = I % 2
            self.P.emit('vector', (lambda o_, i_: (lambda e: e.reciprocal(o_, i_)))(w1[r], PS[1][:, 0:TB]),
                        reads=[self.bps[1]], writes=[bw1[r]])
            self.tt(w1[r], w1[r], PS[0][:, 0:TB], ALU.mult, [bw1[r], self.bps[0]], [bw1[r]])
            self.P.emit('vector', (lambda o_, i_: (lambda e: e.reciprocal(o_, i_)))(w2[r], PS[3][:, 0:TB]),
                        reads=[self.bps[3]], writes=[bw2[r]])
            self.tt(w2[r], w2[r], PS[2][:, 0:TB], ALU.mult, [bw2[r], self.bps[2]], [bw2[r]])
            self.stt(w1[r], w2[r], neglam, w1[r], ALU.mult, ALU.add, [bw2[r], blm, bw1[r]], [bw1[r]])
            self.act(w2[r], w1[r], AF.Square, [bw1[r]], [bw2[r]])
            self.mm(PS[fb][:, 0:TB], self.ones, w2[r], True, True, [self.bc, bw2[r]], [self.bps[fb]])
            self.act(w2[r], PS[fb][:, 0:TB], AF.Sqrt, [self.bps[fb]], [bw2[r]], bias=EPS, scale=1.0 / 128.0)
            self.P.emit('vector', (lambda a_: (lambda e: e.reciprocal(a_, a_)))(w2[r]), reads=[bw2[r]], writes=[bw2[r]])
            self.stt(yb[r], w1[r], ngs, w2[r], ALU.mult, ALU.mult, [bw1[r], blm, bw2[r]], [byb[r]])
            self.dma('sync', S['mixT'][I, :, 8 + h, :], yb[r], reads=[byb[r]])
    self.P.barrier()


@_kb_method
def xattn(self, l, w_q, w_kv, w_o, ln_i):
    A = self.A
    TB, NTB, S = self.TB, self.NTB, self.S
    PS = self.ps
    for (dst, c0) in (('kxT', 0), ('vxT', D)):
        A.reset(self.base)
        eb = [A.take([128, NMEM], BF16) for _ in range(3)]
        beb = [Buf() for _ in range(3)]
        st = {'r': 0}

        def epi(g, tb, tiles, dst=dst, st=st, eb=eb, beb=beb):
            for s, (pst, bp) in enumerate(tiles):
                r = st['r'] % 3
                st['r'] += 1
                self.act(eb[r], pst, AF.Copy, [bp], [beb[r]])
                self.dma('sync', S[dst][0, :, 4 * g + s, :], eb[r], reads=[beb[r]])
        groups = [[(c0 + 512 * g + 128 * s, 128) for s in range(4)] for g in range(4)]
        self.gemm_fm(S['memT'], 16, 1, NMEM, w_kv, groups, epi, 4)
    A.reset(self.base)
    eb2 = [A.take([128, TB], BF16) for _ in range(3)]
    beb2 = [Buf() for _ in range(3)]
    st2 = {'r': 0}

    def epi2(g, tb, tiles):
        for s, (pst, bp) in enumerate(tiles):
            r = st2['r'] % 3
            st2['r'] += 1
            self.act(eb2[r], pst, AF.Copy, [bp], [beb2[r]])
            self.dma('sync', S['qx'][tb, :, 4 * g + s, :], eb2[r], reads=[beb2[r]])
    groups = [[(512 * g + 128 * s, 128) for s in range(4)] for g in range(4)]
    self.gemm_fm(S['xT'], 16, NTB, TB, w_q, groups, epi2, 4)
    A.reset(self.base)
    kx = A.take([128, 16, NMEM], BF16)
    vx = A.take([128, 16, NMEM], BF16)
    bkx, bvx = Buf(), Buf()
    self.dma('sync', kx, S['kxT'][0], writes=[bkx])
    self.dma('sync', vx, S['vxT'][0], writes=[bvx])
    vtm = A.take([128, 2, D], BF16)
    bvtm = Buf()
    trb = [PS[6][:, :].bitcast(BF16), PS[7][:, :].bitcast(BF16)]
    n = 0
    for c in range(16):
        for mj in range(2):
            bank = 6 + n % 2
            self.tr(trb[n % 2][:, 0:128], vx[:, c, mj * 128:(mj + 1) * 128], self.identb, [bvx, self.bc], [self.bps[bank]])
            self.cp(vtm[:, mj, c * 128:(c + 1) * 128], trb[n % 2][:, 0:128], [self.bps[bank]], [bvtm], acc=True)
            n += 1
    qb = [A.take([128, 16, TB], BF16) for _ in range(2)]
    bqb = [Buf(), Buf()]
    pt_ = [A.take([128, TB], BF16) for _ in range(4)]
    bpt = [Buf() for _ in range(4)]
    rz = [A.take([128, TB], F32) for _ in range(2)]
    brz = [Buf(), Buf()]
    ob = [A.take([128, TB], BF16) for _ in range(3)]
    bob = [Buf() for _ in range(3)]
    self.dma('sync', qb[0], S['qx'][0], writes=[bqb[0]])
    n = 0
    m = 0
    for tb in range(NTB):
        if tb + 1 < NTB:
            self.dma('sync', qb[(tb + 1) % 2], S['qx'][tb + 1], writes=[bqb[(tb + 1) % 2]])
        q = qb[tb % 2]
        for h in range(4):
            pp = []
            for mj in range(2):
                bank = 5 + (n % 3)
                j = n % 4
                n += 1
                for dc in range(4):
                    self.mm(PS[bank][:, 0:TB], kx[:, 4 * h + dc, mj * 128:(mj + 1) * 128], q[:, 4 * h + dc, :],
                            dc == 0, dc == 3, [bkx, bqb[tb % 2]], [self.bps[bank]])
                self.act(pt_[j], PS[bank][:, 0:TB], AF.Exp, [self.bps[bank]], [bpt[j]], scale=512 ** -0.5)
                pp.append((pt_[j], bpt[j]))
            for mj in range(2):
                self.mm(PS[4][:, 0:TB], self.onesb, pp[mj][0], mj == 0, mj == 1, [self.bc, pp[mj][1]], [self.bps[4]])
            for dvc in range(4):
                for mj in range(2):
                    self.mm(PS[dvc][:, 0:TB], vtm[:, mj, (4 * h + dvc) * 128:(4 * h + dvc + 1) * 128], pp[mj][0],
                            mj == 0, mj == 1, [bvtm, pp[mj][1]], [self.bps[dvc]])
            r = (tb * 4 + h) % 2
            self.P.emit('vector', (lambda o_, i_: (lambda e: e.reciprocal(o_, i_)))(rz[r], PS[4][:, 0:TB]),
                        reads=[self.bps[4]], writes=[brz[r]])
            for dvc in range(4):
                k3 = m % 3
                m += 1
                self.tt(ob[k3], PS[dvc][:, 0:TB], rz[r], ALU.mult, [self.bps[dvc], brz[r]], [bob[k3]])
                self.dma('sync', S['ox'][tb, :, 4 * h + dvc, :], ob[k3], reads=[bob[k3]])
    self.P.barrier()


def t5_onehots():
    oh = np.zeros((2, 128, 32 * 128 + 128), np.float32)
    j = np.arange(128)[:, None]
    i = np.arange(128)[None, :]
    for ty in range(2):
        rel = i - j + 128 * ty
        n = np.maximum(rel, 0)
        large = 16 + (np.log(np.maximum(n, 1).astype(np.float32) / np.float32(16)) / np.float32(np.log(128 / 16)) * 16).astype(np.int32)
        large = np.clip(large, 16, 31)
        bucket = np.where(n < 16, n, large)
        valid = rel >= 0
        for b in range(32):
            oh[ty, :, b * 128:(b + 1) * 128] = ((bucket == b) & valid).astype(np.float32)
        oh[ty, :, 4096:4224] = np.where(valid, 0.0, -1e30).astype(np.float32)
    return oh


def mlstm_sel():
    sel = np.zeros((8, 512), np.float32)
    for h in range(4):
        sel[h, h * 64:(h + 1) * 64] = 1.0
        sel[4 + h, 256 + h * 64:256 + (h + 1) * 64] = 1.0
    return sel


WNAMES = [('ffn_w_in', [DEPTH, 2, D, 2 * DFF]), ('ffn_w_out', [DEPTH, 2, DFF, D]), ('w_in', [DEPTH, D, NIN]),
          ('w_out', [DEPTH, D, D]), ('gla_gate_w', [DEPTH, 16, 256]), ('xattn_w_q', [DEPTH, D, D]),
          ('xattn_w_kv', [DEPTH, D, 2 * D]), ('xattn_w_o', [DEPTH, D, D])]


def build_program(T, depth=DEPTH, dbg=(), stop=None):
    kb = KB(T, dbg=dbg)
    cst = kb.dram_in("cst", [128, 832 + NSP])
    oh = kb.dram_in("oh", [2, 128, 32 * 128 + 128])
    sel = kb.dram_in("sel", [8, 512])
    x = kb.dram_in("x", [T, D])
    mem = kb.dram_in("mem", [NMEM, D])
    W = {nm: kb.dram_in(nm, shp) for nm, shp in WNAMES}
    out = kb.nc.dram_tensor("out", [T, D], F32, kind="ExternalOutput").ap()
    kb.setup_consts(cst, NSP)
    kb.hgrn_consts()
    kb.t5_bias(oh)
    kb.alloc_scratch()
    S = kb.S
    xres = [S['xres0'], S['xres1']]
    kb.transpose_in(x, T, kb.TB, xres[0], S['xT'])
    kb.transpose_in(mem, NMEM, NMEM, None, S['memT'])
    cur = 0

    def ln(l, i):
        nonlocal cur
        kb.layernorm(S['z'], sp_g(l, i), sp_b(l, i), xres[1 - cur], S['xT'])
        cur = 1 - cur
    for l in range(depth):
        o = l * LSTRIDE
        kb.ffn(S['xT'], xres[cur], W['ffn_w_in'][l, 0], W['ffn_w_out'][l, 0], S['hT'], S['z'])
        ln(l, 0)
        if stop == 'ffn1':
            break
        kb.in_proj(l, W['w_in'][l])
        if stop == 'inproj':
            break
        kb.gla_gate(l, W['gla_gate_w'][l])
        if stop == 'gate':
            break
        kb.mlstm_prep(l, sel)
        if stop == 'mlprep':
            break
        kb.gla_core(S['gla_q'], S['gla_k'], S['gla_g'], S['gla_v'], 64, 'rms', S['gla_r'], o + 130, 0)
        if stop == 'gla':
            break
        kb.gla_core(S['hg_q'], S['hg_k'], S['hg_g'], S['hg_v'], 128, 'rms', S['hg_o'], o + 131, 4)
        if stop == 'hg':
            break
        kb.diff_attn(l)
        if stop == 'diff':
            break
        kb.gla_core(S['ml_qc'], S['ml_kc'], S['ml_g'], S['ml_v'], 64, 'mlstm', S['ml_o'], None, 12)
        kb.gemm_resid(S['mixT'], 16, W['w_out'][l], xres[cur], S['z'])
        ln(l, 1)
        if stop == 'mix':
            break
        kb.xattn(l, W['xattn_w_q'][l], W['xattn_w_kv'][l], W['xattn_w_o'][l], 2)
        kb.gemm_resid(S['ox'], 16, W['xattn_w_o'][l], xres[cur], S['z'])
        ln(l, 2)
        if stop == 'xattn':
            break
        kb.ffn(S['xT'], xres[cur], W['ffn_w_in'][l, 1], W['ffn_w_out'][l, 1], S['hT'], S['z'])
        ln(l, 3)
    kb.transpose_out(xres[cur], out)
    kb.P.finalize()
    return kb


def make_in_maps(inp, T, ncores):
    f = lambda a: np.ascontiguousarray(np.asarray(a, np.float32))
    shared = {"cst": pack_consts(inp), "oh": t5_onehots(), "sel": mlstm_sel()}
    for nm, _ in WNAMES:
        shared[nm] = f(inp[nm])
    maps = []
    for c in range(ncores):
        b = c % inp['x'].shape[0]
        m = dict(shared)
        m["x"] = f(inp['x'][b, :T])
        m["mem"] = f(inp['mem'][b])
        maps.append(m)
    return maps


def kernel(**inputs):
    B, T = inputs['x'].shape[0], inputs['x'].shape[1]
    kb = build_program(T)
    maps = make_in_maps(inputs, T, 8)
    res = run_bass_kernel_spmd(kb.nc, maps, core_ids=list(range(8)))
    out = np.stack([np.asarray(res.results[b]["out"], np.float32) for b in range(B)], axis=0)
    return out
```

```python
import numpy as np
import concourse.bass as bass
import concourse.mybir as mybir
from concourse.bass_utils import run_bass_kernel_spmd

F32 = mybir.dt.float32
BF16 = mybir.dt.bfloat16
AF = mybir.ActivationFunctionType
ALU = mybir.AluOpType
AX = mybir.AxisListType

ENGS = ['sync', 'scalar', 'gpsimd', 'vector', 'tensor']
NDMA = 8


class Buf:
    __slots__ = ('name', 'lw', 'rd', 'excl')

    def __init__(self, name='', excl=False):
        self.name = name
        self.lw = None
        self.rd = {}
        self.excl = excl


class Prog:
    def __init__(self, nc):
        self.nc = nc
        self.q = {e: [] for e in ENGS}
        self.cnt = {e: 0 for e in ENGS}
        self.known = {e: {} for e in ENGS}
        self.dma_next = {e: 0 for e in ENGS}
        self.dma_val = {}
        self.ninst = 0

    def emit(self, eng, fn, reads=(), writes=(), dma=False, acc=False):
        deps = {}

        def add(t):
            if t is None:
                return
            k, v = t
            if deps.get(k, 0) < v:
                deps[k] = v
        for b in reads:
            if b.excl and b.lw is not None and b.lw[0] == eng:
                continue
            add(b.lw)
        for b in writes:
            if not (acc and b.lw is not None and b.lw[0] == eng):
                add(b.lw)
            for k, v in b.rd.items():
                add((k, v))
        if dma:
            slot = self.dma_next[eng]
            self.dma_next[eng] = (slot + 1) % NDMA
            key = ('dma', eng, slot)
            prev = self.dma_val.get(key, 0)
            if prev > 0:
                add((key, prev))
            tok = (key, prev + 16)
            self.dma_val[key] = prev + 16
        else:
            self.cnt[eng] += 1
            tok = (eng, self.cnt[eng])
        kn = self.known[eng]
        waits = []
        for k, v in deps.items():
            if kn.get(k, 0) >= v:
                continue
            kn[k] = v
            waits.append((k, v))
        self.q[eng].append((waits, fn, tok))
        self.ninst += 1
        for b in reads:
            if b.excl:
                b.lw = tok
                b.rd = {}
            elif b.rd.get(tok[0], 0) < tok[1]:
                b.rd[tok[0]] = tok[1]
        for b in writes:
            b.lw = tok
            b.rd = {}
        return tok

    def barrier(self):
        allk = {}
        for e in ENGS:
            if self.cnt[e] > 0:
                allk[e] = self.cnt[e]
        for k, v in self.dma_val.items():
            allk[k] = v
        for e in ENGS:
            kn = self.known[e]
            waits = []
            for k, v in allk.items():
                if kn.get(k, 0) < v:
                    kn[k] = v
                    waits.append((k, v))
            if waits:
                self.q[e].append((waits, None, None))

    def finalize(self):
        nc = self.nc
        self.barrier()
        sems = {}

        def sem(k):
            if k not in sems:
                nm = k if isinstance(k, str) else "d_%s_%d" % (k[1], k[2])
                sems[k] = nc.alloc_semaphore("s_" + nm)
            return sems[k]
        for e in ENGS:
            sem(e)
        for k in self.dma_val:
            sem(k)
        prog = self

        def replay(ename):
            def body(e):
                for waits, fn, tok in prog.q[ename]:
                    for k, v in waits:
                        e.wait_ge(sem(k), v)
                    if fn is None:
                        continue
                    ins = fn(e)
                    if tok[0] == ename:
                        ins.then_inc(sem(ename), 1)
                    else:
                        ins.then_inc(sem(tok[0]), 16)
            return body
        with nc.Block() as block:
            block.sync(replay('sync'))
            block.scalar(replay('scalar'))
            block.gpsimd(replay('gpsimd'))
            block.vector(replay('vector'))
            block.tensor(replay('tensor'))


class Arena:
    def __init__(self, nc, nwords, name="arena"):
        self.t = nc.alloc_sbuf_tensor(name, [128, nwords], F32)
        self.n = nwords
        self.off = 0

    def mark(self):
        return self.off

    def reset(self, m=0):
        self.off = m

    def take(self, shape, dtype, name=''):
        nel = 1
        for s in shape[1:]:
            nel *= s
        nw = nel if dtype == F32 else (nel + 1) // 2
        nw = (nw + 7) // 8 * 8
        assert self.off + nw <= self.n, "arena overflow %d+%d>%d (%s)" % (self.off, nw, self.n, name)
        v = self.t[0:shape[0], self.off:self.off + nw]
        self.off += nw
        if dtype != F32:
            v = v.bitcast(dtype)
        v = v[:, 0:nel]
        if len(shape) == 3:
            v = v.rearrange("p (a b) -> p a b", a=shape[1])
        elif len(shape) == 4:
            v = v.rearrange("p (a b c) -> p a b c", a=shape[1], b=shape[2])
        return v


D = 2048
DFF = 5632
NIN = 6680
NMEM = 256
DEPTH = 2
ALPHA = float(4.0 ** 0.25)
EPS = 1e-5
NWORDS = 52000


class KB:
    def __init__(self, T, dbg=()):
        self.T = T
        self.TB = min(512, T)
        self.NTB = T // self.TB
        self.nc = bass.Bass("TRN2", target_bir_lowering=False)
        self.P = Prog(self.nc)
        self.A = Arena(self.nc, NWORDS)
        self.ps = [self.nc.alloc_psum_tensor("ps%d" % i, [128, 512], F32) for i in range(8)]
        self.bps = [Buf('ps%d' % i, excl=True) for i in range(8)]
        self.dbg = set(dbg)
        self.rr = 0

    def dram(self, name, shape, dtype):
        kind = "ExternalOutput" if name in self.dbg else "Internal"
        return self.nc.dram_tensor(name, list(shape), dtype, kind=kind).ap()

    def dram_in(self, name, shape, dtype=F32):
        return self.nc.dram_tensor(name, list(shape), dtype, kind="ExternalInput").ap()

    def dma(self, q, out, in_, reads=(), writes=()):
        return self.P.emit(q, lambda e: e.dma_start(out=out, in_=in_), reads=reads, writes=writes, dma=True)

    def mm(self, out, lhsT, rhs, start, stop, reads, writes):
        return self.P.emit('tensor', lambda e: e.matmul(out, lhsT, rhs, start=start, stop=stop),
                           reads=reads, writes=writes, acc=True)

    def tr(self, out, in_, ident, reads, writes):
        return self.P.emit('tensor', lambda e: e.transpose(out, in_, ident), reads=reads, writes=writes, acc=True)

    def act(self, out, in_, func, reads, writes, bias=0.0, scale=1.0, acc=False):
        return self.P.emit('scalar', lambda e: e.activation(out, in_, func, bias=bias, scale=scale),
                           reads=reads, writes=writes, acc=acc)

    def tt(self, out, in0, in1, op, reads, writes, eng='vector', acc=False):
        return self.P.emit(eng, lambda e: e.tensor_tensor(out, in0, in1, op), reads=reads, writes=writes, acc=acc)

    def ts(self, out, in0, s1, s2, op0, op1, reads, writes, eng='vector', acc=False):
        if op1 is None:
            return self.P.emit(eng, lambda e: e.tensor_scalar(out, in0, s1, None, op0), reads=reads, writes=writes, acc=acc)
        return self.P.emit(eng, lambda e: e.tensor_scalar(out, in0, s1, s2, op0, op1), reads=reads, writes=writes, acc=acc)

    def stt(self, out, in0, scalar, in1, op0, op1, reads, writes, eng='vector', acc=False):
        return self.P.emit(eng, lambda e: e.scalar_tensor_tensor(out, in0, scalar, in1, op0, op1),
                           reads=reads, writes=writes, acc=acc)

    def cp(self, out, in_, reads, writes, eng='vector', acc=False):
        return self.P.emit(eng, lambda e: e.tensor_copy(out, in_), reads=reads, writes=writes, acc=acc)

    def setup_consts(self, cst, nsp):
        A = self.A
        self.cf = A.take([128, 128 + 128 + 64 + 512 + nsp], F32, 'consts')
        self.bc = Buf('consts')
        self.dma('sync', self.cf, cst, writes=[self.bc])
        self.ident = self.cf[:, 0:128]
        self.ones = self.cf[:, 128:256]
        self.maskT = self.cf[0:64, 256:320]
        self.reset = self.cf[:, 320:832]
        self.sp = self.cf[:, 832:832 + nsp]
        self.identb = A.take([128, 128], BF16, 'identb')
        self.onesb = A.take([128, 128], BF16, 'onesb')
        self.cp(self.identb, self.ident, [self.bc], [self.bc])
        self.cp(self.onesb, self.ones, [self.bc], [self.bc])
        self.P.barrier()
        self.base = A.mark()

    def transpose_in(self, x, T, TB, out_f32, out_bf):
        A = self.A
        A.reset(self.base)
        NTB = T // TB
        nsub = TB // 128
        xin = [A.take([128, D], F32) for _ in range(2)]
        bxin = [Buf(), Buf()]
        blkf = [A.take([128, 16, TB], F32) for _ in range(2)] if out_f32 is not None else None
        blkb = [A.take([128, 16, TB], BF16) for _ in range(2)]
        bblk = [Buf(), Buf()]
        n = 0
        for tb in range(NTB):
            for s in range(nsub):
                ttile = tb * nsub + s
                xi = xin[ttile % 2]
                bxi = bxin[ttile % 2]
                self.dma('sync', xi, x[ttile * 128:(ttile + 1) * 128, :], writes=[bxi])
                for g4 in range(4):
                    bank = n % 8
                    n += 1
                    for j in range(4):
                        kc = g4 * 4 + j
                        self.tr(self.ps[bank][:, j * 128:(j + 1) * 128], xi[:, kc * 128:(kc + 1) * 128], self.ident,
                                [bxi, self.bc], [self.bps[bank]])
                    src = self.ps[bank][:, :].rearrange("p (a b) -> p a b", a=4)
                    if blkf is not None:
                        self.P.emit('scalar', (lambda o, i: (lambda e: e.copy(o, i)))(
                            blkf[tb % 2][:, g4 * 4:(g4 + 1) * 4, s * 128:(s + 1) * 128], src),
                            reads=[self.bps[bank]], writes=[bblk[tb % 2]], acc=True)
                    self.cp(blkb[tb % 2][:, g4 * 4:(g4 + 1) * 4, s * 128:(s + 1) * 128], src,
                            [self.bps[bank]], [bblk[tb % 2]], acc=True)
            if blkf is not None:
                self.dma('sync', out_f32[tb], blkf[tb % 2], reads=[bblk[tb % 2]])
            self.dma('sync', out_bf[tb], blkb[tb % 2], reads=[bblk[tb % 2]])
        self.P.barrier()

    def transpose_out(self, xsrc, out):
        A = self.A
        A.reset(self.base)
        TB, NTB = self.TB, self.NTB
        nsub = TB // 128
        blk = [A.take([128, 16, TB], F32) for _ in range(2)]
        bblk = [Buf(), Buf()]
        ot = [A.take([128, D], F32) for _ in range(2)]
        bot = [Buf(), Buf()]
        n = 0
        for tb in range(NTB):
            self.dma('sync', blk[tb % 2], xsrc[tb], writes=[bblk[tb % 2]])
            for s in range(nsub):
                ttile = tb * nsub + s
                o = ot[ttile % 2]
                bo = bot[ttile % 2]
                for g4 in range(4):
                    bank = n % 8
                    n += 1
                    for j in range(4):
                        kc = g4 * 4 + j
                        self.tr(self.ps[bank][:, j * 128:(j + 1) * 128], blk[tb % 2][:, kc, s * 128:(s + 1) * 128],
                                self.ident, [bblk[tb % 2], self.bc], [self.bps[bank]])
                    if g4 % 2 == 0:
                        self.P.emit('scalar', (lambda o_, i_: (lambda e: e.copy(o_, i_)))(
                            o[:, g4 * 512:(g4 + 1) * 512], self.ps[bank][:, :]),
                            reads=[self.bps[bank]], writes=[bo], acc=True)
                    else:
                        self.cp(o[:, g4 * 512:(g4 + 1) * 512], self.ps[bank][:, :], [self.bps[bank]], [bo], acc=True)
                self.dma('sync', out[ttile * 128:(ttile + 1) * 128, :], o, reads=[bo])
        self.P.barrier()

    def gemm_fm(self, xsrc, KC, NTB, TB, w, groups, epi, MG, pre=None, nwst=3):
        A = self.A
        KS = 16
        wst = [A.take([128, KS, 128], F32) for _ in range(nwst)]
        bwst = [Buf() for _ in range(nwst)]
        wbuf = [A.take([128, KC, MG * 128], BF16) for _ in range(2)]
        bwb = [Buf(), Buf()]
        xb = [A.take([128, KC, TB], BF16) for _ in range(2)]
        bxb = [Buf(), Buf()]
        wv = w.rearrange("(kc p) n -> p kc n", p=128)
        NG = len(groups)
        st = {'wp': 0}

        def pieces(g):
            lst = []
            for s, (c0, n) in enumerate(groups[g]):
                for k0 in range(0, KC, KS):
                    lst.append((g, s, c0, n, k0, min(KC, k0 + KS)))
            return lst

        def load_piece(pc):
            g, s, c0, n, k0, k1 = pc
            j = st['wp'] % nwst
            st['wp'] += 1
            self.dma('sync', wst[j][:, 0:k1 - k0, 0:n], wv[:, k0:k1, c0:c0 + n], writes=[bwst[j]])
            self.cp(wbuf[g % 2][:, k0:k1, s * 128:s * 128 + n], wst[j][:, 0:k1 - k0, 0:n],
                    [bwst[j]], [bwb[g % 2]], eng='gpsimd', acc=True)

        items = [(g, tb) for g in range(NG) for tb in range(NTB)]
        for pc in pieces(0):
            load_piece(pc)
        self.dma('sync', xb[0], xsrc[0], writes=[bxb[0]])
        pend = []
        for i, (g, tb) in enumerate(items):
            if tb == 0:
                pend = pieces(g + 1) if g + 1 < NG else []
            per = (len(pend) + (NTB - tb) - 1) // (NTB - tb)
            for _ in range(per):
                load_piece(pend.pop(0))
            if i + 1 < len(items):
                self.dma('sync', xb[(i + 1) % 2], xsrc[items[i + 1][1]], writes=[bxb[(i + 1) % 2]])
            if pre is not None:
                pre(g, tb)
            tiles = []
            for s, (c0, n) in enumerate(groups[g]):
                bank = (i % 2) * 4 + s
                pst = self.ps[bank][0:n, 0:TB]
                for kc in range(KC):
                    self.mm(pst, wbuf[g % 2][:, kc, s * 128:s * 128 + n], xb[i % 2][:, kc, :], kc == 0, kc == KC - 1,
                            [bwb[g % 2], bxb[i % 2]], [self.bps[bank]])
                tiles.append((pst, self.bps[bank]))
            epi(g, tb, tiles)
        self.P.barrier()

    def layernorm(self, zsrc, gcol, bcol, out_f32, out_bf):
        A = self.A
        A.reset(self.base)
        TB, NTB = self.TB, self.NTB
        zb = [A.take([128, 16, TB], F32) for _ in range(2)]
        bzb = [Buf(), Buf()]
        ob = [A.take([128, 16, TB], BF16) for _ in range(2)]
        bob = [Buf(), Buf()]
        sq = [A.take([128, TB], F32) for _ in range(2)]
        bsq = [Buf(), Buf()]
        tmp = [A.take([128, TB], F32) for _ in range(4)]
        btmp = [Buf() for _ in range(4)]
        mean = A.take([128, TB], F32)
        rstd = A.take([128, TB], F32)
        bst = Buf()
        self.dma('sync', zb[0], zsrc[0], writes=[bzb[0]])
        for tb in range(NTB):
            z = zb[tb % 2]
            bz = bzb[tb % 2]
            if tb + 1 < NTB:
                self.dma('sync', zb[(tb + 1) % 2], zsrc[tb + 1], writes=[bzb[(tb + 1) % 2]])
            b1, b2 = (tb % 2) * 2, (tb % 2) * 2 + 1
            p1 = self.ps[b1][:, 0:TB]
            p2 = self.ps[b2][:, 0:TB]
            for kc in range(16):
                self.mm(p1, self.ones, z[:, kc, :], kc == 0, kc == 15, [self.bc, bz], [self.bps[b1]])
                j = kc % 2
                self.act(sq[j], z[:, kc, :], AF.Square, [bz], [bsq[j]])
                self.mm(p2, self.ones, sq[j], kc == 0, kc == 15, [self.bc, bsq[j]], [self.bps[b2]])
            self.act(mean, p1, AF.Copy, [self.bps[b1]], [bst], scale=1.0 / D)
            self.tt(rstd, mean, mean, ALU.mult, [bst], [bst])
            self.stt(rstd, p2, 1.0 / D, rstd, ALU.mult, ALU.subtract, [self.bps[b2], bst], [bst])
            self.act(rstd, rstd, AF.Sqrt, [bst], [bst], bias=EPS)
            self.P.emit('vector', lambda e: e.reciprocal(rstd, rstd), reads=[bst], writes=[bst])
            for kc in range(16):
                j = kc % 4
                self.tt(tmp[j], z[:, kc, :], mean, ALU.subtract, [bz, bst], [btmp[j]])
                self.tt(tmp[j], tmp[j], rstd, ALU.mult, [btmp[j], bst], [btmp[j]])
                self.act(z[:, kc, :], tmp[j], AF.Identity, [btmp[j], self.bc], [bz],
                         bias=self.sp[:, bcol + kc:bcol + kc + 1], scale=self.sp[:, gcol + kc:gcol + kc + 1])
                self.cp(ob[tb % 2][:, kc, :], z[:, kc, :], [bz], [bob[tb % 2]])
            self.dma('sync', out_f32[tb], z, reads=[bz])
            self.dma('sync', out_bf[tb], ob[tb % 2], reads=[bob[tb % 2]])
        self.P.barrier()

    def ffn(self, xT, xres_in, w_in, w_out, hT, z):
        A = self.A
        TB, NTB = self.TB, self.NTB
        A.reset(self.base)
        sg = [A.take([128, TB], F32) for _ in range(2)]
        bsg = [Buf(), Buf()]
        ht = [A.take([128, TB], BF16) for _ in range(3)]
        bht = [Buf() for _ in range(3)]
        st = {'r': 0}

        def epi(g, tb, tiles):
            for s in range(2):
                r = st['r']
                st['r'] += 1
                pg, bg = tiles[s]
                pu, bu = tiles[s + 2]
                self.act(sg[r % 2], pg, AF.Silu, [bg], [bsg[r % 2]])
                self.stt(ht[r % 3], pu, 0.5, sg[r % 2], ALU.mult, ALU.mult, [bu, bsg[r % 2]], [bht[r % 3]])
                self.dma('sync', hT[tb, :, 2 * g + s, :], ht[r % 3], reads=[bht[r % 3]])
        groups = [[(256 * g, 128), (256 * g + 128, 128), (DFF + 256 * g, 128), (DFF + 256 * g + 128, 128)]
                  for g in range(DFF // 256)]
        self.gemm_fm(xT, 16, NTB, TB, w_in, groups, epi, 4)
        self.gemm_resid(hT, 44, w_out, xres_in, z)

    def gemm_resid(self, src, KC, w, xres_in, z):
        A = self.A
        TB, NTB = self.TB, self.NTB
        A.reset(self.base)
        MG = 4 if KC <= 16 else 3
        groups = []
        c = 0
        while c < 16:
            n = min(MG, 16 - c)
            groups.append([(128 * (c + s), 128) for s in range(n)])
            c += n
        NXR = 2 * MG
        xr = [A.take([128, TB], F32) for _ in range(NXR)]
        bxr = [Buf() for _ in range(NXR)]
        zt = [A.take([128, TB], F32) for _ in range(3)]
        bzt = [Buf() for _ in range(3)]
        st = {'r': 0, 'q': 0}
        gstart = [sum(len(g) for g in groups[:i]) for i in range(len(groups))]

        def pre(g, tb):
            for s in range(len(groups[g])):
                q = st['q']
                st['q'] += 1
                self.dma('sync', xr[q % NXR], xres_in[tb, :, gstart[g] + s, :], writes=[bxr[q % NXR]])

        def epi(g, tb, tiles):
            for s in range(len(groups[g])):
                r = st['r']
                st['r'] += 1
                pz, bz = tiles[s]
                self.stt(zt[r % 3], xr[r % NXR], ALPHA, pz, ALU.mult, ALU.add, [bxr[r % NXR], bz], [bzt[r % 3]])
                self.dma('sync', z[tb, :, gstart[g] + s, :], zt[r % 3], reads=[bzt[r % 3]])
        self.gemm_fm(src, KC, NTB, TB, w, groups, epi, MG, pre=pre, nwst=(3 if KC <= 16 else 2))


LSTRIDE = 416
SP_HLB = 2 * LSTRIDE
SP_T5 = SP_HLB + 8
NSP = SP_T5 + 128


def sp_g(l, i):
    return l * LSTRIDE + i * 32


def sp_b(l, i):
    return l * LSTRIDE + i * 32 + 16


def pack_consts(inp):
    c = np.zeros((128, 832 + NSP), np.float32)
    c[:, 0:128] = np.eye(128, dtype=np.float32)
    c[:, 128:256] = 1.0
    jj, ii = np.meshgrid(np.arange(64), np.arange(64), indexing='ij')
    c[0:64, 256:320] = (jj <= ii).astype(np.float32)
    r = np.ones(512, np.float32)
    r[::64] = 0.0
    c[:, 320:832] = r[None, :]
    sp = c[:, 832:]

    def fm(v):
        return np.asarray(v, np.float32).reshape(-1, 128).T
    for l in range(DEPTH):
        o = l * LSTRIDE
        for i in range(4):
            sp[:, o + i * 32:o + i * 32 + 16] = fm(inp['ln_g'][l, i])
            sp[:, o + i * 32 + 16:o + i * 32 + 32] = fm(inp['ln_b'][l, i])
        sp[:, o + 128:o + 130] = fm(inp['gla_gate_b'][l])
        sp[:, o + 130] = inp['gla_norm_g'][l]
        sp[:, o + 131] = inp['hgrn_norm_g'][l]
        sp[:, o + 132] = inp['diff_norm_g'][l]
        for tap in range(4):
            sp[:, o + 133 + tap * 4:o + 133 + tap * 4 + 4] = fm(inp['mlstm_conv_w'][l, tap])
        sp[0:8, o + 149] = inp['mlstm_gate_b'][l]
        sp[:, o + 150:o + 406] = np.asarray(inp['diff_lambda'][l], np.float32).reshape(1, 256)
        sp[:, SP_HLB + 4 * l:SP_HLB + 4 * l + 4] = fm(inp['hgrn_lb'][l])
    sp[:, SP_T5:SP_T5 + 128] = np.asarray(inp['t5_table'], np.float32).reshape(1, 128)
    return c


SEGS = [('a_q', 0, 256), ('a_k', 256, 256), ('a_v', 512, 512), ('a_lr', 1024, 16), ('a_r', 1040, 512),
        ('b_q', 1552, 512), ('b_f', 2064, 512), ('b_i', 2576, 512), ('b_g', 3088, 512),
        ('c_q', 3600, 512), ('c_k', 4112, 512), ('c_v', 4624, 512),
        ('d_q', 5136, 256), ('d_k', 5392, 256), ('d_v', 5648, 512), ('d_if', 6160, 8), ('d_o', 6168, 512)]


def _kb_method(f):
    setattr(KB, f.__name__, f)
    return f


@_kb_method
def alloc_scratch(self):
    NTB, TB = self.NTB, self.TB
    S = {}

    def fmt(name, nch, dt):
        S[name] = self.dram(name, [NTB, 128, nch, TB], dt)
    for i in range(2):
        fmt('xres%d' % i, 16, F32)
    fmt('xT', 16, BF16)
    fmt('z', 16, F32)
    fmt('hT', 44, BF16)
    fmt('qx', 16, BF16)
    fmt('ox', 16, BF16)
    fmt('mixT', 16, BF16)
    for nm, n, dt in [('gla_q', 2, BF16), ('gla_k', 2, BF16), ('gla_v', 4, BF16), ('gla_r', 4, F32), ('gla_g', 2, F32),
                      ('hg_q', 4, BF16), ('hg_k', 4, BF16), ('hg_g', 4, F32), ('hg_v', 4, BF16), ('hg_o', 4, F32),
                      ('df_q', 4, BF16), ('df_k', 4, BF16), ('df_v', 4, BF16),
                      ('ml_q', 2, F32), ('ml_k', 2, F32), ('ml_v', 4, BF16), ('ml_o', 4, F32),
                      ('ml_qc', 2, BF16), ('ml_kc', 2, BF16), ('ml_g', 2, F32)]:
        fmt(nm, n, dt)
    S['gla_lr'] = self.dram('gla_lr', [NTB, 16, TB], F32)
    S['ml_if'] = self.dram('ml_if', [NTB, 8, TB], F32)
    S['memT'] = self.dram('memT', [1, 128, 16, NMEM], BF16)
    S['kxT'] = self.dram('kxT', [1, 128, 16, NMEM], BF16)
    S['vxT'] = self.dram('vxT', [1, 128, 16, NMEM], BF16)
    self.S = S


@_kb_method
def hgrn_consts(self):
    A = self.A
    self.hl = A.take([128, 16], F32, 'hl')
    self.bhl = Buf()
    hl = self.hl
    h0 = self.sp[:, SP_HLB:SP_HLB + 4]
    h1 = self.sp[:, SP_HLB + 4:SP_HLB + 8]
    self.P.emit('vector', lambda e: e.memset(hl[:, 0:4], 1e-12), writes=[self.bhl])
    self.P.emit('vector', lambda e: e.memset(hl[:, 4:8], 1.0), writes=[self.bhl])
    self.tt(hl[:, 8:12], h1, h0, ALU.subtract, [self.bc], [self.bhl])
    self.act(hl[:, 8:12], hl[:, 8:12], AF.Sigmoid, [self.bhl], [self.bhl])
    self.ts(hl[:, 8:12], hl[:, 8:12], 1.0 - 1e-6, 0.0, ALU.min, ALU.max, [self.bhl], [self.bhl])
    self.ts(hl[:, 12:16], hl[:, 8:12], -1.0, 1.0, ALU.mult, ALU.add, [self.bhl], [self.bhl])
    self.ts(hl[:, 8:12], hl[:, 8:12], 1e-12, None, ALU.max, None, [self.bhl], [self.bhl])
    self.P.barrier()
    self.base = A.mark()


@_kb_method
def log_sigmoid(self, out, x, t1, reads, bufs, post_scale=1.0):
    bx, bo, bt = bufs
    self.act(t1, x, AF.Abs, reads + [bx], [bt])
    self.act(t1, t1, AF.Exp, [bt], [bt], scale=-1.0)
    self.act(t1, t1, AF.Ln, [bt], [bt], bias=1.0)
    self.ts(out, x, 0.0, None, ALU.min, None, [bx], [bo])
    self.tt(out, out, t1, ALU.subtract, [bo, bt], [bo])
    if post_scale != 1.0:
        self.ts(out, out, post_scale, None, ALU.mult, None, [bo], [bo])


@_kb_method
def in_proj(self, l, w_in):
    A = self.A
    A.reset(self.base)
    TB, NTB, S = self.TB, self.NTB, self.S
    chunks = []
    for nm, c0, wd in SEGS:
        for i in range(0, wd, 128):
            chunks.append((nm, i // 128, c0 + i, min(128, wd - i)))
    groups = [chunks[i:i + 4] for i in range(0, len(chunks), 4)]
    ef = [A.take([128, TB], F32) for _ in range(4)]
    bef = [Buf() for _ in range(4)]
    eb = [A.take([128, TB], BF16) for _ in range(4)]
    beb = [Buf() for _ in range(4)]
    st = {'f': 0, 'b': 0}
    o = l * LSTRIDE

    def nf():
        st['f'] += 1
        return ef[st['f'] % 4], bef[st['f'] % 4]

    def nb():
        st['b'] += 1
        return eb[st['b'] % 4], beb[st['b'] % 4]
    plain = {'a_q': ('gla_q', 64 ** -0.5), 'a_k': ('gla_k', 1.0), 'a_v': ('gla_v', 1.0), 'b_i': ('hg_v', 1.0),
             'c_q': ('df_q', 64 ** -0.5), 'c_k': ('df_k', 1.0), 'c_v': ('df_v', 1.0), 'd_v': ('ml_v', 1.0)}
    actf = {'a_r': ('gla_r', AF.Silu), 'b_g': ('hg_o', AF.Silu), 'd_o': ('ml_o', AF.Sigmoid),
            'd_q': ('ml_q', AF.Copy), 'd_k': ('ml_k', AF.Copy)}

    def epi(g, tb, tiles):
        for (nm, idx, c0, n), (pst, bp) in zip(groups[g], tiles):
            if nm in plain:
                dst, sc = plain[nm]
                t, bt = nb()
                self.act(t, pst, AF.Copy, [bp], [bt], scale=sc)
                self.dma('sync', S[dst][tb, :, idx, :], t, reads=[bt])
            elif nm in actf:
                dst, fn = actf[nm]
                t, bt = nf()
                self.act(t, pst, fn, [bp], [bt])
                self.dma('sync', S[dst][tb, :, idx, :], t, reads=[bt])
            elif nm == 'a_lr':
                t, bt = nf()
                self.act(t[0:16, :], pst, AF.Copy, [bp], [bt])
                self.dma('sync', S['gla_lr'][tb], t[0:16, :], reads=[bt])
            elif nm == 'd_if':
                t, bt = nf()
                self.act(t[0:8, :], pst, AF.Identity, [bp, self.bc], [bt], bias=self.sp[0:8, o + 149:o + 150])
                self.dma('sync', S['ml_if'][tb], t[0:8, :], reads=[bt])
            elif nm == 'b_q':
                t, bt = nf()
                self.act(t, pst, AF.Silu, [bp], [bt])
                t2, bt2 = nb()
                self.ts(t2, t, 128 ** -0.5, None, ALU.mult, None, [bt], [bt2])
                self.dma('sync', S['hg_q'][tb, :, idx, :], t2, reads=[bt2])
            elif nm == 'b_f':
                sg, bsg = nf()
                self.act(sg, pst, AF.Sigmoid, [bp], [bsg])
                f, bf_ = nf()
                lbc = self.hl[:, l * 8 + idx:l * 8 + idx + 1]
                oml = self.hl[:, l * 8 + 4 + idx:l * 8 + 4 + idx + 1]
                self.ts(f, sg, oml, lbc, ALU.mult, ALU.add, [bsg, self.bhl], [bf_])
                self.act(f, f, AF.Ln, [bf_], [bf_])
                self.dma('sync', S['hg_g'][tb, :, idx, :], f, reads=[bf_])
                self.ts(sg, sg, -1.0, 1.0, ALU.mult, ALU.add, [bsg], [bsg])
                t2, bt2 = nb()
                self.ts(t2, sg, oml, None, ALU.mult, None, [bsg, self.bhl], [bt2])
                self.dma('sync', S['hg_k'][tb, :, idx, :], t2, reads=[bt2])
            else:
                raise ValueError(nm)
    self.gemm_fm(S['xT'], 16, NTB, TB, w_in, [[(c[2], c[3]) for c in g] for g in groups], epi, 4)


@_kb_method
def gla_gate(self, l, gate_w):
    A = self.A
    A.reset(self.base)
    TB, NTB, S = self.TB, self.NTB, self.S
    o = l * LSTRIDE
    gw = A.take([16, 256], F32)
    bgw = Buf()
    self.dma('sync', gw, gate_w, writes=[bgw])
    lr = [A.take([16, TB], F32) for _ in range(2)]
    blr = [Buf(), Buf()]
    xs = [A.take([128, TB], F32) for _ in range(2)]
    bxs = [Buf(), Buf()]
    t1 = [A.take([128, TB], F32) for _ in range(2)]
    bt1 = [Buf(), Buf()]
    n = 0
    for tb in range(NTB):
        self.dma('sync', lr[tb % 2], S['gla_lr'][tb], writes=[blr[tb % 2]])
        for t in range(2):
            bank = n % 8
            j = n % 2
            n += 1
            pst = self.ps[bank][:, 0:TB]
            self.mm(pst, gw[:, t * 128:(t + 1) * 128], lr[tb % 2], True, True, [bgw, blr[tb % 2]], [self.bps[bank]])
            self.act(xs[j], pst, AF.Identity, [self.bps[bank], self.bc], [bxs[j]], bias=self.sp[:, o + 128 + t:o + 129 + t])
            self.log_sigmoid(xs[j], xs[j], t1[j], [], (bxs[j], bxs[j], bt1[j]), post_scale=1.0 / 16.0)
            self.dma('sync', S['gla_g'][tb, :, t, :], xs[j], reads=[bxs[j]])
    self.P.barrier()


@_kb_method
def mlstm_prep(self, l, sel):
    A = self.A
    A.reset(self.base)
    TB, NTB, S = self.TB, self.NTB, self.S
    o = l * LSTRIDE
    sl = A.take([8, 512], F32)
    bsl = Buf()
    self.dma('sync', sl, sel, writes=[bsl])
    gi = [A.take([8, TB], F32) for _ in range(2)]
    bgi = [Buf(), Buf()]
    ei = [A.take([128, TB], F32) for _ in range(2)]
    bei = [Buf(), Buf()]
    gt = [A.take([128, TB], F32) for _ in range(2)]
    bgt = [Buf(), Buf()]
    t1 = [A.take([128, TB], F32) for _ in range(2)]
    bt1 = [Buf(), Buf()]
    xr = [A.take([128, TB + 3], F32) for _ in range(2)]
    bxr = [Buf(), Buf()]
    cv = [A.take([128, TB], F32) for _ in range(2)]
    bcv = [Buf(), Buf()]
    ob = [A.take([128, TB], BF16) for _ in range(2)]
    bob = [Buf(), Buf()]
    n = 0
    m = 0
    for tb in range(NTB):
        self.dma('sync', gi[tb % 2], S['ml_if'][tb], writes=[bgi[tb % 2]])
        for t in range(2):
            j = n % 2
            b1, b2 = (n % 4) * 2, (n % 4) * 2 + 1
            n += 1
            p1 = self.ps[b1][:, 0:TB]
            p2 = self.ps[b2][:, 0:TB]
            self.mm(p1, sl[:, t * 128:(t + 1) * 128], gi[tb % 2], True, True, [bsl, bgi[tb % 2]], [self.bps[b1]])
            self.mm(p2, sl[:, 256 + t * 128:256 + (t + 1) * 128], gi[tb % 2], True, True, [bsl, bgi[tb % 2]], [self.bps[b2]])
            self.act(ei[j], p1, AF.Exp, [self.bps[b1]], [bei[j]])
            self.act(gt[j], p2, AF.Copy, [self.bps[b2]], [bgt[j]])
            self.log_sigmoid(gt[j], gt[j], t1[j], [], (bgt[j], bgt[j], bt1[j]))
            self.dma('sync', S['ml_g'][tb, :, t, :], gt[j], reads=[bgt[j]])
            for which in range(2):
                src = S['ml_q'] if which == 0 else S['ml_k']
                dst = S['ml_qc'] if which == 0 else S['ml_kc']
                jj = m % 2
                m += 1
                x_ = xr[jj]
                if tb == 0:
                    self.P.emit('vector', (lambda a: (lambda e: e.memset(a, 0.0)))(x_[:, 0:3]), writes=[bxr[jj]])
                else:
                    self.dma('sync', x_[:, 0:3], src[tb - 1, :, t, TB - 3:TB], writes=[bxr[jj]])
                self.dma('sync', x_[:, 3:3 + TB], src[tb, :, t, :], writes=[bxr[jj]])
                c_ = cv[jj]
                tile_ = which * 2 + t
                for tap in range(4):
                    cw = self.sp[:, o + 133 + tap * 4 + tile_:o + 134 + tap * 4 + tile_]
                    if tap == 0:
                        self.ts(c_, x_[:, 0:TB], cw, None, ALU.mult, None, [bxr[jj], self.bc], [bcv[jj]])
                    else:
                        self.stt(c_, x_[:, tap:tap + TB], cw, c_, ALU.mult, ALU.add, [bxr[jj], self.bc, bcv[jj]], [bcv[jj]])
                if which == 0:
                    self.act(ob[jj], c_, AF.Silu, [bcv[jj]], [bob[jj]])
                else:
                    self.act(c_, c_, AF.Silu, [bcv[jj]], [bcv[jj]])
                    self.stt(ob[jj], c_, 64 ** -0.5, ei[j], ALU.mult, ALU.mult, [bcv[jj], bei[j]], [bob[jj]])
                self.dma('sync', dst[tb, :, t, :], ob[jj], reads=[bob[jj]])
    self.P.barrier()


@_kb_method
def gla_core(self, qsrc, ksrc, gsrc, vsrc, dk, mode, gsrc_gate, ngcol, dst_c0):
    A = self.A
    A.reset(self.base)
    TB, NTB, S = self.TB, self.NTB, self.S
    NCH = TB // 64
    NH = 4
    ml = (mode == 'mlstm')
    PS = self.ps
    ABANK = (4, 7)
    trb = PS[5][:, :].bitcast(BF16)
    Sst = [A.take([dk, 128], F32) for _ in range(NH)]
    bS = [Buf() for _ in range(NH)]
    Snt = [A.take([dk, 128], F32) for _ in range(NH)] if ml else None
    bSn = [Buf() for _ in range(NH)]
    for h in range(NH):
        self.P.emit('vector', (lambda a: (lambda e: e.memset(a, 0.0)))(Sst[h]), writes=[bS[h]])
        if ml:
            self.P.emit('vector', (lambda a: (lambda e: e.memset(a, 0.0)))(Snt[h]), writes=[bSn[h]])

    def ring(n, shape, dt):
        return [A.take(shape, dt) for _ in range(n)], [Buf() for _ in range(n)]
    qb, bqb = ring(4, [dk, TB], BF16)
    kb_, bkb = ring(4, [dk, TB], BF16)
    gb, bgb = ring(4, [dk, TB], F32)
    vb, bvb = ring(4, [128, TB], BF16)
    cum, bcum = ring(4, [dk, TB], F32)
    eq, beq = ring(4, [dk, TB], F32)
    ek, bek = ring(4, [dk, TB], F32)
    qt, bqt = ring(4, [dk, TB], BF16)
    kt, bkt = ring(4, [dk, TB], BF16)
    esm, besm = ring(4, [dk, 2 * NCH], F32)
    vk, bvk = ring(4, [64, 128 + dk], BF16)
    Sp, bSp = ring(4, [dk, 128], BF16)
    Snp, bSnp = ring(4, [dk, 128], BF16)
    at, bat = ring(4, [64, 64], BF16)
    osb, bosb = ring(2, [128, TB], F32)
    dsb, bdsb = ring(2, [128, TB], F32)
    gate, bgate = ring(4, [128, TB], F32)
    w1, bw1 = ring(2, [128, TB], F32)
    yb, byb = ring(2, [128, TB], BF16)
    cnt = {'c': 0}

    def load(tb, hp, jb):
        for hh in range(2):
            h = hp * 2 + hh
            j = jb * 2 + hh
            tile_, r0 = (h * dk) // 128, (h * dk) % 128
            self.dma('sync', qb[j], qsrc[tb, r0:r0 + dk, tile_, :], writes=[bqb[j]])
            self.dma('sync', kb_[j], ksrc[tb, r0:r0 + dk, tile_, :], writes=[bkb[j]])
            self.dma('sync', gb[j], gsrc[tb, r0:r0 + dk, tile_, :], writes=[bgb[j]])
            self.dma('sync', vb[j], vsrc[tb, :, h, :], writes=[bvb[j]])
            self.dma('sync', gate[j], gsrc_gate[tb, :, h, :], writes=[bgate[j]])
    items = [(tb, hp) for tb in range(NTB) for hp in range(2)]
    load(items[0][0], items[0][1], 0)
    for it, (tb, hp) in enumerate(items):
        jb = it % 2
        if it + 1 < len(items):
            load(items[it + 1][0], items[it + 1][1], (it + 1) % 2)
        for hh in range(2):
            j = jb * 2 + hh
            self.P.emit('vector', (lambda o_, m_, g_: (lambda e: e.tensor_tensor_scan(o_, m_, g_, 0.0, ALU.mult, ALU.add)))(
                cum[j], self.reset[0:dk, 0:TB], gb[j]), reads=[self.bc, bgb[j]], writes=[bcum[j]])
            c3 = cum[j].rearrange("p (c t) -> p c t", t=64)
            e3 = eq[j].rearrange("p (c t) -> p c t", t=64)
            self.tt(e3, c3, c3[:, :, 31:32].broadcast_to([dk, NCH, 64]), ALU.subtract, [bcum[j]], [beq[j]])
            self.act(ek[j], eq[j], AF.Exp, [beq[j]], [bek[j]], scale=-1.0)
            self.act(eq[j], eq[j], AF.Exp, [beq[j]], [beq[j]])
            self.act(esm[j][:, 0:NCH], c3[:, :, 31], AF.Exp, [bcum[j]], [besm[j]])
            self.act(esm[j][:, NCH:2 * NCH], c3[:, :, 63], AF.Exp, [bcum[j]], [besm[j]], acc=True)
            self.tt(qt[j], qb[j], eq[j], ALU.mult, [bqb[j], beq[j]], [bqt[j]])
            self.tt(kt[j], kb_[j], ek[j], ALU.mult, [bkb[j], bek[j]], [bkt[j]], eng='gpsimd')
        TRB = (5, 5) if ml else (5, 3)
        KVB = (6, 6) if ml else (6, 2)
        for c in range(NCH):
            cs = slice(c * 64, (c + 1) * 64)
            ctx = []
            for hh in range(2):
                h = hp * 2 + hh
                j = jb * 2 + hh
                n = cnt['c']
                cnt['c'] += 1
                tb_ = TRB[hh]
                trh = PS[tb_][:, :].bitcast(BF16)[:, (hh * 512 if ml else 0):]
                self.tr(trh[0:64, 0:128], vb[j][:, cs], self.identb, [bvb[j], self.bc], [self.bps[tb_]])
                self.tr(trh[0:64, 128:128 + dk], kt[j][:, cs], self.identb[0:dk, 0:dk], [bkt[j], self.bc], [self.bps[tb_]])
                ab = ABANK[hh]
                aps = PS[ab][0:64, 0:64]
                self.mm(aps, kt[j][:, cs], qt[j][:, cs], True, True, [bkt[j], bqt[j]], [self.bps[ab]])
                sj = n % 4
                self.act(Sp[sj], Sst[h], AF.Copy, [bS[h], besm[j]], [bSp[sj]], scale=esm[j][:, c:c + 1])
                if ml:
                    self.act(Snp[sj], Snt[h], AF.Copy, [bSn[h], besm[j]], [bSnp[sj]], scale=esm[j][:, c:c + 1])
                ctx.append((h, j, n, tb_, trh, ab, aps, sj))
            for hh in range(2):
                h, j, n, tb_, trh, ab, aps, sj = ctx[hh]
                self.cp(vk[n % 4], trh[0:64, 0:128 + dk], [self.bps[tb_]], [bvk[n % 4]])
                self.tt(at[n % 4], aps, self.maskT, ALU.mult, [self.bps[ab], self.bc], [bat[n % 4]])
            for hh in range(2):
                h, j, n, tb_, trh, ab, aps, sj = ctx[hh]
                v_ = vk[n % 4]
                bv_ = bvk[n % 4]
                ktm = v_[:, 128:128 + dk]
                self.mm(PS[hh][:, cs], v_[:, 0:128], at[n % 4], True, False, [bv_, bat[n % 4]], [self.bps[hh]])
                self.mm(PS[hh][:, cs], Sp[sj], qt[j][:, cs], False, True, [bSp[sj], bqt[j]], [self.bps[hh]])
                if ml:
                    self.mm(PS[2 + hh][:, cs], self.onesb[0:64, :], at[n % 4], True, False,
                            [self.bc, bat[n % 4]], [self.bps[2 + hh]])
                    self.mm(PS[2 + hh][:, cs], Snp[sj], qt[j][:, cs], False, True, [bSnp[sj], bqt[j]], [self.bps[2 + hh]])
                kb2 = KVB[hh]
                ko = hh * 128 if ml else 0
                self.mm(PS[kb2][0:dk, ko:ko + 128], ktm, v_[:, 0:128], True, True, [bv_], [self.bps[kb2]])
                if ml:
                    self.mm(PS[kb2][0:dk, 256 + ko:384 + ko], ktm, self.onesb[0:64, :], True, True, [bv_, self.bc], [self.bps[kb2]])
            for hh in range(2):
                h, j, n, tb_, trh, ab, aps, sj = ctx[hh]
                kb2 = KVB[hh]
                ko = hh * 128 if ml else 0
                kvs = PS[kb2][0:dk, ko:ko + 128]
                self.ts(Sst[h], Sst[h], esm[j][:, NCH + c:NCH + c + 1], None, ALU.mult, None, [bS[h], besm[j]], [bS[h]])
                self.stt(Sst[h], kvs, eq[j][:, c * 64 + 63:c * 64 + 64], Sst[h], ALU.mult, ALU.add,
                         [self.bps[kb2], beq[j], bS[h]], [bS[h]])
                if ml:
                    kvn = PS[kb2][0:dk, 256 + ko:384 + ko]
                    self.ts(Snt[h], Snt[h], esm[j][:, NCH + c:NCH + c + 1], None, ALU.mult, None,
                            [bSn[h], besm[j]], [bSn[h]])
                    self.stt(Snt[h], kvn, eq[j][:, c * 64 + 63:c * 64 + 64], Snt[h], ALU.mult, ALU.add,
                             [self.bps[kb2], beq[j], bSn[h]], [bSn[h]])
        for hh in range(2):
            h = hp * 2 + hh
            j = jb * 2 + hh
            r = hh
            self.act(osb[r], PS[hh][:, 0:TB], AF.Copy, [self.bps[hh]], [bosb[r]])
            if ml:
                self.act(dsb[r], PS[2 + hh][:, 0:TB], AF.Abs, [self.bps[2 + hh]], [bdsb[r]])
                self.ts(dsb[r], dsb[r], 1.0, None, ALU.max, None, [bdsb[r]], [bdsb[r]])
                self.P.emit('vector', (lambda a_: (lambda e: e.reciprocal(a_, a_)))(dsb[r]), reads=[bdsb[r]], writes=[bdsb[r]])
                self.tt(w1[r], osb[r], dsb[r], ALU.mult, [bosb[r], bdsb[r]], [bw1[r]])
                self.tt(yb[r], w1[r], gate[j], ALU.mult, [bw1[r], bgate[j]], [byb[r]])
            else:
                self.act(w1[r], osb[r], AF.Square, [bosb[r]], [bw1[r]])
                self.mm(PS[2 + hh][:, 0:TB], self.ones, w1[r], True, True, [self.bc, bw1[r]], [self.bps[2 + hh]])
                self.act(w1[r], PS[2 + hh][:, 0:TB], AF.Sqrt, [self.bps[2 + hh]], [bw1[r]], bias=EPS, scale=1.0 / 128.0)
                self.P.emit('vector', (lambda a_: (lambda e: e.reciprocal(a_, a_)))(w1[r]), reads=[bw1[r]], writes=[bw1[r]])
                self.tt(w1[r], w1[r], osb[r], ALU.mult, [bw1[r], bosb[r]], [bw1[r]])
                self.stt(yb[r], w1[r], self.sp[:, ngcol:ngcol + 1], gate[j], ALU.mult, ALU.mult,
                         [bw1[r], self.bc, bgate[j]], [byb[r]])
            self.dma('sync', S['mixT'][tb, :, dst_c0 + h, :], yb[r], reads=[byb[r]])
    self.P.barrier()


@_kb_method
def t5_bias(self, oh):
    A = self.A
    self.bt5 = A.take([128, 4, 2, 128], F32, 't5b')
    self.b31 = A.take([128, 4], F32, 'b31')
    self.bbt = Buf()
    m = A.mark()
    ohs = A.take([128, 32 * 128 + 128], F32)
    boh = Buf()
    prod = A.take([128, 32, 128], F32)
    bpr = Buf()
    t5 = self.sp[:, SP_T5:SP_T5 + 128].rearrange("p (b h) -> p b h", h=4)
    self.cp(self.b31, t5[:, 31, :], [self.bc], [self.bbt])
    for ty in range(2):
        self.dma('sync', ohs, oh[ty], writes=[boh])
        o3 = ohs[:, 0:4096].rearrange("p (b i) -> p b i", b=32)
        for h in range(4):
            self.tt(prod, o3, t5[:, :, h:h + 1].broadcast_to([128, 32, 128]), ALU.mult, [boh, self.bc], [bpr])
            self.P.emit('vector', (lambda o_, i_: (lambda e: e.tensor_reduce(o_, i_, AX.X, ALU.add)))(
                self.bt5[:, h, ty, :], prod.rearrange("p b i -> p i b")), reads=[bpr], writes=[self.bbt])
            self.tt(self.bt5[:, h, ty, :], self.bt5[:, h, ty, :], ohs[:, 4096:4224], ALU.add, [self.bbt, boh], [self.bbt])
    self.P.barrier()
    A.reset(m)
    self.base = A.mark()


@_kb_method
def diff_attn(self, l):
    A = self.A
    A.reset(self.base)
    TB, NTB, S, T = self.TB, self.NTB, self.S, self.T
    o = l * LSTRIDE
    NKT = T // 128
    nsub = TB // 128
    lam_init = 0.8 - 0.6 * float(np.exp(-0.3 * l))
    PS = self.ps
    lm = A.take([128, 8], F32)
    blm = Buf()
    dl = self.sp[:, o + 150:o + 406]
    pr = A.take([128, 128], F32)
    self.tt(pr[:, 0:64], dl[:, 0:64], dl[:, 64:128], ALU.mult, [self.bc], [blm])
    self.tt(pr[:, 64:128], dl[:, 128:192], dl[:, 192:256], ALU.mult, [self.bc], [blm])
    self.P.emit('vector', lambda e: e.tensor_reduce(lm[:, 0:2], pr.rearrange("p (a b) -> p a b", a=2), AX.X, ALU.add),
                reads=[blm], writes=[blm])
    self.act(lm[:, 0:2], lm[:, 0:2], AF.Exp, [blm], [blm])
    self.tt(lm[:, 2:3], lm[:, 1:2], lm[:, 0:1], ALU.subtract, [blm], [blm])
    self.ts(lm[:, 3:4], lm[:, 2:3], -lam_init, None, ALU.add, None, [blm], [blm])
    self.ts(lm[:, 4:5], self.sp[:, o + 132:o + 133], 1.0 - lam_init, None, ALU.mult, None, [self.bc], [blm])
    neglam = lm[:, 3:4]
    ngs = lm[:, 4:5]
    qa = A.take([128, T], BF16)
    ka = A.take([128, T], BF16)
    va = A.take([128, T], BF16)
    bq, bk, bv = Buf(), Buf(), Buf()
    vt = A.take([128, NKT, 128], BF16)
    bvt = Buf()
    trb = [PS[4][:, :].bitcast(BF16), PS[5][:, :].bitcast(BF16)]

    def ring(n, shape, dt):
        return [A.take(shape, dt) for _ in range(n)], [Buf() for _ in range(n)]
    p1, bp1 = ring(2, [128, TB], BF16)
    p2, bp2 = ring(2, [128, TB], BF16)
    tmp, btmp = ring(2, [128, 128], F32)
    w1, bw1 = ring(2, [128, TB], F32)
    w2, bw2 = ring(2, [128, TB], F32)
    yb, byb = ring(2, [128, TB], BF16)
    nn = 0
    tn = [0]
    for h in range(4):
        for tb in range(NTB):
            self.dma('sync', qa[:, tb * TB:(tb + 1) * TB], S['df_q'][tb, :, h, :], writes=[bq])
            self.dma('sync', ka[:, tb * TB:(tb + 1) * TB], S['df_k'][tb, :, h, :], writes=[bk])
            self.dma('sync', va[:, tb * TB:(tb + 1) * TB], S['df_v'][tb, :, h, :], writes=[bv])
        for kt_ in range(NKT):
            bank = 4 + kt_ % 2
            self.tr(trb[kt_ % 2][:, 0:128], va[:, kt_ * 128:(kt_ + 1) * 128], self.identb, [bv, self.bc], [self.bps[bank]])
            self.cp(vt[:, kt_, :], trb[kt_ % 2][:, 0:128], [self.bps[bank]], [bvt], acc=True)
        b31 = self.b31[:, h:h + 1]
        pairs = [(I, J) for I in range(NTB) for J in range(nsub * I + nsub)]

        def emit_S(idx):
            I, J = pairs[idx]
            st_ = idx % 2
            sA, sB = PS[4 + st_ * 2], PS[5 + st_ * 2]
            bA, bB = self.bps[4 + st_ * 2], self.bps[5 + st_ * 2]
            ks = slice(J * 128, (J + 1) * 128)
            qs = slice(I * TB, (I + 1) * TB)
            self.mm(sA[:, 0:TB], ka[0:64, ks], qa[0:64, qs], True, True, [bk, bq], [bA])
            self.mm(sB[:, 0:TB], ka[64:128, ks], qa[64:128, qs], True, True, [bk, bq], [bB])
        emit_S(0)
        for idx, (I, J) in enumerate(pairs):
            if idx + 1 < len(pairs):
                emit_S(idx + 1)
            last = nsub * I + nsub - 1
            a = J - nsub * I
            st_ = idx % 2
            sA, sB = PS[4 + st_ * 2], PS[5 + st_ * 2]
            bA, bB = self.bps[4 + st_ * 2], self.bps[5 + st_ * 2]
            c0 = max(a, 0) * 128
            for (sX, bX, pX, bpX) in ((sA, bA, p1[st_], bp1[st_]), (sB, bB, p2[st_], bp2[st_])):
                first = True
                if c0 > 0:
                    self.P.emit('gpsimd', (lambda a_: (lambda e: e.memset(a_, 0.0)))(pX[:, 0:c0]), writes=[bpX])
                cc = c0
                for sb_ in range(max(a, 0), nsub):
                    ty = sb_ - a
                    if a < -1 or ty >= 2:
                        break
                    if ty < 0:
                        continue
                    tj = tn[0] % 2
                    tn[0] += 1
                    sl_ = slice(sb_ * 128, (sb_ + 1) * 128)
                    self.tt(tmp[tj], sX[:, sl_], self.bt5[:, h, ty, :], ALU.add, [bX, self.bbt], [btmp[tj]])
                    self.act(pX[:, sl_], tmp[tj], AF.Exp, [btmp[tj]], [bpX], acc=not first)
                    first = False
                    cc = (sb_ + 1) * 128
                if cc < TB:
                    self.act(pX[:, cc:TB], sX[:, cc:TB], AF.Exp, [bX, self.bbt], [bpX], bias=b31, acc=not first)
            v_ = vt[:, J, :]
            accs = ((0, v_, p1[st_], bp1[st_]), (1, self.onesb, p1[st_], bp1[st_]),
                    (2, v_, p2[st_], bp2[st_]), (3, self.onesb, p2[st_], bp2[st_]))
            for (bk_, lhs, pX, bpX) in accs:
                self.mm(PS[bk_][:, 0:TB], lhs, pX[:, 0:TB], J == 0, J == last, [bvt, self.bc, bpX], [self.bps[bk_]])
            if J != last:
                continue
            fb = 4 + st_ * 2
            r = I % 2
            self.P.emit('vector', (lambda o_, i_: (lambda e: e.reciprocal(o_, i_)))(w1[r], PS[1][:, 0:TB]),
                        reads=[self.bps[1]], writes=[bw1[r]])
            self.tt(w1[r], w1[r], PS[0][:, 0:TB], ALU.mult, [bw1[r], self.bps[0]], [bw1[r]])
            self.P.emit('vector', (lambda o_, i_: (lambda e: e.reciprocal(o_, i_)))(w2[r], PS[3][:, 0:TB]),
                        reads=[self.bps[3]], writes=[bw2[r]])
            self.tt(w2[r], w2[r], PS[2][:, 0:TB], ALU.mult, [bw2[r], self.bps[2]], [bw2[r]])
            self.stt(w1[r], w2[r], neglam, w1[r], ALU.mult, ALU.add, [bw2[r], blm, bw1[r]], [bw1[r]])
            self.act(w2[r], w1[r], AF.Square, [bw1[r]], [bw2[r]])
            self.mm(PS[fb][:, 0:TB], self.ones, w2[r], True, True, [self.bc, bw2[r]], [self.bps[fb]])
            self.act(w2[r], PS[fb][:, 0:TB], AF.Sqrt, [self.bps[fb]], [bw2[r]], bias=EPS, scale=1.0 / 128.0)
            self.P.emit('vector', (lambda a_: (lambda e: e.reciprocal(a_, a_)))(w2[r]), reads=[bw2[r]], writes=[bw2[r]])
            self.stt(yb[r], w1[r], ngs, w2[r], ALU.mult, ALU.mult, [bw1[r], blm, bw2[r]], [byb[r]])
            self.dma('sync', S['mixT'][I, :, 8 + h, :], yb[r], reads=[byb[r]])
    self.P.barrier()


@_kb_method
def xattn(self, l, w_q, w_kv, w_o, ln_i):
    A = self.A
    TB, NTB, S = self.TB, self.NTB, self.S
    PS = self.ps
    for (dst, c0) in (('kxT', 0), ('vxT', D)):
        A.reset(self.base)
        eb = [A.take([128, NMEM], BF16) for _ in range(3)]
        beb = [Buf() for _ in range(3)]
        st = {'r': 0}

        def epi(g, tb, tiles, dst=dst, st=st, eb=eb, beb=beb):
            for s, (pst, bp) in enumerate(tiles):
                r = st['r'] % 3
                st['r'] += 1
                self.act(eb[r], pst, AF.Copy, [bp], [beb[r]])
                self.dma('sync', S[dst][0, :, 4 * g + s, :], eb[r], reads=[beb[r]])
        groups = [[(c0 + 512 * g + 128 * s, 128) for s in range(4)] for g in range(4)]
        self.gemm_fm(S['memT'], 16, 1, NMEM, w_kv, groups, epi, 4)
    A.reset(self.base)
    eb2 = [A.take([128, TB], BF16) for _ in range(3)]
    beb2 = [Buf() for _ in range(3)]
    st2 = {'r': 0}

    def epi2(g, tb, tiles):
        for s, (pst, bp) in enumerate(tiles):
            r = st2['r'] % 3
            st2['r'] += 1
            self.act(eb2[r], pst, AF.Copy, [bp], [beb2[r]])
            self.dma('sync', S['qx'][tb, :, 4 * g + s, :], eb2[r], reads=[beb2[r]])
    groups = [[(512 * g + 128 * s, 128) for s in range(4)] for g in range(4)]
    self.gemm_fm(S['xT'], 16, NTB, TB, w_q, groups, epi2, 4)
    A.reset(self.base)
    kx = A.take([128, 16, NMEM], BF16)
    vx = A.take([128, 16, NMEM], BF16)
    bkx, bvx = Buf(), Buf()
    self.dma('sync', kx, S['kxT'][0], writes=[bkx])
    self.dma('sync', vx, S['vxT'][0], writes=[bvx])
    vtm = A.take([128, 2, D], BF16)
    bvtm = Buf()
    trb = [PS[6][:, :].bitcast(BF16), PS[7][:, :].bitcast(BF16)]
    n = 0
    for c in range(16):
        for mj in range(2):
            bank = 6 + n % 2
            self.tr(trb[n % 2][:, 0:128], vx[:, c, mj * 128:(mj + 1) * 128], self.identb, [bvx, self.bc], [self.bps[bank]])
            self.cp(vtm[:, mj, c * 128:(c + 1) * 128], trb[n % 2][:, 0:128], [self.bps[bank]], [bvtm], acc=True)
            n += 1
    qb = [A.take([128, 16, TB], BF16) for _ in range(2)]
    bqb = [Buf(), Buf()]
    pt_ = [A.take([128, TB], BF16) for _ in range(4)]
    bpt = [Buf() for _ in range(4)]
    rz = [A.take([128, TB], F32) for _ in range(2)]
    brz = [Buf(), Buf()]
    ob = [A.take([128, TB], BF16) for _ in range(3)]
    bob = [Buf() for _ in range(3)]
    self.dma('sync', qb[0], S['qx'][0], writes=[bqb[0]])
    n = 0
    m = 0
    for tb in range(NTB):
        if tb + 1 < NTB:
            self.dma('sync', qb[(tb + 1) % 2], S['qx'][tb + 1], writes=[bqb[(tb + 1) % 2]])
        q = qb[tb % 2]
        for h in range(4):
            pp = []
            for mj in range(2):
                bank = 5 + (n % 3)
                j = n % 4
                n += 1
                for dc in range(4):
                    self.mm(PS[bank][:, 0:TB], kx[:, 4 * h + dc, mj * 128:(mj + 1) * 128], q[:, 4 * h + dc, :],
                            dc == 0, dc == 3, [bkx, bqb[tb % 2]], [self.bps[bank]])
                self.act(pt_[j], PS[bank][:, 0:TB], AF.Exp, [self.bps[bank]], [bpt[j]], scale=512 ** -0.5)
                pp.append((pt_[j], bpt[j]))
            for mj in range(2):
                self.mm(PS[4][:, 0:TB], self.onesb, pp[mj][0], mj == 0, mj == 1, [self.bc, pp[mj][1]], [self.bps[4]])
            for dvc in range(4):
                for mj in range(2):
                    self.mm(PS[dvc][:, 0:TB], vtm[:, mj, (4 * h + dvc) * 128:(4 * h + dvc + 1) * 128], pp[mj][0],
                            mj == 0, mj == 1, [bvtm, pp[mj][1]], [self.bps[dvc]])
            r = (tb * 4 + h) % 2
            self.P.emit('vector', (lambda o_, i_: (lambda e: e.reciprocal(o_, i_)))(rz[r], PS[4][:, 0:TB]),
                        reads=[self.bps[4]], writes=[brz[r]])
            for dvc in range(4):
                k3 = m % 3
                m += 1
                self.tt(ob[k3], PS[dvc][:, 0:TB], rz[r], ALU.mult, [self.bps[dvc], brz[r]], [bob[k3]])
                self.dma('sync', S['ox'][tb, :, 4 * h + dvc, :], ob[k3], reads=[bob[k3]])
    self.P.barrier()


def t5_onehots():
    oh = np.zeros((2, 128, 32 * 128 + 128), np.float32)
    j = np.arange(128)[:, None]
    i = np.arange(128)[None, :]
    for ty in range(2):
        rel = i - j + 128 * ty
        n = np.maximum(rel, 0)
        large = 16 + (np.log(np.maximum(n, 1).astype(np.float32) / np.float32(16)) / np.float32(np.log(128 / 16)) * 16).astype(np.int32)
        large = np.clip(large, 16, 31)
        bucket = np.where(n < 16, n, large)
        valid = rel >= 0
        for b in range(32):
            oh[ty, :, b * 128:(b + 1) * 128] = ((bucket == b) & valid).astype(np.float32)
        oh[ty, :, 4096:4224] = np.where(valid, 0.0, -1e30).astype(np.float32)
    return oh


def mlstm_sel():
    sel = np.zeros((8, 512), np.float32)
    for h in range(4):
        sel[h, h * 64:(h + 1) * 64] = 1.0
        sel[4 + h, 256 + h * 64:256 + (h + 1) * 64] = 1.0
    return sel


WNAMES = [('ffn_w_in', [DEPTH, 2, D, 2 * DFF]), ('ffn_w_out', [DEPTH, 2, DFF, D]), ('w_in', [DEPTH, D, NIN]),
          ('w_out', [DEPTH, D, D]), ('gla_gate_w', [DEPTH, 16, 256]), ('xattn_w_q', [DEPTH, D, D]),
          ('xattn_w_kv', [DEPTH, D, 2 * D]), ('xattn_w_o', [DEPTH, D, D])]


def build_program(T, depth=DEPTH, dbg=(), stop=None):
    kb = KB(T, dbg=dbg)
    cst = kb.dram_in("cst", [128, 832 + NSP])
    oh = kb.dram_in("oh", [2, 128, 32 * 128 + 128])
    sel = kb.dram_in("sel", [8, 512])
    x = kb.dram_in("x", [T, D])
    mem = kb.dram_in("mem", [NMEM, D])
    W = {nm: kb.dram_in(nm, shp) for nm, shp in WNAMES}
    out = kb.nc.dram_tensor("out", [T, D], F32, kind="ExternalOutput").ap()
    kb.setup_consts(cst, NSP)
    kb.hgrn_consts()
    kb.t5_bias(oh)
    kb.alloc_scratch()
    S = kb.S
    xres = [S['xres0'], S['xres1']]
    kb.transpose_in(x, T, kb.TB, xres[0], S['xT'])
    kb.transpose_in(mem, NMEM, NMEM, None, S['memT'])
    cur = 0

    def ln(l, i):
        nonlocal cur
        kb.layernorm(S['z'], sp_g(l, i), sp_b(l, i), xres[1 - cur], S['xT'])
        cur = 1 - cur
    for l in range(depth):
        o = l * LSTRIDE
        kb.ffn(S['xT'], xres[cur], W['ffn_w_in'][l, 0], W['ffn_w_out'][l, 0], S['hT'], S['z'])
        ln(l, 0)
        if stop == 'ffn1':
            break
        kb.in_proj(l, W['w_in'][l])
        if stop == 'inproj':
            break
        kb.gla_gate(l, W['gla_gate_w'][l])
        if stop == 'gate':
            break
        kb.mlstm_prep(l, sel)
        if stop == 'mlprep':
            break
        kb.gla_core(S['gla_q'], S['gla_k'], S['gla_g'], S['gla_v'], 64, 'rms', S['gla_r'], o + 130, 0)
        if stop == 'gla':
            break
        kb.gla_core(S['hg_q'], S['hg_k'], S['hg_g'], S['hg_v'], 128, 'rms', S['hg_o'], o + 131, 4)
        if stop == 'hg':
            break
        kb.diff_attn(l)
        if stop == 'diff':
            break
        kb.gla_core(S['ml_qc'], S['ml_kc'], S['ml_g'], S['ml_v'], 64, 'mlstm', S['ml_o'], None, 12)
        kb.gemm_resid(S['mixT'], 16, W['w_out'][l], xres[cur], S['z'])
        ln(l, 1)
        if stop == 'mix':
            break
        kb.xattn(l, W['xattn_w_q'][l], W['xattn_w_kv'][l], W['xattn_w_o'][l], 2)
        kb.gemm_resid(S['ox'], 16, W['xattn_w_o'][l], xres[cur], S['z'])
        ln(l, 2)
        if stop == 'xattn':
            break
        kb.ffn(S['xT'], xres[cur], W['ffn_w_in'][l, 1], W['ffn_w_out'][l, 1], S['hT'], S['z'])
        ln(l, 3)
    kb.transpose_out(xres[cur], out)
    kb.P.finalize()
    return kb


def make_in_maps(inp, T, ncores):
    f = lambda a: np.ascontiguousarray(np.asarray(a, np.float32))
    shared = {"cst": pack_consts(inp), "oh": t5_onehots(), "sel": mlstm_sel()}
    for nm, _ in WNAMES:
        shared[nm] = f(inp[nm])
    maps = []
    for c in range(ncores):
        b = c % inp['x'].shape[0]
        m = dict(shared)
        m["x"] = f(inp['x'][b, :T])
        m["mem"] = f(inp['mem'][b])
        maps.append(m)
    return maps


def kernel(**inputs):
    B, T = inputs['x'].shape[0], inputs['x'].shape[1]
    kb = build_program(T)
    maps = make_in_maps(inputs, T, 8)
    res = run_bass_kernel_spmd(kb.nc, maps, core_ids=list(range(8)))
    out = np.stack([np.asarray(res.results[b]["out"], np.float32) for b in range(B)], axis=0)
    return out
```
